# Optimizing a Trainium2 kernel written in Bass

```python
import jax, jax.numpy as jnp
from jax import lax
import numpy as np


D_MODEL = 1024
BATCH = 4
SEQ = 4096
DEPTH = 4

CHUNK = 64
Q_BLOCK = 128
NORM_EPS = 1e-6
D_FF = 256 * ((8 * D_MODEL // 3 + 255) // 256)

D_RNN = D_MODEL
RG_BLOCKS = 16
RG_BLOCK_W = D_RNN // RG_BLOCKS
CONV_W = 4
RG_C = 8.0

SB_HEAD_DIM = 128
SB_HEADS = D_MODEL // 128
SB_W = SB_HEADS * SB_HEAD_DIM

MLA_HEADS = 8
MLA_Q_LORA = D_MODEL // 4
MLA_KV_LORA = D_MODEL // 4
MLA_NOPE = 128
MLA_ROPE = 64
MLA_V = 128
MLA_V_W = MLA_HEADS * MLA_V
ROPE_THETA = 10000.0

N_BRANCHES = 3
IN_SPLITS = (D_RNN, D_RNN, SB_W, SB_W, SB_W, MLA_Q_LORA, MLA_KV_LORA, MLA_ROPE, N_BRANCHES * D_MODEL)
N_IN = D_RNN * 2 + SB_W * 3 + MLA_Q_LORA + MLA_KV_LORA + MLA_ROPE + N_BRANCHES * D_MODEL

kernel_name = 'hybrid_rglru_stickbreak_mla_macaron_trunk'


def rms_norm(x, g):
    xf = x.astype(jnp.float32)
    y = xf * lax.rsqrt(jnp.mean(xf * xf, axis=-1, keepdims=True) + NORM_EPS)
    return (y * g.astype(jnp.float32)).astype(x.dtype)


def swiglu_ffn(h, w_gate_up, w_down):
    gate, up = jnp.split(h @ w_gate_up, 2, axis=-1)
    return (jax.nn.silu(gate) * up) @ w_down


def causal_depthwise_conv(u, w, b):
    S = u.shape[1]
    up = jnp.pad(u, ((0, 0), (CONV_W - 1, 0), (0, 0)))
    out = b
    for tap in range(CONV_W):
        out = out + up[:, tap:tap + S] * w[tap]
    return out


def _lin_rec_combine(earlier, later):
    a1, b1 = earlier
    a2, b2 = later
    return a1 * a2, a2 * b1 + b2


def rg_lru(u, w_a, b_a, w_x, b_x, lam):
    B, S, _ = u.shape
    ub = u.reshape(B, S, RG_BLOCKS, RG_BLOCK_W)
    r = jax.nn.sigmoid(jnp.einsum('bsnj,njk->bsnk', ub, w_a).reshape(B, S, D_RNN) + b_a)
    i = jax.nn.sigmoid(jnp.einsum('bsnj,njk->bsnk', ub, w_x).reshape(B, S, D_RNN) + b_x)
    log_a = (-RG_C * jax.nn.softplus(-lam.astype(jnp.float32))) * r.astype(jnp.float32)
    a = jnp.exp(log_a)
    b = jnp.sqrt(-jnp.expm1(2.0 * log_a)) * (i * u).astype(jnp.float32)
    _, h = lax.associative_scan(_lin_rec_combine, (a, b), axis=1)
    return h.astype(u.dtype)


def stick_breaking_attention(q, k, v):
    S = q.shape[1]
    scale = SB_HEAD_DIM ** -0.5
    outs = []
    for blk in range(S // Q_BLOCK):
        q0 = blk * Q_BLOCK
        k_end = q0 + Q_BLOCK
        z = jnp.einsum('bqhd,bkhd->bhqk', q[:, q0:k_end], k[:, :k_end]).astype(jnp.float32) * scale
        q_pos = q0 + jnp.arange(Q_BLOCK)
        k_pos = jnp.arange(k_end)
        earlier = k_pos[None, :] < q_pos[:, None]
        log_keep = jnp.where(earlier, jax.nn.log_sigmoid(-z), 0.0)
        between = lax.cumsum(log_keep, axis=3, reverse=True) - log_keep
        w = jnp.where(earlier, jnp.exp(jax.nn.log_sigmoid(z) + between), 0.0)
        outs.append(jnp.einsum('bhqk,bkhd->bqhd', w.astype(v.dtype), v[:, :k_end]))
    return jnp.concatenate(outs, axis=1)


def chunk_causal_softmax_attention(q, k, v, scale):
    S = q.shape[1]
    outs = []
    for blk in range(S // Q_BLOCK):
        q0 = blk * Q_BLOCK
        k_end = q0 + Q_BLOCK
        s = jnp.einsum('bqhd,bkhd->bhqk', q[:, q0:k_end], k[:, :k_end]).astype(jnp.float32) * scale
        q_chunk = (q0 + jnp.arange(Q_BLOCK)) // CHUNK
        k_chunk = jnp.arange(k_end) // CHUNK
        s = jnp.where(k_chunk[None, :] <= q_chunk[:, None], s, -jnp.inf)
        p = jax.nn.softmax(s, axis=-1).astype(v.dtype)
        outs.append(jnp.einsum('bhqk,bkhd->bqhd', p, v[:, :k_end]))
    return jnp.concatenate(outs, axis=1)


def rope_tables(positions, dtype):
    inv = ROPE_THETA ** (-jnp.arange(0, MLA_ROPE, 2, dtype=jnp.float32) / MLA_ROPE)
    ang = positions.astype(jnp.float32)[..., None] * inv
    return jnp.cos(ang).astype(dtype), jnp.sin(ang).astype(dtype)


def apply_rope(x, cos, sin):
    x1, x2 = jnp.split(x, 2, axis=-1)
    return jnp.concatenate([x1 * cos - x2 * sin, x2 * cos + x1 * sin], axis=-1)


def mla_branch(c_q_raw, c_kv_raw, k_rope_raw, q_norm, w_uq, kv_norm, w_ukv, cos, sin):
    B, S, _ = c_q_raw.shape
    q = (rms_norm(c_q_raw, q_norm) @ w_uq).reshape(B, S, MLA_HEADS, MLA_NOPE + MLA_ROPE)
    q_nope, q_rope = jnp.split(q, [MLA_NOPE], axis=-1)
    q_rope = apply_rope(q_rope, cos[:, :, None, :], sin[:, :, None, :])
    kv = (rms_norm(c_kv_raw, kv_norm) @ w_ukv).reshape(B, S, MLA_HEADS, MLA_NOPE + MLA_V)
    k_nope, v = jnp.split(kv, [MLA_NOPE], axis=-1)
    k_rope = apply_rope(k_rope_raw, cos, sin)
    k = jnp.concatenate([k_nope, jnp.broadcast_to(k_rope[:, :, None, :], (B, S, MLA_HEADS, MLA_ROPE))], axis=-1)
    q = jnp.concatenate([q_nope, q_rope], axis=-1)
    o = chunk_causal_softmax_attention(q, k, v, (MLA_NOPE + MLA_ROPE) ** -0.5)
    return o.reshape(B, S, MLA_V_W)


def setup_inputs(seed: int = 0) -> dict:
    key = jax.random.key(seed)
    ks = iter(jax.random.split(key, 48))
    L, D = DEPTH, D_MODEL

    def normal(shape, fan_in):
        return jax.random.normal(next(ks), shape, jnp.float32) * (fan_in ** -0.5)

    def gain(shape):
        return 1.0 + 0.02 * jax.random.normal(next(ks), shape, jnp.float32)

    def bias(shape):
        return 0.02 * jax.random.normal(next(ks), shape, jnp.float32)

    x = jax.random.normal(next(ks), (BATCH, SEQ, D), jnp.float32)
    offsets = jax.random.randint(next(ks), (BATCH, 1), 0, 16384, dtype=jnp.int32)
    positions = (offsets + jnp.arange(SEQ, dtype=jnp.int32)[None, :]).astype(jnp.int32)
    a0 = jax.random.uniform(next(ks), (L, D_RNN), jnp.float32, minval=0.9, maxval=0.999)
    base = a0 ** (1.0 / RG_C)
    rg_lambda = jnp.log(base) - jnp.log1p(-base)
    return {
        'x': x,
        'positions': positions,
        'ffn1_norm': gain((L, D)),
        'ffn1_w_gate_up': normal((L, D, 2 * D_FF), D),
        'ffn1_w_down': normal((L, D_FF, D), D_FF),
        'mix_norm': gain((L, D)),
        'w_in': normal((L, D, N_IN), D),
        'conv_w': normal((L, CONV_W, D_RNN), CONV_W),
        'conv_b': bias((L, D_RNN)),
        'rg_w_a': normal((L, RG_BLOCKS, RG_BLOCK_W, RG_BLOCK_W), RG_BLOCK_W),
        'rg_b_a': bias((L, D_RNN)),
        'rg_w_x': normal((L, RG_BLOCKS, RG_BLOCK_W, RG_BLOCK_W), RG_BLOCK_W),
        'rg_b_x': bias((L, D_RNN)),
        'rg_lambda': rg_lambda,
        'mla_q_norm': gain((L, MLA_Q_LORA)),
        'mla_w_uq': normal((L, MLA_Q_LORA, MLA_HEADS * (MLA_NOPE + MLA_ROPE)), MLA_Q_LORA),
        'mla_kv_norm': gain((L, MLA_KV_LORA)),
        'mla_w_ukv': normal((L, MLA_KV_LORA, MLA_HEADS * (MLA_NOPE + MLA_V)), MLA_KV_LORA),
        'w_branch_a': normal((L, D_RNN, D), D_RNN),
        'w_branch_b': normal((L, SB_W, D), SB_W),
        'w_branch_c': normal((L, MLA_V_W, D), MLA_V_W),
        'w_out': normal((L, D, D), D),
        'ffn2_norm': gain((L, D)),
        'ffn2_w_gate_up': normal((L, D, 2 * D_FF), D),
        'ffn2_w_down': normal((L, D_FF, D), D_FF),
        'final_norm': gain((D,)),
    }


def reference(x, positions, ffn1_norm, ffn1_w_gate_up, ffn1_w_down, mix_norm, w_in,
              conv_w, conv_b, rg_w_a, rg_b_a, rg_w_x, rg_b_x, rg_lambda,
              mla_q_norm, mla_w_uq, mla_kv_norm, mla_w_ukv,
              w_branch_a, w_branch_b, w_branch_c, w_out,
              ffn2_norm, ffn2_w_gate_up, ffn2_w_down, final_norm):
    B, S, _ = x.shape
    split_points = []
    acc = 0
    for width in IN_SPLITS[:-1]:
        acc += width
        split_points.append(acc)
    cos, sin = rope_tables(positions, x.dtype)

    for l in range(DEPTH):
        x = x + 0.5 * swiglu_ffn(rms_norm(x, ffn1_norm[l]), ffn1_w_gate_up[l], ffn1_w_down[l])

        h = rms_norm(x, mix_norm[l])
        rg_x, rg_g, sb_q, sb_k, sb_v, c_q, c_kv, k_r, gate_logits = jnp.split(h @ w_in[l], split_points, axis=-1)

        u = causal_depthwise_conv(rg_x, conv_w[l], conv_b[l])
        y_a = rg_lru(u, rg_w_a[l], rg_b_a[l], rg_w_x[l], rg_b_x[l], rg_lambda[l]) * jax.nn.gelu(rg_g)

        hd = (B, S, SB_HEADS, SB_HEAD_DIM)
        y_b = stick_breaking_attention(sb_q.reshape(hd), sb_k.reshape(hd), sb_v.reshape(hd)).reshape(B, S, SB_W)

        y_c = mla_branch(c_q, c_kv, k_r, mla_q_norm[l], mla_w_uq[l], mla_kv_norm[l], mla_w_ukv[l], cos, sin)

        g_a, g_b, g_c = jnp.split(jax.nn.sigmoid(gate_logits), N_BRANCHES, axis=-1)
        merged = g_a * (y_a @ w_branch_a[l]) + g_b * (y_b @ w_branch_b[l]) + g_c * (y_c @ w_branch_c[l])
        x = x + merged @ w_out[l]

        x = x + 0.5 * swiglu_ffn(rms_norm(x, ffn2_norm[l]), ffn2_w_gate_up[l], ffn2_w_down[l])

    return rms_norm(x, final_norm)
```

```python
import numpy as np
import concourse.bass as bass
import concourse.mybir as mybir

F32 = mybir.dt.float32
BF16 = mybir.dt.bfloat16
I32 = mybir.dt.int32
AF = mybir.ActivationFunctionType
ALU = mybir.AluOpType

ENGS = ("sp", "pool", "act", "dve", "pe")
NDMA = 40


class Slot:
    def __init__(self, P, name=""):
        self.name = name
        self.last_w = None
        self.readers = []
        self.ld_sem = None
        self.st_sem = None
        P.slots.append(self)

    def reset(self):
        self.last_w = None
        self.readers = []
        self.ld_sem = None
        self.st_sem = None


class Prog:
    def __init__(self, nc):
        self.nc = nc
        self.slots = []
        self.sets = []
        for i in range(2):
            es = {e: nc.alloc_semaphore(name=f"s{i}_{e}") for e in ENGS}
            ds = [nc.alloc_semaphore(name=f"s{i}_d{j}") for j in range(NDMA)]
            self.sets.append((es, ds))
        self.phase_idx = 0
        self.in_phase = False

    def slot(self, name=""):
        return Slot(self, name)

    def begin(self):
        assert not self.in_phase
        self.in_phase = True
        self.es, ds = self.sets[self.phase_idx % 2]
        self.dma_free = list(ds)
        self.dma_cnt = {}
        self.ops = {e: [] for e in ENGS}
        self.cnt = {e: 0 for e in ENGS}
        self.waited = {e: {} for e in ENGS}
        for s in self.slots:
            s.reset()
        self.misc_sem = self._new_dma_sem()

    def _new_dma_sem(self):
        assert self.dma_free, "out of DMA semaphores"
        s = self.dma_free.pop()
        self.dma_cnt[s] = 0
        return s

    def _deps(self, eng, reads, writes, skip_sem=None):
        toks = []
        for s in reads:
            if s.last_w is not None:
                toks.append(s.last_w)
        for s in writes:
            if s.last_w is not None:
                toks.append(s.last_w)
            toks.extend(s.readers)
        need = {}
        for (sem, val, src) in toks:
            if src == "pe" and eng == "pe":
                continue
            if skip_sem is not None and sem is skip_sem:
                continue
            k = id(sem)
            if self.waited[eng].get(k, 0) >= val:
                continue
            if k not in need or need[k][1] < val:
                need[k] = (sem, val)
        waits = []
        for k, (sem, val) in need.items():
            self.waited[eng][k] = val
            waits.append((sem, val))
        return waits

    def op(self, eng, fn, reads=(), writes=()):
        waits = self._deps(eng, reads, writes)
        self.cnt[eng] += 1
        sem = self.es[eng]
        tok = (sem, self.cnt[eng], eng)
        wset = set(id(s) for s in writes)
        for s in reads:
            if id(s) not in wset:
                s.readers.append(tok)
        for s in writes:
            s.last_w = tok
            s.readers = []
        self.ops[eng].append((waits, fn, sem, 1))

    def set_stage(self, tiles, slots, size):
        self.stg = (tiles, slots, size)
        self.stg_i = 0

    def cast_load(self, out, in_, writes, pbase=0):
        shape = tuple(int(x) for x in out.shape)
        p, free = shape[0], shape[1:]
        n = 1
        for x in free:
            n *= x
        tiles, slots, size = self.stg
        if n > size:
            a = free[0]
            per = n // a
            step = max(1, size // per)
            for a0 in range(0, a, step):
                a1 = min(a, a0 + step)
                self.cast_load(out[:, a0:a1], in_[:, a0:a1], writes, pbase)
            return
        i = self.stg_i % len(tiles)
        self.stg_i += 1
        st = tiles[i][pbase:pbase + p, :n]
        if len(free) == 2:
            st = st.rearrange("p (a b) -> p a b", a=free[0])
        elif len(free) == 3:
            st = st.rearrange("p (a b c) -> p a b c", a=free[0], b=free[1])
        self.dma("sp", st, in_, writes=[slots[i]])
        self.op("pool", lambda e, out=out, st=st: e.tensor_copy(out, st), reads=[slots[i]], writes=list(writes))

    def dma(self, q, out, in_, reads=(), writes=(), part=False, **kw):
        if q == "pool":
            return self.cast_load(out, in_, writes, kw.get("pbase", 0))
        if writes:
            sl = writes[0]
            if sl.ld_sem is None:
                sl.ld_sem = self._new_dma_sem()
            sem = sl.ld_sem
        elif reads:
            sl = reads[0]
            if sl.st_sem is None:
                sl.st_sem = self._new_dma_sem()
            sem = sl.st_sem
        else:
            sem = self.misc_sem
        waits = self._deps(q, reads, writes, skip_sem=(sem if part else None))
        self.dma_cnt[sem] += 16
        tok = (sem, self.dma_cnt[sem], "dma")
        for s in reads:
            s.readers.append(tok)
        for s in writes:
            s.last_w = tok
            s.readers = []

        def fn(e, out=out, in_=in_, kw=kw):
            return e.dma_start(out=out, in_=in_, **kw)
        self.ops[q].append((waits, fn, sem, 16))

    def end(self):
        nc = self.nc
        other_es, other_ds = self.sets[(self.phase_idx + 1) % 2]
        final = [(s, c) for s, c in self.dma_cnt.items() if c > 0]
        with nc.Block() as block:
            decos = {"sp": block.sync, "pool": block.gpsimd, "act": block.scalar,
                     "dve": block.vector, "pe": block.tensor}
            for ename in ENGS:
                ops = self.ops[ename]

                def body(eng, ops=ops, ename=ename):
                    if ename == "pool":
                        for s in list(other_es.values()) + list(other_ds):
                            eng.sem_clear(s)
                    for waits, fn, sem, inc in ops:
                        for wsem, val in waits:
                            eng.wait_ge(wsem, val)
                        ins = fn(eng)
                        ins.then_inc(sem, inc)
                    if ename == "sp":
                        for s, c in final:
                            eng.wait_ge(s, c)
                decos[ename](body)
        self.phase_idx += 1
        self.in_phase = False

from contextlib import ExitStack

D = 1024
KD = 8
DFF = 2816
NFF = 22
EPS = 1e-6

PC_FFN1 = 0
PC_MIX = 8
PC_FFN2 = 16
PC_CONVW = 24
PC_CONVB = 56
PC_RGBA = 64
PC_RGBX = 72
PC_LAM = 80
PC_QN = 88
PC_KVN = 90
PC_FIN = 92
NPAR = 100


_UID = [0]


def mk_sb(nc, es):
    _UID[0] += 1
    u = _UID[0]
    return lambda name, shape, dt: es.enter_context(nc.sbuf_tensor(f"{name}_{u}", shape, dt))


def mk_pt(nc, es):
    _UID[0] += 1
    u = _UID[0]
    return lambda name, shape: es.enter_context(nc.psum_tensor(f"{name}_{u}", shape, F32))


def mk_stage(P, sb, n=3, size=2048):
    P.set_stage([sb(f"stg{i}", [128, size], F32) for i in range(n)], [P.slot() for _ in range(n)], size)


class Ctx:
    pass


def setup_consts(P, nc, es, par_d, L):
    C = Ctx()
    C.ones = es.enter_context(nc.sbuf_tensor("c_ones", [128, 128], BF16))
    C.par = es.enter_context(nc.sbuf_tensor("c_par", [128, L, NPAR], F32))
    C.eps = es.enter_context(nc.sbuf_tensor("c_eps", [128, 2], F32))
    C.s_ones = P.slot("ones")
    C.s_par = P.slot("par")
    C.s_eps = P.slot("eps")
    P.begin()
    P.op("pool", lambda e: e.memset(C.ones[:], 1.0), writes=[C.s_ones])

    def _e(e):
        e.memset(C.eps[:, 0:1], EPS)
        return e.memset(C.eps[:, 1:2], 1.0)
    P.op("pool", _e, writes=[C.s_eps])
    P.dma("sp", C.par[:], par_d.rearrange("l p n -> p l n"), writes=[C.s_par])
    P.end()
    return C


def rmsnorm_tile(P, C, x_ap3, s_x, h_ap3, s_h, gcol_ap, nk, sq, s_sq, ps, s_ps, lnv, s_ln, rstd, s_rstd, n, inv_dim):
    P.op("act", lambda e: e.activation(sq[:, :nk, :n], x_ap3, AF.Square), reads=[s_x], writes=[s_sq])

    def mm(e):
        ins = None
        for k in range(nk):
            ins = e.matmul(ps[:, :n], C.ones[:], sq[:, k, :n], start=(k == 0), stop=(k == nk - 1))
        return ins
    P.op("pe", mm, reads=[s_sq, C.s_ones], writes=[s_ps])
    P.op("act", lambda e: e.activation(lnv[:, :n], ps[:, :n], AF.Ln, bias=C.eps[:, 0:1], scale=inv_dim),
         reads=[s_ps, C.s_eps], writes=[s_ln])
    P.op("act", lambda e: e.activation(rstd[:, :n], lnv[:, :n], AF.Exp, scale=-0.5), reads=[s_ln], writes=[s_rstd])

    def nrm(e):
        ins = None
        for k in range(nk):
            ins = e.scalar_tensor_tensor(h_ap3[:, k, :], x_ap3[:, k, :], gcol_ap(k), rstd[:, :n], ALU.mult, ALU.mult)
        return ins
    P.op("dve", nrm, reads=[s_x, s_rstd, C.s_par], writes=[s_h])


def phase_ffn(P, nc, C, xT, w_gu, w_dn, l, gcol, T):
    TG = min(1024, T)
    NTI = TG // 512
    xT3 = xT.rearrange("(k p) t -> p k t", p=128)
    wgu3 = w_gu.rearrange("(k p) n -> p k n", p=128)
    wdn3 = w_dn.rearrange("(c p) n -> p c n", p=128)
    with ExitStack() as es:
        sb = mk_sb(nc, es)
        pt = mk_pt(nc, es)
        mk_stage(P, sb)
        xg = sb("f_xg", [128, KD, TG], F32)
        hT = sb("f_hT", [128, KD, TG], BF16)
        aT = sb("f_aT", [128, NFF, TG], BF16)
        sq = sb("f_sq", [128, KD, 512], BF16)
        lnv = sb("f_lnv", [128, 512], F32)
        rstd = sb("f_rstd", [128, 512], F32)
        sg = [sb(f"f_sg{i}", [128, 512], F32) for i in range(2)]
        NWB = 3
        wgu = [sb(f"f_wgu{i}", [128, KD, 2, 128], BF16) for i in range(NWB)]
        wd = [sb(f"f_wd{i}", [128, NFF, 128], BF16) for i in range(2)]
        pss = pt("f_pss", [128, 512])
        pg = [pt(f"f_pg{i}", [128, 512]) for i in range(2)]
        pu = [pt(f"f_pu{i}", [128, 512]) for i in range(2)]
        po = [pt(f"f_po{i}", [128, 512]) for i in range(2)]
        s_xg = [P.slot() for _ in range(NTI)]
        s_hT = [P.slot() for _ in range(NTI)]
        s_aT = [P.slot() for _ in range(NTI)]
        s_sq, s_ln, s_rstd, s_pss = P.slot(), P.slot(), P.slot(), P.slot()
        s_sg = [P.slot() for _ in range(2)]
        s_wgu = [P.slot() for _ in range(NWB)]
        s_wd = [P.slot() for _ in range(2)]
        s_pg = [P.slot() for _ in range(2)]
        s_pu = [P.slot() for _ in range(2)]
        s_po = [P.slot() for _ in range(2)]
        P.begin()
        it = 0
        wi = 0
        di = 0
        for g in range(T // TG):
            for ti in range(NTI):
                t0 = g * TG + ti * 512
                ts = slice(ti * 512, (ti + 1) * 512)
                P.dma("sp", xg[:, :, ts], xT3[:, :, t0:t0 + 512], writes=[s_xg[ti]])
                rmsnorm_tile(P, C, xg[:, :, ts], s_xg[ti], hT[:, :, ts], s_hT[ti],
                             lambda k: C.par[:, l, gcol + k:gcol + k + 1], KD,
                             sq, s_sq, pss, s_pss, lnv, s_ln, rstd, s_rstd, 512, 1.0 / D)
            for c in range(NFF):
                b = wi % NWB
                wi += 1
                P.dma("pool", wgu[b][:, :, 0, :], wgu3[:, :, c * 128:(c + 1) * 128], writes=[s_wgu[b]])
                P.dma("pool", wgu[b][:, :, 1, :], wgu3[:, :, DFF + c * 128:DFF + (c + 1) * 128],
                      writes=[s_wgu[b]], part=True)
                for ti in range(NTI):
                    ts = slice(ti * 512, (ti + 1) * 512)
                    j = it % 2
                    it += 1

                    def mm(e, b=b, ts=ts, j=j):
                        ins = None
                        for k in range(KD):
                            ins = e.matmul(pg[j][:], wgu[b][:, k, 0, :], hT[:, k, ts], start=(k == 0), stop=(k == KD - 1))
                        for k in range(KD):
                            ins = e.matmul(pu[j][:], wgu[b][:, k, 1, :], hT[:, k, ts], start=(k == 0), stop=(k == KD - 1))
                        return ins
                    P.op("pe", mm, reads=[s_wgu[b], s_hT[ti]], writes=[s_pg[j], s_pu[j]])
                    P.op("act", lambda e, j=j: e.activation(sg[j][:], pg[j][:], AF.Silu), reads=[s_pg[j]], writes=[s_sg[j]])
                    P.op("dve", lambda e, j=j, c=c, ts=ts: e.tensor_tensor(aT[:, c, ts], sg[j][:], pu[j][:], ALU.mult),
                         reads=[s_sg[j], s_pu[j]], writes=[s_aT[ti]])
            for oc in range(KD):
                b = di % 2
                P.dma("pool", wd[b][:], wdn3[:, :, oc * 128:(oc + 1) * 128], writes=[s_wd[b]])
                for ti in range(NTI):
                    ts = slice(ti * 512, (ti + 1) * 512)
                    j = di % 2
                    di2 = (di * NTI + ti) % 2

                    def mm2(e, b=b, ts=ts, j=di2):
                        ins = None
                        for c in range(NFF):
                            ins = e.matmul(po[j][:], wd[b][:, c, :], aT[:, c, ts], start=(c == 0), stop=(c == NFF - 1))
                        return ins
                    P.op("pe", mm2, reads=[s_wd[b], s_aT[ti]], writes=[s_po[di2]])
                    P.op("dve", lambda e, j=di2, oc=oc, ts=ts: e.scalar_tensor_tensor(
                        xg[:, oc, ts], po[j][:], 0.5, xg[:, oc, ts], ALU.mult, ALU.add),
                        reads=[s_po[di2], s_xg[ti]], writes=[s_xg[ti]])
                di += 1
            for ti in range(NTI):
                t0 = g * TG + ti * 512
                ts = slice(ti * 512, (ti + 1) * 512)
                P.dma("sp", xT3[:, :, t0:t0 + 512], xg[:, :, ts], reads=[s_xg[ti]])
        P.end()


SBW = 1024
OFF_RGX = 0
OFF_RGG = 1024
OFF_Q = 2048
OFF_K = 3072
OFF_V = 4096
OFF_CQ = 5120
OFF_CKV = 5376
OFF_KR = 5632
OFF_GATE = 5696
NIN = 8768


def phase_win(P, nc, C, xT, w_in, l, S, T, tok0=0):
    TG = min(2048, T)
    NTI = TG // 512
    xT3 = xT.rearrange("(k p) t -> p k t", p=128)
    w3 = w_in.rearrange("(k p) n -> p k n", p=128)
    chunks = []
    for c in range(8):
        chunks.append(([(OFF_RGX + c * 128, 128)], "copy", S["rgx"], c * 128, 128))
    for c in range(8):
        chunks.append(([(OFF_Q + c * 128, 128)], "qscale", S["qT"], c * 128, 128))
    for c in range(8):
        chunks.append(([(OFF_K + c * 128, 128)], "copy", S["kT"], c * 128, 128))
    for c in range(2):
        chunks.append(([(OFF_CQ + c * 128, 128)], "copy", S["cq"], c * 128, 128))
    for c in range(2):
        chunks.append(([(OFF_CKV + c * 128, 128)], "copy", S["ckv"], c * 128, 128))
    chunks.append(([(OFF_KR, 64)], "copy", S["kr"], 0, 64))
    chunks.append(([(OFF_KR + 32, 32), (OFF_KR, 32)], "copy", S["krsw"], 0, 64))
    for c in range(8):
        chunks.append(([(OFF_RGG + c * 128, 128)], "gelu", S["rgg"], c * 128, 128))
    for c in range(24):
        chunks.append(([(OFF_GATE + c * 128, 128)], "sigmoid", S["gate"], c * 128, 128))
    with ExitStack() as es:
        sb = mk_sb(nc, es)
        pt = mk_pt(nc, es)
        mk_stage(P, sb)
        xg = [sb(f"w_xg{i}", [128, KD, 512], F32) for i in range(2)]
        hT = sb("w_hT", [128, KD, TG], BF16)
        sq = sb("w_sq", [128, KD, 512], BF16)
        lnv = sb("w_lnv", [128, 512], F32)
        rstd = sb("w_rstd", [128, 512], F32)
        NWB = 3
        wt = [sb(f"w_wt{i}", [128, KD, 128], BF16) for i in range(NWB)]
        wv = [sb(f"w_wv{i}", [128, KD, 512], BF16) for i in range(2)]
        NST = 4
        stf = [sb(f"w_stf{i}", [128, 512], F32) for i in range(NST)]
        stb = [sb(f"w_stb{i}", [128, 512], BF16) for i in range(NST)]
        pss = pt("w_pss", [128, 512])
        NPW = 4
        pw = [pt(f"w_pw{i}", [128, 512]) for i in range(NPW)]
        s_xg = [P.slot() for _ in range(2)]
        s_hT = [P.slot() for _ in range(NTI)]
        s_sq, s_ln, s_rstd, s_pss = P.slot(), P.slot(), P.slot(), P.slot()
        s_wt = [P.slot() for _ in range(NWB)]
        s_wv = [P.slot() for _ in range(2)]
        s_stf = [P.slot() for _ in range(NST)]
        s_stb = [P.slot() for _ in range(NST)]
        s_pw = [P.slot() for _ in range(NPW)]
        P.begin()
        wi = 0
        pi = 0
        si = 0
        xi = 0
        for g in range(T // TG):
            for ti in range(NTI):
                t0 = g * TG + ti * 512
                ts = slice(ti * 512, (ti + 1) * 512)
                xb = xi % 2
                xi += 1
                P.dma("sp", xg[xb][:], xT3[:, :, t0:t0 + 512], writes=[s_xg[xb]])
                rmsnorm_tile(P, C, xg[xb][:], s_xg[xb], hT[:, :, ts], s_hT[ti],
                             lambda k: C.par[:, l, PC_MIX + k:PC_MIX + k + 1], KD,
                             sq, s_sq, pss, s_pss, lnv, s_ln, rstd, s_rstd, 512, 1.0 / D)
            for (pieces, evac, dst, row0, M) in chunks:
                b = wi % NWB
                wi += 1
                o = 0
                for pi_, (c0, w) in enumerate(pieces):
                    P.dma("pool", wt[b][:, :, o:o + w], w3[:, :, c0:c0 + w], writes=[s_wt[b]], part=(pi_ > 0))
                    o += w
                for ti in range(NTI):
                    t0 = tok0 + g * TG + ti * 512
                    ts = slice(ti * 512, (ti + 1) * 512)
                    j = pi % NPW
                    pi += 1
                    sj = si % NST
                    si += 1

                    def mm(e, b=b, ts=ts, j=j, M=M):
                        ins = None
                        for k in range(KD):
                            ins = e.matmul(pw[j][:M, :], wt[b][:, k, :M], hT[:, k, ts], start=(k == 0), stop=(k == KD - 1))
                        return ins
                    P.op("pe", mm, reads=[s_wt[b], s_hT[ti]], writes=[s_pw[j]])
                    if evac == "copy":
                        if dst.dtype == BF16:
                            P.op("dve", lambda e, j=j, sj=sj, M=M: e.tensor_copy(stb[sj][:M, :], pw[j][:M, :]),
                                 reads=[s_pw[j]], writes=[s_stb[sj]])
                            P.dma("sp", dst[row0:row0 + M, t0:t0 + 512], stb[sj][:M, :], reads=[s_stb[sj]])
                        else:
                            P.op("dve", lambda e, j=j, sj=sj, M=M: e.tensor_copy(stf[sj][:M, :], pw[j][:M, :]),
                                 reads=[s_pw[j]], writes=[s_stf[sj]])
                            P.dma("sp", dst[row0:row0 + M, t0:t0 + 512], stf[sj][:M, :], reads=[s_stf[sj]])
                    elif evac == "qscale":
                        P.op("dve", lambda e, j=j, sj=sj, M=M: e.tensor_scalar(stb[sj][:M, :], pw[j][:M, :], 128 ** -0.5, None, ALU.mult),
                             reads=[s_pw[j]], writes=[s_stb[sj]])
                        P.dma("sp", dst[row0:row0 + M, t0:t0 + 512], stb[sj][:M, :], reads=[s_stb[sj]])
                    else:
                        fn = AF.Gelu if evac == "gelu" else AF.Sigmoid
                        P.op("act", lambda e, j=j, sj=sj, M=M, fn=fn: e.activation(stf[sj][:M, :], pw[j][:M, :], fn),
                             reads=[s_pw[j]], writes=[s_stf[sj]])
                        P.dma("sp", dst[row0:row0 + M, t0:t0 + 512], stf[sj][:M, :], reads=[s_stf[sj]])
            for ch in range(2):
                b = ch
                P.dma("pool", wv[b][:], w3[:, :, OFF_V + ch * 512:OFF_V + (ch + 1) * 512], writes=[s_wv[b]])
                for tb in range(TG // 128):
                    t0 = tok0 + g * TG + tb * 128
                    ti = tb // 4
                    j = pi % NPW
                    pi += 1
                    sj = si % NST
                    si += 1

                    def mmv(e, b=b, tb=tb, j=j):
                        ins = None
                        for k in range(KD):
                            ins = e.matmul(pw[j][:], hT[:, k, tb * 128:(tb + 1) * 128], wv[b][:, k, :], start=(k == 0), stop=(k == KD - 1))
                        return ins
                    P.op("pe", mmv, reads=[s_wv[b], s_hT[ti]], writes=[s_pw[j]])
                    P.op("dve", lambda e, j=j, sj=sj: e.tensor_copy(stb[sj][:], pw[j][:]),
                         reads=[s_pw[j]], writes=[s_stb[sj]])
                    P.dma("sp", S["v"][t0:t0 + 128, ch * 512:(ch + 1) * 512], stb[sj][:], reads=[s_stb[sj]])
        P.end()


def alloc_scratch(nc, TA, pfx=""):
    S = {}
    S["rgx"] = nc.dram_tensor(pfx + "s_rgx", [1024, TA], F32).ap()
    S["rgg"] = nc.dram_tensor(pfx + "s_rgg", [1024, TA], F32).ap()
    S["qT"] = nc.dram_tensor(pfx + "s_qT", [1024, TA], BF16).ap()
    S["kT"] = nc.dram_tensor(pfx + "s_kT", [1024, TA], BF16).ap()
    S["v"] = nc.dram_tensor(pfx + "s_v", [TA, 1024], BF16).ap()
    S["cq"] = nc.dram_tensor(pfx + "s_cq", [256, TA], F32).ap()
    S["ckv"] = nc.dram_tensor(pfx + "s_ckv", [256, TA], F32).ap()
    S["kr"] = nc.dram_tensor(pfx + "s_kr", [64, TA], F32).ap()
    S["krsw"] = nc.dram_tensor(pfx + "s_krsw", [64, TA], F32).ap()
    S["gate"] = nc.dram_tensor(pfx + "s_gate", [3072, TA], F32).ap()
    S["yaT"] = nc.dram_tensor(pfx + "s_yaT", [1024, TA], BF16).ap()
    S["ybT"] = nc.dram_tensor(pfx + "s_ybT", [1024, TA], BF16).ap()
    S["ycT"] = nc.dram_tensor(pfx + "s_ycT", [1024, TA], BF16).ap()
    S["mqn"] = nc.dram_tensor(pfx + "s_mqn", [1024, TA], BF16).ap()
    S["mqr"] = nc.dram_tensor(pfx + "s_mqr", [512, TA], BF16).ap()
    S["mkn"] = nc.dram_tensor(pfx + "s_mkn", [1024, TA], BF16).ap()
    S["mkr"] = nc.dram_tensor(pfx + "s_mkr", [64, TA], BF16).ap()
    S["mv"] = nc.dram_tensor(pfx + "s_mv", [TA, 1024], BF16).ap()
    return S


def phase_rg(P, nc, C, rg_wa, rg_wx, l, S, TA, chunks=range(8)):
    NS = min(2048, TA)
    with ExitStack() as es:
        sb = mk_sb(nc, es)
        pt = mk_pt(nc, es)
        mk_stage(P, sb)
        wab = sb("r_wab", [128, 8, 2, 128], BF16)
        cc_t = sb("r_c", [128, 8, 4], F32)
        xin = [sb(f"r_xin{i}", [128, NS + 3], F32) for i in range(2)]
        gg = [sb(f"r_gg{i}", [128, NS], F32) for i in range(2)]
        u = sb("r_u", [128, NS], F32)
        ubf = sb("r_ubf", [128, NS], BF16)
        r_t = sb("r_r", [128, NS], F32)
        i_t = sb("r_i", [128, NS], F32)
        a_t = sb("r_a", [128, NS], F32)
        m_t = sb("r_m", [128, NS], F32)
        h_t = sb("r_h", [128, NS], F32)
        y_t = [sb(f"r_y{i}", [128, NS], BF16) for i in range(2)]
        hl = sb("r_hl", [128, 2], F32)
        pa = [pt(f"r_pa{i}", [128, 512]) for i in range(2)]
        px = [pt(f"r_px{i}", [128, 512]) for i in range(2)]
        s_wab, s_c = P.slot(), P.slot()
        s_xin = [P.slot() for _ in range(2)]
        s_gg = [P.slot() for _ in range(2)]
        s_u, s_ubf, s_r, s_i, s_a, s_m, s_h, s_hl = [P.slot() for _ in range(8)]
        s_y = [P.slot() for _ in range(2)]
        s_pa = [P.slot() for _ in range(2)]
        s_px = [P.slot() for _ in range(2)]
        par = C.par
        P.begin()
        P.op("pool", lambda e: e.memset(wab[:], 0.0), writes=[s_wab])
        for gi, wsrc in enumerate((rg_wa, rg_wx)):
            wr = wsrc.rearrange("(c two) j k -> two j c k", two=2)
            for half in range(2):
                P.dma("pool", wab[half * 64:(half + 1) * 64, :, gi, half * 64:(half + 1) * 64], wr[half],
                      writes=[s_wab], pbase=half * 64)
        P.op("act", lambda e: e.activation(cc_t[:, :, 2], par[:, l, PC_LAM:PC_LAM + 8], AF.Exp, scale=-1.0),
             reads=[C.s_par], writes=[s_c])
        P.op("act", lambda e: e.activation(cc_t[:, :, 3], cc_t[:, :, 2], AF.Ln, bias=C.eps[:, 1:2]),
             reads=[s_c, C.s_eps], writes=[s_c])

        def cfin(e):
            e.tensor_scalar(cc_t[:, :, 0], cc_t[:, :, 3], -8.0, None, ALU.mult)
            return e.tensor_scalar(cc_t[:, :, 1], cc_t[:, :, 3], -16.0, None, ALU.mult)
        P.op("dve", cfin, reads=[s_c], writes=[s_c])
        it = 0
        pi = 0
        for cc in chunks:
            rows = slice(cc * 128, (cc + 1) * 128)
            for sg_ in range(TA // NS):
                t0 = sg_ * NS
                b = it % 2
                it += 1
                if t0 == 0:
                    P.op("pool", lambda e, b=b: e.memset(xin[b][:, 0:3], 0.0), writes=[s_xin[b]])
                    P.dma("sp", xin[b][:, 3:], S["rgx"][rows, 0:NS], writes=[s_xin[b]])
                else:
                    P.dma("sp", xin[b][:], S["rgx"][rows, t0 - 3:t0 + NS], writes=[s_xin[b]])
                P.dma("sp", gg[b][:], S["rgg"][rows, t0:t0 + NS], writes=[s_gg[b]])

                cw = lambda tap, cc=cc: par[:, l, PC_CONVW + tap * 8 + cc:PC_CONVW + tap * 8 + cc + 1]
                P.op("dve", lambda e, b=b, cc=cc, cw=cw: e.tensor_scalar(
                    u[:], xin[b][:, 0:NS], cw(0), par[:, l, PC_CONVB + cc:PC_CONVB + cc + 1], ALU.mult, ALU.add),
                    reads=[s_xin[b], C.s_par], writes=[s_u])
                for tap in range(1, 4):
                    P.op("dve", lambda e, b=b, tap=tap, cw=cw: e.scalar_tensor_tensor(
                        u[:], xin[b][:, tap:tap + NS], cw(tap), u[:], ALU.mult, ALU.add),
                        reads=[s_xin[b], C.s_par, s_u], writes=[s_u])
                P.op("pool", lambda e: e.tensor_copy(ubf[:], u[:]), reads=[s_u], writes=[s_ubf])
                for blk in range(NS // 512):
                    j = pi % 2
                    pi += 1
                    bs = slice(blk * 512, (blk + 1) * 512)

                    def mm(e, j=j, bs=bs, cc=cc):
                        e.matmul(pa[j][:], wab[:, cc, 0, :], ubf[:, bs], start=True, stop=True)
                        return e.matmul(px[j][:], wab[:, cc, 1, :], ubf[:, bs], start=True, stop=True)
                    P.op("pe", mm, reads=[s_wab, s_ubf], writes=[s_pa[j], s_px[j]])

                    def sig(e, j=j, bs=bs, cc=cc):
                        e.activation(r_t[:, bs], pa[j][:], AF.Sigmoid, bias=par[:, l, PC_RGBA + cc:PC_RGBA + cc + 1])
                        return e.activation(i_t[:, bs], px[j][:], AF.Sigmoid, bias=par[:, l, PC_RGBX + cc:PC_RGBX + cc + 1])
                    P.op("act", sig, reads=[s_pa[j], s_px[j], C.s_par], writes=[s_r, s_i])

                def aexp(e, cc=cc):
                    e.activation(a_t[:], r_t[:], AF.Exp, scale=cc_t[:, cc, 0:1])
                    return e.activation(m_t[:], r_t[:], AF.Exp, scale=cc_t[:, cc, 1:2])
                P.op("act", aexp, reads=[s_r, s_c], writes=[s_a, s_m])
                P.op("act", lambda e: e.activation(m_t[:], m_t[:], AF.Sqrt, bias=C.eps[:, 1:2], scale=-1.0),
                     reads=[s_m, C.s_eps], writes=[s_m])
                P.op("pool", lambda e: e.tensor_tensor(i_t[:], i_t[:], m_t[:], ALU.mult), reads=[s_i, s_m], writes=[s_i])
                P.op("dve", lambda e: e.tensor_tensor(i_t[:], i_t[:], u[:], ALU.mult), reads=[s_i, s_u], writes=[s_i])
                if t0 == 0:
                    P.op("dve", lambda e: e.tensor_tensor_scan(h_t[:], a_t[:], i_t[:], 0.0, ALU.mult, ALU.add),
                         reads=[s_a, s_i], writes=[s_h])
                else:
                    P.op("dve", lambda e: e.tensor_tensor_scan(h_t[:], a_t[:], i_t[:], hl[:, 0:1], ALU.mult, ALU.add),
                         reads=[s_a, s_i, s_hl], writes=[s_h])
                P.op("dve", lambda e: e.tensor_copy(hl[:, 0:1], h_t[:, NS - 1:NS]), reads=[s_h], writes=[s_hl])
                P.op("pool", lambda e, b=b: e.tensor_tensor(y_t[b][:], h_t[:], gg[b][:], ALU.mult),
                     reads=[s_h, s_gg[b]], writes=[s_y[b]])
                P.dma("sp", S["yaT"][rows, t0:t0 + NS], y_t[b][:], reads=[s_y[b]])
        P.end()


import math
TWO_PI = 2.0 * math.pi
CW1 = 6.28125
CW2 = TWO_PI - CW1
MLA_SCALE = 192 ** -0.5


def phase_rope(P, nc, C, pos_d, ropec_d, R, TA):
    N = min(2048, TA)
    with ExitStack() as es:
        sb = mk_sb(nc, es)
        rc = sb("rp_c", [64, 2], F32)
        pi_ = sb("rp_pi", [64, N], I32)
        ang = sb("rp_ang", [64, N], F32)
        kf = sb("rp_kf", [64, N], F32)
        ki = sb("rp_ki", [64, N], I32)
        r = sb("rp_r", [64, N], F32)
        m = sb("rp_m", [64, N], F32)
        o = [sb(f"rp_o{i}", [64, N], F32) for i in range(4)]
        s_rc, s_pi, s_w = P.slot(), P.slot(), P.slot()
        s_o = [P.slot() for _ in range(4)]
        P.begin()
        P.dma("sp", rc[:], ropec_d, writes=[s_rc])
        for sg_ in range(TA // N):
            t0 = sg_ * N
            P.dma("sp", pi_[:], pos_d[0:1, t0:t0 + N].broadcast_to([64, N]), writes=[s_pi])

            def D1(fn, reads=(), extra_w=()):
                P.op("dve", fn, reads=[s_w] + list(reads), writes=[s_w] + list(extra_w))

            def wrap():
                D1(lambda e: e.tensor_scalar(m[:], r[:], math.pi, -TWO_PI, ALU.is_gt, ALU.mult))
                D1(lambda e: e.tensor_tensor(r[:], r[:], m[:], ALU.add))
                D1(lambda e: e.tensor_scalar(m[:], r[:], -math.pi, TWO_PI, ALU.is_lt, ALU.mult))
                D1(lambda e: e.tensor_tensor(r[:], r[:], m[:], ALU.add))

            D1(lambda e: e.tensor_copy(ang[:], pi_[:]), reads=[s_pi])
            D1(lambda e: e.tensor_scalar(ang[:], ang[:], rc[:, 0:1], None, ALU.mult), reads=[s_rc])
            D1(lambda e: e.tensor_scalar(ki[:], ang[:], 1.0 / TWO_PI, None, ALU.mult))
            D1(lambda e: e.tensor_copy(kf[:], ki[:]))
            D1(lambda e: e.scalar_tensor_tensor(r[:], kf[:], -CW1, ang[:], ALU.mult, ALU.add), reads=[s_o[0], s_o[1]])
            D1(lambda e: e.scalar_tensor_tensor(r[:], kf[:], -CW2, r[:], ALU.mult, ALU.add))
            wrap()
            P.op("act", lambda e: e.activation(o[1][:], r[:], AF.Sin), reads=[s_w], writes=[s_o[1]])
            D1(lambda e: e.tensor_scalar(r[:], r[:], math.pi / 2, None, ALU.add), reads=[s_o[1]])
            wrap()
            P.op("act", lambda e: e.activation(o[0][:], r[:], AF.Sin), reads=[s_w], writes=[s_o[0]])
            P.op("dve", lambda e: e.tensor_scalar(o[1][:], o[1][:], rc[:, 1:2], None, ALU.mult), reads=[s_o[1], s_rc], writes=[s_o[1]])
            P.op("dve", lambda e: e.tensor_scalar(o[2][:], o[0][:], MLA_SCALE, None, ALU.mult), reads=[s_o[0]], writes=[s_o[2]])
            P.op("dve", lambda e: e.tensor_scalar(o[3][:], o[1][:], MLA_SCALE, None, ALU.mult), reads=[s_o[1]], writes=[s_o[3]])
            for i, nm in enumerate(("cos", "sin", "cosq", "sinq")):
                P.dma("sp", R[nm][:, t0:t0 + N], o[i][:], reads=[s_o[i]])
        P.end()


def alloc_rope(nc, TA):
    return {nm: nc.dram_tensor("r_" + nm, [64, TA], F32).ap() for nm in ("cos", "sin", "cosq", "sinq")}


def phase_mlaprep(P, nc, C, w_uq, w_ukv, l, S, R, TA, heads=range(8)):
    heads = list(heads)
    NH = len(heads)
    wq3 = w_uq.rearrange("(k p) n -> p k n", p=128)
    wq4 = w_uq.rearrange("(k p) (h c) -> p k h c", p=128, c=192)
    wkv4 = w_ukv.rearrange("(k p) (h c) -> p k h c", p=128, c=256)
    with ExitStack() as es:
        sb = mk_sb(nc, es)
        pt = mk_pt(nc, es)
        mk_stage(P, sb)
        wq = sb("m_wq", [128, 2, 8, 192], BF16)
        wqs = sb("m_wqs", [128, 2, 8, 64], BF16)
        wkv = sb("m_wkv", [128, 2, 8, 256], BF16)
        wv = sb("m_wv", [128, 2, NH, 128], BF16)
        cin = [sb(f"m_cin{i}", [128, 2, 512], F32) for i in range(2)]
        cn = [sb(f"m_cn{i}", [128, 2, 512], BF16) for i in range(2)]
        sq = sb("m_sq", [128, 2, 512], BF16)
        lnv = sb("m_lnv", [128, 512], F32)
        rstd = sb("m_rstd", [128, 512], F32)
        tab = [sb(f"m_tab{i}", [64, 512], F32) for i in range(4)]
        krt = [sb(f"m_kr{i}", [64, 512], F32) for i in range(2)]
        t1 = sb("m_t1", [64, 512], F32)
        t2 = sb("m_t2", [64, 512], F32)
        NST = 4
        stb = [sb(f"m_stb{i}", [128, 512], BF16) for i in range(NST)]
        pss = pt("m_pss", [128, 512])
        NPW = 4
        pw = [pt(f"m_pw{i}", [128, 512]) for i in range(NPW)]
        pr = [pt(f"m_pr{i}", [64, 512]) for i in range(2)]
        s_w = P.slot()
        s_cin = [P.slot() for _ in range(2)]
        s_cn = [P.slot() for _ in range(2)]
        s_sq, s_ln, s_rstd, s_pss = P.slot(), P.slot(), P.slot(), P.slot()
        s_tab = [P.slot() for _ in range(4)]
        s_kr = [P.slot() for _ in range(2)]
        s_t1, s_t2 = P.slot(), P.slot()
        s_stb = [P.slot() for _ in range(NST)]
        s_pw = [P.slot() for _ in range(NPW)]
        s_pr = [P.slot() for _ in range(2)]
        P.begin()
        for k in range(2):
            P.dma("pool", wq[:, k], wq4[:, k], writes=[s_w], part=(k > 0))
        for k in range(2):
            P.dma("pool", wqs[:, k, :, 0:32], wq4[:, k, :, 160:192], writes=[s_w], part=True)
            P.dma("pool", wqs[:, k, :, 32:64], wq4[:, k, :, 128:160], writes=[s_w], part=True)
        for k in range(2):
            P.dma("pool", wkv[:, k], wkv4[:, k], writes=[s_w], part=True)
        for hi, h in enumerate(heads):
            P.dma("pool", wv[:, :, hi, :], wkv4[:, :, h, 128:256], writes=[s_w], part=True)
        pi = 0
        si = 0

        def evac_store(src_ap, s_src, M, dst_ap, scale=None):
            nonlocal si
            sj = si % NST
            si += 1
            if scale is None:
                P.op("dve", lambda e: e.tensor_copy(stb[sj][:M, :], src_ap), reads=[s_src], writes=[s_stb[sj]])
            else:
                P.op("dve", lambda e: e.tensor_scalar(stb[sj][:M, :], src_ap, scale, None, ALU.mult), reads=[s_src], writes=[s_stb[sj]])
            P.dma("sp", dst_ap, stb[sj][:M, :], reads=[s_stb[sj]])

        for tix in range(TA // 512):
            t0 = tix * 512
            tsl = slice(t0, t0 + 512)
            P.dma("sp", cin[0][:], S["cq"].rearrange("(k p) t -> p k t", p=128)[:, :, tsl], writes=[s_cin[0]])
            P.dma("sp", cin[1][:], S["ckv"].rearrange("(k p) t -> p k t", p=128)[:, :, tsl], writes=[s_cin[1]])
            for i, nm in enumerate(("cos", "sin", "cosq", "sinq")):
                P.dma("sp", tab[i][:], R[nm][:, tsl], writes=[s_tab[i]])
            P.dma("sp", krt[0][:], S["kr"][:, tsl], writes=[s_kr[0]])
            P.dma("sp", krt[1][:], S["krsw"][:, tsl], writes=[s_kr[1]])
            for i, pc in enumerate((PC_QN, PC_KVN)):
                rmsnorm_tile(P, C, cin[i][:], s_cin[i], cn[i][:], s_cn[i],
                             lambda k, pc=pc: C.par[:, l, pc + k:pc + k + 1], 2,
                             sq, s_sq, pss, s_pss, lnv, s_ln, rstd, s_rstd, 512, 1.0 / 256)
            P.op("dve", lambda e: e.tensor_tensor(t1[:], krt[0][:], tab[0][:], ALU.mult), reads=[s_kr[0], s_tab[0]], writes=[s_t1])
            P.op("dve", lambda e: e.tensor_tensor(t2[:], krt[1][:], tab[1][:], ALU.mult), reads=[s_kr[1], s_tab[1]], writes=[s_t2])
            sj = si % NST
            si += 1
            P.op("dve", lambda e, sj=sj: e.tensor_tensor(stb[sj][:64, :], t1[:], t2[:], ALU.add), reads=[s_t1, s_t2], writes=[s_stb[sj]])
            P.dma("sp", S["mkr"][:, tsl], stb[sj][:64, :], reads=[s_stb[sj]])
            for hi, h in enumerate(heads):
                j = pi % NPW
                pi += 1

                def mm(e, j=j, h=h):
                    e.matmul(pw[j][:], wq[:, 0, h, 0:128], cn[0][:, 0, :], start=True, stop=False)
                    return e.matmul(pw[j][:], wq[:, 1, h, 0:128], cn[0][:, 1, :], start=False, stop=True)
                P.op("pe", mm, reads=[s_w, s_cn[0]], writes=[s_pw[j]])
                evac_store(pw[j][:], s_pw[j], 128, S["mqn"][h * 128:(h + 1) * 128, tsl], scale=MLA_SCALE)
                def mmr(e, h=h):
                    e.matmul(pr[0][:], wq[:, 0, h, 128:192], cn[0][:, 0, :], start=True, stop=False)
                    e.matmul(pr[0][:], wq[:, 1, h, 128:192], cn[0][:, 1, :], start=False, stop=True)
                    e.matmul(pr[1][:], wqs[:, 0, h, :], cn[0][:, 0, :], start=True, stop=False)
                    return e.matmul(pr[1][:], wqs[:, 1, h, :], cn[0][:, 1, :], start=False, stop=True)
                P.op("pe", mmr, reads=[s_w, s_cn[0]], writes=[s_pr[0], s_pr[1]])
                P.op("dve", lambda e: e.tensor_tensor(t1[:], pr[0][:], tab[2][:], ALU.mult), reads=[s_pr[0], s_tab[2]], writes=[s_t1])
                P.op("dve", lambda e: e.tensor_tensor(t2[:], pr[1][:], tab[3][:], ALU.mult), reads=[s_pr[1], s_tab[3]], writes=[s_t2])
                sj = si % NST
                si += 1
                P.op("dve", lambda e, sj=sj: e.tensor_tensor(stb[sj][:64, :], t1[:], t2[:], ALU.add), reads=[s_t1, s_t2], writes=[s_stb[sj]])
                P.dma("sp", S["mqr"][h * 64:(h + 1) * 64, tsl], stb[sj][:64, :], reads=[s_stb[sj]])
                j = pi % NPW
                pi += 1

                def mmk(e, j=j, h=h):
                    e.matmul(pw[j][:], wkv[:, 0, h, 0:128], cn[1][:, 0, :], start=True, stop=False)
                    return e.matmul(pw[j][:], wkv[:, 1, h, 0:128], cn[1][:, 1, :], start=False, stop=True)
                P.op("pe", mmk, reads=[s_w, s_cn[1]], writes=[s_pw[j]])
                evac_store(pw[j][:], s_pw[j], 128, S["mkn"][h * 128:(h + 1) * 128, tsl])
            for tb in range(4):
                for c0 in range(0, NH * 128, 512):
                    n = min(512, NH * 128 - c0)
                    j = pi % NPW
                    pi += 1

                    def mmv(e, j=j, tb=tb, c0=c0, n=n):
                        wv2 = wv[:].rearrange("p k h c -> p k (h c)")
                        e.matmul(pw[j][:, :n], cn[1][:, 0, tb * 128:(tb + 1) * 128], wv2[:, 0, c0:c0 + n], start=True, stop=False)
                        return e.matmul(pw[j][:, :n], cn[1][:, 1, tb * 128:(tb + 1) * 128], wv2[:, 1, c0:c0 + n], start=False, stop=True)
                    P.op("pe", mmv, reads=[s_w, s_cn[1]], writes=[s_pw[j]])
                    sj = si % NST
                    si += 1
                    P.op("dve", lambda e, j=j, sj=sj, n=n: e.tensor_copy(stb[sj][:, :n], pw[j][:, :n]), reads=[s_pw[j]], writes=[s_stb[sj]])
                    col0 = heads[0] * 128 + c0
                    P.dma("sp", S["mv"][t0 + tb * 128:t0 + (tb + 1) * 128, col0:col0 + n], stb[sj][:, :n], reads=[s_stb[sj]])
        P.end()


def setup_attn_consts(P, nc, C, es):
    C.negtri = es.enter_context(nc.sbuf_tensor("c_negtri", [128, 128], BF16))
    C.s_negtri = P.slot("negtri")
    P.begin()
    P.op("pool", lambda e: e.memset(C.negtri[:], -1.0), writes=[C.s_negtri])
    P.op("pool", lambda e: e.affine_select(C.negtri[:], C.negtri[:], [[-1, 128]], ALU.is_ge, 0.0, base=0, channel_multiplier=1),
         reads=[C.s_negtri], writes=[C.s_negtri])
    P.end()


def phase_sb(P, nc, C, S, TA, heads=range(8)):
    heads = list(heads)
    NB = TA // 128
    NG = TA // 512
    v3 = S["v"].rearrange("(b p) c -> p b c", p=128)
    with ExitStack() as es:
        sb = mk_sb(nc, es)
        pt = mk_pt(nc, es)
        mk_stage(P, sb)
        kT = [sb(f"a_kT{i}", [128, TA], BF16) for i in range(2)]
        qT = [sb(f"a_qT{i}", [128, TA], BF16) for i in range(2)]
        vt = [sb(f"a_v{i}", [128, NB, 128], BF16) for i in range(2)]
        ef = [sb(f"a_ef{i}", [128, 512], F32) for i in range(2)]
        spb = [sb(f"a_spb{i}", [128, 512], BF16) for i in range(2)]
        lw = [sb(f"a_lw{i}", [128, 512], F32) for i in range(2)]
        wb = [sb(f"a_wb{i}", [128, 512], BF16) for i in range(2)]
        cB = [sb(f"a_cB{i}", [128, 512], F32) for i in range(2)]
        yst = [sb(f"a_yst{i}", [128, 512], BF16) for i in range(2)]
        pA = [pt(f"a_pA{i}", [128, 512]) for i in range(2)]
        pB = [pt(f"a_pB{i}", [128, 512]) for i in range(2)]
        pC = [pt(f"a_pC{i}", [128, 512]) for i in range(2)]
        pY = [pt(f"a_pY{i}", [128, 512]) for i in range(2)]
        mk = lambda n: [P.slot() for _ in range(n)]
        s_kT, s_qT, s_vt, s_ef, s_spb, s_lw, s_wb, s_cB, s_yst = [mk(2) for _ in range(9)]
        s_pA, s_pB, s_pC, s_pY = [mk(2) for _ in range(4)]
        units = []
        for hi, h in enumerate(heads):
            for g in range(NG):
                nkb = 4 * g + 4
                for kb in reversed(range(nkb)):
                    units.append((hi, h, g, kb, kb == nkb - 1, kb == 0))
        gidx = {}
        hstart = {}
        for n_, u in enumerate(units):
            key = (u[0], u[2])
            if key not in gidx:
                gidx[key] = len(gidx)
            if u[0] not in hstart:
                hstart[u[0]] = n_
        P.begin()

        def load_head(hi, h):
            hb = hi % 2
            rows = slice(h * 128, (h + 1) * 128)
            P.dma("sp", kT[hb][:], S["kT"][rows, 0:TA], writes=[s_kT[hb]])
            P.dma("sp", qT[hb][:], S["qT"][rows, 0:TA], writes=[s_qT[hb]])
            P.dma("sp", vt[hb][:], v3[:, :, rows], writes=[s_vt[hb]])

        def st1(n):
            hi, h, g, kb, first, last = units[n]
            hb, j = hi % 2, n % 2
            if n == 0:
                load_head(0, heads[0])
            if n == hstart[hi] + 3 and hi + 1 < len(heads):
                load_head(hi + 1, heads[hi + 1])
            ks = slice(kb * 128, (kb + 1) * 128)
            qs = slice(g * 512, (g + 1) * 512)
            P.op("pe", lambda e: e.matmul(pA[j][:], kT[hb][:, ks], qT[hb][:, qs], start=True, stop=True),
                 reads=[s_kT[hb], s_qT[hb]], writes=[s_pA[j]])
            P.op("act", lambda e: e.activation(ef[j][:], pA[j][:], AF.Exp), reads=[s_pA[j]], writes=[s_ef[j]])
            P.op("act", lambda e: e.activation(spb[j][:], ef[j][:], AF.Ln, bias=C.eps[:, 1:2]),
                 reads=[s_ef[j], C.s_eps], writes=[s_spb[j]])
            i = kb - 4 * g
            if i >= 0:
                P.op("pool", lambda e: e.affine_select(spb[j][:], spb[j][:], [[1, 512]], ALU.is_gt, 0.0,
                                                       base=-128 * i, channel_multiplier=-1),
                     reads=[s_spb[j]], writes=[s_spb[j]])

        def st2(n):
            hi, h, g, kb, first, last = units[n]
            hb, j = hi % 2, n % 2
            gb = gidx[(hi, g)] % 2
            ks = slice(kb * 128, (kb + 1) * 128)
            qs = slice(g * 512, (g + 1) * 512)
            if first:
                P.op("pool", lambda e: e.memset(cB[gb][:], 0.0), writes=[s_cB[gb]])

            def mm(e):
                e.matmul(pB[j][:], kT[hb][:, ks], qT[hb][:, qs], start=True, stop=False)
                e.matmul(pB[j][:], C.negtri[:], spb[j][:], start=False, stop=True)
                return e.matmul(pC[j][:], C.ones[:], spb[j][:], start=True, stop=True)
            P.op("pe", mm, reads=[s_kT[hb], s_qT[hb], s_spb[j], C.s_negtri, C.s_ones], writes=[s_pB[j], s_pC[j]])
            P.op("dve", lambda e: e.tensor_tensor(lw[j][:], pB[j][:], cB[gb][:], ALU.subtract),
                 reads=[s_pB[j], s_cB[gb]], writes=[s_lw[j]])
            if not last:
                P.op("dve", lambda e: e.tensor_tensor(cB[gb][:], pC[j][:], cB[gb][:], ALU.add),
                     reads=[s_pC[j], s_cB[gb]], writes=[s_cB[gb]])
            P.op("act", lambda e: e.activation(wb[j][:], lw[j][:], AF.Exp), reads=[s_lw[j]], writes=[s_wb[j]])
            i = kb - 4 * g
            if i >= 0:
                P.op("pool", lambda e: e.affine_select(wb[j][:], wb[j][:], [[1, 512]], ALU.is_gt, 0.0,
                                                       base=-128 * i, channel_multiplier=-1),
                     reads=[s_wb[j]], writes=[s_wb[j]])

        def st3(n):
            hi, h, g, kb, first, last = units[n]
            hb, j = hi % 2, n % 2
            gb = gidx[(hi, g)] % 2
            P.op("pe", lambda e: e.matmul(pY[gb][:], vt[hb][:, kb, :], wb[j][:], start=first, stop=last),
                 reads=[s_vt[hb], s_wb[j]], writes=[s_pY[gb]])
            if last:
                P.op("dve", lambda e: e.tensor_copy(yst[gb][:], pY[gb][:]), reads=[s_pY[gb]], writes=[s_yst[gb]])
                P.dma("sp", S["ybT"][h * 128:(h + 1) * 128, g * 512:(g + 1) * 512], yst[gb][:], reads=[s_yst[gb]])

        NU = len(units)
        for step in range(NU + 2):
            if step < NU:
                st1(step)
            if 0 <= step - 1 < NU:
                st2(step - 1)
            if 0 <= step - 2 < NU:
                st3(step - 2)
        P.end()


def phase_mla(P, nc, C, S, TA, heads=range(8)):
    heads = list(heads)
    NB = TA // 128
    NG = TA // 512
    v3 = S["mv"].rearrange("(b p) c -> p b c", p=128)
    with ExitStack() as es:
        sb = mk_sb(nc, es)
        pt = mk_pt(nc, es)
        mk_stage(P, sb)
        kr = sb("b_kr", [64, TA], BF16)
        kn = [sb(f"b_kn{i}", [128, TA], BF16) for i in range(2)]
        qn = [sb(f"b_qn{i}", [128, TA], BF16) for i in range(2)]
        qr = [sb(f"b_qr{i}", [64, TA], BF16) for i in range(2)]
        vt = [sb(f"b_v{i}", [128, NB, 128], BF16) for i in range(2)]
        pb = [sb(f"b_pb{i}", [128, 512], BF16) for i in range(3)]
        lnd = sb("b_lnd", [128, 512], F32)
        rd = sb("b_rd", [128, 512], F32)
        yst = [sb(f"b_yst{i}", [128, 512], BF16) for i in range(2)]
        pS = [pt(f"b_pS{i}", [128, 512]) for i in range(3)]
        pN = [pt(f"b_pN{i}", [128, 512]) for i in range(2)]
        pD = [pt(f"b_pD{i}", [128, 512]) for i in range(2)]
        mk = lambda n: [P.slot() for _ in range(n)]
        s_kn, s_qn, s_qr, s_vt, s_yst, s_pN, s_pD = [mk(2) for _ in range(7)]
        s_pb, s_pS = mk(3), mk(3)
        s_kr, s_lnd, s_rd = P.slot(), P.slot(), P.slot()
        units = []
        for hi, h in enumerate(heads):
            for g in range(NG):
                nkb = 4 * g + 4
                for kb in range(nkb):
                    units.append((hi, h, g, kb, kb == 0, kb == nkb - 1))
        gidx = {}
        hstart = {}
        for n_, u in enumerate(units):
            key = (u[0], u[2])
            if key not in gidx:
                gidx[key] = len(gidx)
            if u[0] not in hstart:
                hstart[u[0]] = n_
        P.begin()
        P.dma("sp", kr[:], S["mkr"][:, 0:TA], writes=[s_kr])

        def load_head(hi, h):
            hb = hi % 2
            rows = slice(h * 128, (h + 1) * 128)
            P.dma("sp", kn[hb][:], S["mkn"][rows, 0:TA], writes=[s_kn[hb]])
            P.dma("sp", qn[hb][:], S["mqn"][rows, 0:TA], writes=[s_qn[hb]])
            P.dma("sp", qr[hb][:], S["mqr"][h * 64:(h + 1) * 64, 0:TA], writes=[s_qr[hb]])
            P.dma("sp", vt[hb][:], v3[:, :, rows], writes=[s_vt[hb]])

        def st1(n):
            hi, h, g, kb, first, last = units[n]
            hb, j = hi % 2, n % 3
            if n == 0:
                load_head(0, heads[0])
            if n == hstart[hi] + 3 and hi + 1 < len(heads):
                load_head(hi + 1, heads[hi + 1])
            ks = slice(kb * 128, (kb + 1) * 128)
            qs = slice(g * 512, (g + 1) * 512)

            def mm(e):
                e.matmul(pS[j][:], kn[hb][:, ks], qn[hb][:, qs], start=True, stop=False)
                return e.matmul(pS[j][:], kr[:, ks], qr[hb][:, qs], start=False, stop=True)
            P.op("pe", mm, reads=[s_kn[hb], s_qn[hb], s_kr, s_qr[hb]], writes=[s_pS[j]])
            P.op("act", lambda e: e.activation(pb[j][:], pS[j][:], AF.Exp), reads=[s_pS[j]], writes=[s_pb[j]])
            i = kb - 4 * g
            if i >= 0:
                def msk(e):
                    ins = e.memset(pb[j][64:128, 128 * i:128 * i + 64], 0.0)
                    if i > 0:
                        ins = e.memset(pb[j][:, 0:128 * i], 0.0)
                    return ins
                P.op("pool", msk, reads=[s_pb[j]], writes=[s_pb[j]])

        def st2(n):
            hi, h, g, kb, first, last = units[n]
            hb, j = hi % 2, n % 3
            gb = gidx[(hi, g)] % 2

            def mm(e):
                e.matmul(pN[gb][:], vt[hb][:, kb, :], pb[j][:], start=first, stop=last)
                return e.matmul(pD[gb][:], C.ones[:], pb[j][:], start=first, stop=last)
            P.op("pe", mm, reads=[s_vt[hb], s_pb[j], C.s_ones], writes=[s_pN[gb], s_pD[gb]])
            if last:
                P.op("act", lambda e: e.activation(lnd[:], pD[gb][:], AF.Ln), reads=[s_pD[gb]], writes=[s_lnd])
                P.op("act", lambda e: e.activation(rd[:], lnd[:], AF.Exp, scale=-1.0), reads=[s_lnd], writes=[s_rd])
                P.op("dve", lambda e: e.tensor_tensor(yst[gb][:], pN[gb][:], rd[:], ALU.mult),
                     reads=[s_pN[gb], s_rd], writes=[s_yst[gb]])
                P.dma("sp", S["ycT"][h * 128:(h + 1) * 128, g * 512:(g + 1) * 512], yst[gb][:], reads=[s_yst[gb]])

        NU = len(units)
        for step in range(NU + 1):
            if step < NU:
                st1(step)
            if 0 <= step - 1 < NU:
                st2(step - 1)
        P.end()


def phase_merge(P, nc, C, xT, w_a, w_b, w_c, w_o, S, T, tok0=0):
    xT3 = xT.rearrange("(k p) t -> p k t", p=128)
    g4 = S["gate"].rearrange("(b k p) t -> p b k t", p=128, k=8)
    with ExitStack() as es:
        sb = mk_sb(nc, es)
        pt = mk_pt(nc, es)
        mk_stage(P, sb)
        W = [sb(f"g_w{i}", [128, KD, 1024], BF16) for i in range(4)]
        y = [sb(f"g_y{i}", [128, KD, 512], BF16) for i in range(3)]
        gt = [sb(f"g_gt{i}", [128, 3, 512], F32) for i in range(2)]
        m = [sb(f"g_m{i}", [128, 512], F32) for i in range(3)]
        mg = sb("g_mg", [128, KD, 512], BF16)
        xg = sb("g_xg", [128, KD, 512], F32)
        pP = [[pt(f"g_p{b}{i}", [128, 512]) for i in range(2)] for b in range(3)]
        pO = [pt(f"g_po{i}", [128, 512]) for i in range(2)]
        mk = lambda n: [P.slot() for _ in range(n)]
        s_W, s_y, s_gt, s_m, s_pO = mk(4), mk(3), mk(2), mk(3), mk(2)
        s_pP = [mk(2) for _ in range(3)]
        s_mg, s_xg = P.slot(), P.slot()
        P.begin()
        for i, w in enumerate((w_a, w_b, w_c, w_o)):
            w3 = w.rearrange("(k p) n -> p k n", p=128)
            for k in range(KD):
                P.dma("pool", W[i][:, k, :], w3[:, k, :], writes=[s_W[i]], part=(k > 0))
        it = 0
        for tix in range(T // 512):
            t0 = tix * 512
            ta = tok0 + t0
            for i, nm in enumerate(("yaT", "ybT", "ycT")):
                P.dma("sp", y[i][:], S[nm].rearrange("(k p) t -> p k t", p=128)[:, :, ta:ta + 512], writes=[s_y[i]])
            P.dma("sp", xg[:], xT3[:, :, t0:t0 + 512], writes=[s_xg])
            for oc in range(KD):
                j = it % 2
                it += 1
                P.dma("sp", gt[j][:], g4[:, :, oc, ta:ta + 512], writes=[s_gt[j]])
                cs = slice(oc * 128, (oc + 1) * 128)

                def mm(e, j=j, cs=cs):
                    ins = None
                    for b in range(3):
                        for k in range(KD):
                            ins = e.matmul(pP[b][j][:], W[b][:, k, cs], y[b][:, k, :], start=(k == 0), stop=(k == KD - 1))
                    return ins
                P.op("pe", mm, reads=s_W[:3] + s_y, writes=[s_pP[b][j] for b in range(3)])
                for b in range(3):
                    P.op("dve", lambda e, b=b, j=j: e.tensor_tensor(m[b][:], pP[b][j][:], gt[j][:, b, :], ALU.mult),
                         reads=[s_pP[b][j], s_gt[j]], writes=[s_m[b]])
                P.op("pool", lambda e: e.tensor_tensor(m[0][:], m[0][:], m[1][:], ALU.add), reads=[s_m[0], s_m[1]], writes=[s_m[0]])
                P.op("pool", lambda e, oc=oc: e.tensor_tensor(mg[:, oc, :], m[0][:], m[2][:], ALU.add),
                     reads=[s_m[0], s_m[2]], writes=[s_mg])
            for oc in range(KD):
                j = oc % 2
                cs = slice(oc * 128, (oc + 1) * 128)

                def mm2(e, j=j, cs=cs):
                    ins = None
                    for k in range(KD):
                        ins = e.matmul(pO[j][:], W[3][:, k, cs], mg[:, k, :], start=(k == 0), stop=(k == KD - 1))
                    return ins
                P.op("pe", mm2, reads=[s_W[3], s_mg], writes=[s_pO[j]])
                P.op("dve", lambda e, j=j, oc=oc: e.tensor_tensor(xg[:, oc, :], pO[j][:], xg[:, oc, :], ALU.add),
                     reads=[s_pO[j], s_xg], writes=[s_xg])
            P.dma("sp", xT3[:, :, t0:t0 + 512], xg[:], reads=[s_xg])
        P.end()


def phase_final(P, nc, C, xT, outT, T):
    xT3 = xT.rearrange("(k p) t -> p k t", p=128)
    oT3 = outT.rearrange("(k p) t -> p k t", p=128)
    with ExitStack() as es:
        sb = mk_sb(nc, es)
        pt = mk_pt(nc, es)
        mk_stage(P, sb)
        xg = [sb(f"n_xg{i}", [128, KD, 512], F32) for i in range(2)]
        og = [sb(f"n_og{i}", [128, KD, 512], F32) for i in range(2)]
        sq = sb("n_sq", [128, KD, 512], BF16)
        lnv = sb("n_lnv", [128, 512], F32)
        rstd = sb("n_rstd", [128, 512], F32)
        pss = pt("n_pss", [128, 512])
        s_xg = [P.slot() for _ in range(2)]
        s_og = [P.slot() for _ in range(2)]
        s_sq, s_ln, s_rstd, s_pss = P.slot(), P.slot(), P.slot(), P.slot()
        P.begin()
        for tix in range(T // 512):
            t0 = tix * 512
            b = tix % 2
            P.dma("sp", xg[b][:], xT3[:, :, t0:t0 + 512], writes=[s_xg[b]])
            rmsnorm_tile(P, C, xg[b][:], s_xg[b], og[b][:], s_og[b],
                         lambda k: C.par[:, 0, PC_FIN + k:PC_FIN + k + 1], KD,
                         sq, s_sq, pss, s_pss, lnv, s_ln, rstd, s_rstd, 512, 1.0 / D)
            P.dma("sp", oT3[:, :, t0:t0 + 512], og[b][:], reads=[s_og[b]])
        P.end()


from concourse.bass_utils import run_bass_kernel_spmd

DEPTH = 4
SEQ = 4096
BATCH = 4
WNAMES = ["ffn1_w_gate_up", "ffn1_w_down", "w_in", "rg_w_a", "rg_w_x", "mla_w_uq", "mla_w_ukv",
          "w_branch_a", "w_branch_b", "w_branch_c", "w_out", "ffn2_w_gate_up", "ffn2_w_down"]


ALLPH = ("ffn1", "win", "rg", "prep", "sb", "mla", "merge", "ffn2")


def build_program(wshapes, depth=DEPTH, seq=SEQ, sel=ALLPH):
    nc = bass.Bass("TRN2", target_bir_lowering=False)
    TA = seq
    xT_in = nc.dram_tensor("xT_in", [D, TA], F32, kind="ExternalInput").ap()
    pos_d = nc.dram_tensor("pos", [1, TA], I32, kind="ExternalInput").ap()
    par_d = nc.dram_tensor("par", [DEPTH, 128, NPAR], F32, kind="ExternalInput").ap()
    ropec_d = nc.dram_tensor("ropec", [64, 2], F32, kind="ExternalInput").ap()
    Wd = {n: nc.dram_tensor(n, list(wshapes[n]), F32, kind="ExternalInput").ap() for n in WNAMES}
    outT = nc.dram_tensor("outT", [D, TA], F32, kind="ExternalOutput").ap()
    xs = nc.dram_tensor("xs", [D, TA], F32).ap()
    S = alloc_scratch(nc, TA)
    R = alloc_rope(nc, TA)
    with ExitStack() as es:
        P = Prog(nc)
        C = setup_consts(P, nc, es, par_d, DEPTH)
        setup_attn_consts(P, nc, C, es)
        P.begin()
        P.dma("sp", xs, xT_in)
        P.end()
        phase_rope(P, nc, C, pos_d, ropec_d, R, TA)
        for l in range(depth):
            if "ffn1" in sel:
                phase_ffn(P, nc, C, xs, Wd["ffn1_w_gate_up"][l], Wd["ffn1_w_down"][l], l, PC_FFN1, TA)
            if "win" in sel:
                phase_win(P, nc, C, xs, Wd["w_in"][l], l, S, TA)
            if "rg" in sel:
                phase_rg(P, nc, C, Wd["rg_w_a"][l], Wd["rg_w_x"][l], l, S, TA)
            if "prep" in sel:
                phase_mlaprep(P, nc, C, Wd["mla_w_uq"][l], Wd["mla_w_ukv"][l], l, S, R, TA)
            if "sb" in sel:
                phase_sb(P, nc, C, S, TA)
            if "mla" in sel:
                phase_mla(P, nc, C, S, TA)
            if "merge" in sel:
                phase_merge(P, nc, C, xs, Wd["w_branch_a"][l], Wd["w_branch_b"][l], Wd["w_branch_c"][l], Wd["w_out"][l], S, TA)
            if "ffn2" in sel:
                phase_ffn(P, nc, C, xs, Wd["ffn2_w_gate_up"][l], Wd["ffn2_w_down"][l], l, PC_FFN2, TA)
        phase_final(P, nc, C, xs, outT, TA)
    return nc


def pack_params(inp):
    par = np.zeros((DEPTH, 128, NPAR), np.float32)
    pm = lambda v: np.asarray(v, np.float32).reshape(-1, 128).T
    for l in range(DEPTH):
        par[l, :, PC_FFN1:PC_FFN1 + 8] = pm(inp["ffn1_norm"][l])
        par[l, :, PC_MIX:PC_MIX + 8] = pm(inp["mix_norm"][l])
        par[l, :, PC_FFN2:PC_FFN2 + 8] = pm(inp["ffn2_norm"][l])
        for tap in range(4):
            par[l, :, PC_CONVW + tap * 8:PC_CONVW + tap * 8 + 8] = pm(inp["conv_w"][l][tap])
        par[l, :, PC_CONVB:PC_CONVB + 8] = pm(inp["conv_b"][l])
        par[l, :, PC_RGBA:PC_RGBA + 8] = pm(inp["rg_b_a"][l])
        par[l, :, PC_RGBX:PC_RGBX + 8] = pm(inp["rg_b_x"][l])
        par[l, :, PC_LAM:PC_LAM + 8] = pm(inp["rg_lambda"][l])
        par[l, :, PC_QN:PC_QN + 2] = pm(inp["mla_q_norm"][l])
        par[l, :, PC_KVN:PC_KVN + 2] = pm(inp["mla_kv_norm"][l])
        par[l, :, PC_FIN:PC_FIN + 8] = pm(inp["final_norm"])
    return par


def rope_consts():
    inv = (np.float32(10000.0) ** (-np.arange(0, 64, 2, dtype=np.float32) / np.float32(64))).astype(np.float32)
    ropec = np.zeros((64, 2), np.float32)
    ropec[:, 0] = np.concatenate([inv, inv])
    ropec[:, 1] = np.concatenate([-np.ones(32, np.float32), np.ones(32, np.float32)])
    return ropec


def kernel(**inputs):
    inp = {k: np.asarray(v) for k, v in inputs.items()}
    x = inp["x"].astype(np.float32, copy=False)
    pos = inp["positions"].astype(np.int32, copy=False)
    par = pack_params(inp)
    ropec = rope_consts()
    W = {n: np.ascontiguousarray(inp[n], dtype=np.float32) for n in WNAMES}
    nc = build_program({n: W[n].shape for n in WNAMES})
    in_maps = []
    for c in range(8):
        b = c % BATCH
        m = {"xT_in": np.ascontiguousarray(x[b].T), "pos": np.ascontiguousarray(pos[b][None, :]),
             "par": par, "ropec": ropec}
        m.update(W)
        in_maps.append(m)
    res = run_bass_kernel_spmd(nc, in_maps, core_ids=list(range(8)))
    out = np.empty((BATCH, SEQ, D), np.float32)
    for b in range(BATCH):
        out[b] = np.asarray(res.results[b]["outT"]).T
    return out
```

```python
import numpy as np
import concourse.bass as bass
import concourse.mybir as mybir

F32 = mybir.dt.float32
BF16 = mybir.dt.bfloat16
I32 = mybir.dt.int32
AF = mybir.ActivationFunctionType
ALU = mybir.AluOpType

ENGS = ("sp", "pool", "act", "dve", "pe")
NDMA = 40


class Slot:
    def __init__(self, P, name=""):
        self.name = name
        self.last_w = None
        self.readers = []
        self.ld_sem = None
        self.st_sem = None
        P.slots.append(self)

    def reset(self):
        self.last_w = None
        self.readers = []
        self.ld_sem = None
        self.st_sem = None


class Prog:
    def __init__(self, nc):
        self.nc = nc
        self.slots = []
        self.sets = []
        for i in range(2):
            es = {e: nc.alloc_semaphore(name=f"s{i}_{e}") for e in ENGS}
            ds = [nc.alloc_semaphore(name=f"s{i}_d{j}") for j in range(NDMA)]
            self.sets.append((es, ds))
        self.phase_idx = 0
        self.in_phase = False

    def slot(self, name=""):
        return Slot(self, name)

    def begin(self):
        assert not self.in_phase
        self.in_phase = True
        self.es, ds = self.sets[self.phase_idx % 2]
        self.dma_free = list(ds)
        self.dma_cnt = {}
        self.ops = {e: [] for e in ENGS}
        self.cnt = {e: 0 for e in ENGS}
        self.waited = {e: {} for e in ENGS}
        for s in self.slots:
            s.reset()
        self.misc_sem = self._new_dma_sem()

    def _new_dma_sem(self):
        assert self.dma_free, "out of DMA semaphores"
        s = self.dma_free.pop()
        self.dma_cnt[s] = 0
        return s

    def _deps(self, eng, reads, writes, skip_sem=None):
        toks = []
        for s in reads:
            if s.last_w is not None:
                toks.append(s.last_w)
        for s in writes:
            if s.last_w is not None:
                toks.append(s.last_w)
            toks.extend(s.readers)
        need = {}
        for (sem, val, src) in toks:
            if src == "pe" and eng == "pe":
                continue
            if skip_sem is not None and sem is skip_sem:
                continue
            k = id(sem)
            if self.waited[eng].get(k, 0) >= val:
                continue
            if k not in need or need[k][1] < val:
                need[k] = (sem, val)
        waits = []
        for k, (sem, val) in need.items():
            self.waited[eng][k] = val
            waits.append((sem, val))
        return waits

    def op(self, eng, fn, reads=(), writes=()):
        waits = self._deps(eng, reads, writes)
        self.cnt[eng] += 1
        sem = self.es[eng]
        tok = (sem, self.cnt[eng], eng)
        wset = set(id(s) for s in writes)
        for s in reads:
            if id(s) not in wset:
                s.readers.append(tok)
        for s in writes:
            s.last_w = tok
            s.readers = []
        self.ops[eng].append((waits, fn, sem, 1))

    def set_stage(self, tiles, slots, size):
        self.stg = (tiles, slots, size)
        self.stg_i = 0

    def cast_load(self, out, in_, writes, pbase=0):
        shape = tuple(int(x) for x in out.shape)
        p, free = shape[0], shape[1:]
        n = 1
        for x in free:
            n *= x
        tiles, slots, size = self.stg
        if n > size:
            a = free[0]
            per = n // a
            step = max(1, size // per)
            for a0 in range(0, a, step):
                a1 = min(a, a0 + step)
                self.cast_load(out[:, a0:a1], in_[:, a0:a1], writes, pbase)
            return
        i = self.stg_i % len(tiles)
        self.stg_i += 1
        st = tiles[i][pbase:pbase + p, :n]
        if len(free) == 2:
            st = st.rearrange("p (a b) -> p a b", a=free[0])
        elif len(free) == 3:
            st = st.rearrange("p (a b c) -> p a b c", a=free[0], b=free[1])
        self.dma("sp", st, in_, writes=[slots[i]])
        self.op("pool", lambda e, out=out, st=st: e.tensor_copy(out, st), reads=[slots[i]], writes=list(writes))

    def dma(self, q, out, in_, reads=(), writes=(), part=False, **kw):
        if q == "pool":
            return self.cast_load(out, in_, writes, kw.get("pbase", 0))
        if writes:
            sl = writes[0]
            if sl.ld_sem is None:
                sl.ld_sem = self._new_dma_sem()
            sem = sl.ld_sem
        elif reads:
            sl = reads[0]
            if sl.st_sem is None:
                sl.st_sem = self._new_dma_sem()
            sem = sl.st_sem
        else:
            sem = self.misc_sem
        waits = self._deps(q, reads, writes, skip_sem=(sem if part else None))
        self.dma_cnt[sem] += 16
        tok = (sem, self.dma_cnt[sem], "dma")
        for s in reads:
            s.readers.append(tok)
        for s in writes:
            s.last_w = tok
            s.readers = []

        def fn(e, out=out, in_=in_, kw=kw):
            return e.dma_start(out=out, in_=in_, **kw)
        self.ops[q].append((waits, fn, sem, 16))

    def end(self):
        nc = self.nc
        other_es, other_ds = self.sets[(self.phase_idx + 1) % 2]
        final = [(s, c) for s, c in self.dma_cnt.items() if c > 0]
        with nc.Block() as block:
            decos = {"sp": block.sync, "pool": block.gpsimd, "act": block.scalar,
                     "dve": block.vector, "pe": block.tensor}
            for ename in ENGS:
                ops = self.ops[ename]

                def body(eng, ops=ops, ename=ename):
                    if ename == "pool":
                        for s in list(other_es.values()) + list(other_ds):
                            eng.sem_clear(s)
                    for waits, fn, sem, inc in ops:
                        for wsem, val in waits:
                            eng.wait_ge(wsem, val)
                        ins = fn(eng)
                        ins.then_inc(sem, inc)
                    if ename == "sp":
                        for s, c in final:
                            eng.wait_ge(s, c)
                decos[ename](body)
        self.phase_idx += 1
        self.in_phase = False

from contextlib import ExitStack

D = 1024
KD = 8
DFF = 2816
NFF = 22
EPS = 1e-6

PC_FFN1 = 0
PC_MIX = 8
PC_FFN2 = 16
PC_CONVW = 24
PC_CONVB = 56
PC_RGBA = 64
PC_RGBX = 72
PC_LAM = 80
PC_QN = 88
PC_KVN = 90
PC_FIN = 92
NPAR = 100


_UID = [0]


def mk_sb(nc, es):
    _UID[0] += 1
    u = _UID[0]
    return lambda name, shape, dt: es.enter_context(nc.sbuf_tensor(f"{name}_{u}", shape, dt))


def mk_pt(nc, es):
    _UID[0] += 1
    u = _UID[0]
    return lambda name, shape: es.enter_context(nc.psum_tensor(f"{name}_{u}", shape, F32))


def mk_stage(P, sb, n=3, size=2048):
    P.set_stage([sb(f"stg{i}", [128, size], F32) for i in range(n)], [P.slot() for _ in range(n)], size)


class Ctx:
    pass


def setup_consts(P, nc, es, par_d, L):
    C = Ctx()
    C.ones = es.enter_context(nc.sbuf_tensor("c_ones", [128, 128], BF16))
    C.par = es.enter_context(nc.sbuf_tensor("c_par", [128, L, NPAR], F32))
    C.eps = es.enter_context(nc.sbuf_tensor("c_eps", [128, 2], F32))
    C.s_ones = P.slot("ones")
    C.s_par = P.slot("par")
    C.s_eps = P.slot("eps")
    P.begin()
    P.op("pool", lambda e: e.memset(C.ones[:], 1.0), writes=[C.s_ones])

    def _e(e):
        e.memset(C.eps[:, 0:1], EPS)
        return e.memset(C.eps[:, 1:2], 1.0)
    P.op("pool", _e, writes=[C.s_eps])
    P.dma("sp", C.par[:], par_d.rearrange("l p n -> p l n"), writes=[C.s_par])
    P.end()
    return C


def rmsnorm_tile(P, C, x_ap3, s_x, h_ap3, s_h, gcol_ap, nk, sq, s_sq, ps, s_ps, lnv, s_ln, rstd, s_rstd, n, inv_dim):
    P.op("act", lambda e: e.activation(sq[:, :nk, :n], x_ap3, AF.Square), reads=[s_x], writes=[s_sq])

    def mm(e):
        ins = None
        for k in range(nk):
            ins = e.matmul(ps[:, :n], C.ones[:], sq[:, k, :n], start=(k == 0), stop=(k == nk - 1))
        return ins
    P.op("pe", mm, reads=[s_sq, C.s_ones], writes=[s_ps])
    P.op("act", lambda e: e.activation(lnv[:, :n], ps[:, :n], AF.Ln, bias=C.eps[:, 0:1], scale=inv_dim),
         reads=[s_ps, C.s_eps], writes=[s_ln])
    P.op("act", lambda e: e.activation(rstd[:, :n], lnv[:, :n], AF.Exp, scale=-0.5), reads=[s_ln], writes=[s_rstd])

    def nrm(e):
        ins = None
        for k in range(nk):
            ins = e.scalar_tensor_tensor(h_ap3[:, k, :], x_ap3[:, k, :], gcol_ap(k), rstd[:, :n], ALU.mult, ALU.mult)
        return ins
    P.op("dve", nrm, reads=[s_x, s_rstd, C.s_par], writes=[s_h])


def phase_ffn(P, nc, C, xT, w_gu, w_dn, l, gcol, T):
    TG = min(1024, T)
    NTI = TG // 512
    xT3 = xT.rearrange("(k p) t -> p k t", p=128)
    wgu3 = w_gu.rearrange("(k p) n -> p k n", p=128)
    wdn3 = w_dn.rearrange("(c p) n -> p c n", p=128)
    with ExitStack() as es:
        sb = mk_sb(nc, es)
        pt = mk_pt(nc, es)
        mk_stage(P, sb)
        xg = sb("f_xg", [128, KD, TG], F32)
        hT = sb("f_hT", [128, KD, TG], BF16)
        aT = sb("f_aT", [128, NFF, TG], BF16)
        sq = sb("f_sq", [128, KD, 512], BF16)
        lnv = sb("f_lnv", [128, 512], F32)
        rstd = sb("f_rstd", [128, 512], F32)
        sg = [sb(f"f_sg{i}", [128, 512], F32) for i in range(2)]
        NWB = 3
        wgu = [sb(f"f_wgu{i}", [128, KD, 2, 128], BF16) for i in range(NWB)]
        wd = [sb(f"f_wd{i}", [128, NFF, 128], BF16) for i in range(2)]
        pss = pt("f_pss", [128, 512])
        pg = [pt(f"f_pg{i}", [128, 512]) for i in range(2)]
        pu = [pt(f"f_pu{i}", [128, 512]) for i in range(2)]
        po = [pt(f"f_po{i}", [128, 512]) for i in range(2)]
        s_xg = [P.slot() for _ in range(NTI)]
        s_hT = [P.slot() for _ in range(NTI)]
        s_aT = [P.slot() for _ in range(NTI)]
        s_sq, s_ln, s_rstd, s_pss = P.slot(), P.slot(), P.slot(), P.slot()
        s_sg = [P.slot() for _ in range(2)]
        s_wgu = [P.slot() for _ in range(NWB)]
        s_wd = [P.slot() for _ in range(2)]
        s_pg = [P.slot() for _ in range(2)]
        s_pu = [P.slot() for _ in range(2)]
        s_po = [P.slot() for _ in range(2)]
        P.begin()
        it = 0
        wi = 0
        di = 0
        for g in range(T // TG):
            for ti in range(NTI):
                t0 = g * TG + ti * 512
                ts = slice(ti * 512, (ti + 1) * 512)
                P.dma("sp", xg[:, :, ts], xT3[:, :, t0:t0 + 512], writes=[s_xg[ti]])
                rmsnorm_tile(P, C, xg[:, :, ts], s_xg[ti], hT[:, :, ts], s_hT[ti],
                             lambda k: C.par[:, l, gcol + k:gcol + k + 1], KD,
                             sq, s_sq, pss, s_pss, lnv, s_ln, rstd, s_rstd, 512, 1.0 / D)
            for c in range(NFF):
                b = wi % NWB
                wi += 1
                P.dma("pool", wgu[b][:, :, 0, :], wgu3[:, :, c * 128:(c + 1) * 128], writes=[s_wgu[b]])
                P.dma("pool", wgu[b][:, :, 1, :], wgu3[:, :, DFF + c * 128:DFF + (c + 1) * 128],
                      writes=[s_wgu[b]], part=True)
                for ti in range(NTI):
                    ts = slice(ti * 512, (ti + 1) * 512)
                    j = it % 2
                    it += 1

                    def mm(e, b=b, ts=ts, j=j):
                        ins = None
                        for k in range(KD):
                            ins = e.matmul(pg[j][:], wgu[b][:, k, 0, :], hT[:, k, ts], start=(k == 0), stop=(k == KD - 1))
                        for k in range(KD):
                            ins = e.matmul(pu[j][:], wgu[b][:, k, 1, :], hT[:, k, ts], start=(k == 0), stop=(k == KD - 1))
                        return ins
                    P.op("pe", mm, reads=[s_wgu[b], s_hT[ti]], writes=[s_pg[j], s_pu[j]])
                    P.op("act", lambda e, j=j: e.activation(sg[j][:], pg[j][:], AF.Silu), reads=[s_pg[j]], writes=[s_sg[j]])
                    P.op("dve", lambda e, j=j, c=c, ts=ts: e.tensor_tensor(aT[:, c, ts], sg[j][:], pu[j][:], ALU.mult),
                         reads=[s_sg[j], s_pu[j]], writes=[s_aT[ti]])
            for oc in range(KD):
                b = di % 2
                P.dma("pool", wd[b][:], wdn3[:, :, oc * 128:(oc + 1) * 128], writes=[s_wd[b]])
                for ti in range(NTI):
                    ts = slice(ti * 512, (ti + 1) * 512)
                    j = di % 2
                    di2 = (di * NTI + ti) % 2

                    def mm2(e, b=b, ts=ts, j=di2):
                        ins = None
                        for c in range(NFF):
                            ins = e.matmul(po[j][:], wd[b][:, c, :], aT[:, c, ts], start=(c == 0), stop=(c == NFF - 1))
                        return ins
                    P.op("pe", mm2, reads=[s_wd[b], s_aT[ti]], writes=[s_po[di2]])
                    P.op("dve", lambda e, j=di2, oc=oc, ts=ts: e.scalar_tensor_tensor(
                        xg[:, oc, ts], po[j][:], 0.5, xg[:, oc, ts], ALU.mult, ALU.add),
                        reads=[s_po[di2], s_xg[ti]], writes=[s_xg[ti]])
                di += 1
            for ti in range(NTI):
                t0 = g * TG + ti * 512
                ts = slice(ti * 512, (ti + 1) * 512)
                P.dma("sp", xT3[:, :, t0:t0 + 512], xg[:, :, ts], reads=[s_xg[ti]])
        P.end()


SBW = 1024
OFF_RGX = 0
OFF_RGG = 1024
OFF_Q = 2048
OFF_K = 3072
OFF_V = 4096
OFF_CQ = 5120
OFF_CKV = 5376
OFF_KR = 5632
OFF_GATE = 5696
NIN = 8768


def phase_win(P, nc, C, xT, w_in, l, S, T, tok0=0):
    TG = min(2048, T)
    NTI = TG // 512
    xT3 = xT.rearrange("(k p) t -> p k t", p=128)
    w3 = w_in.rearrange("(k p) n -> p k n", p=128)
    chunks = []
    for c in range(8):
        chunks.append(([(OFF_RGX + c * 128, 128)], "copy", S["rgx"], c * 128, 128))
    for c in range(8):
        chunks.append(([(OFF_Q + c * 128, 128)], "qscale", S["qT"], c * 128, 128))
    for c in range(8):
        chunks.append(([(OFF_K + c * 128, 128)], "copy", S["kT"], c * 128, 128))
    for c in range(2):
        chunks.append(([(OFF_CQ + c * 128, 128)], "copy", S["cq"], c * 128, 128))
    for c in range(2):
        chunks.append(([(OFF_CKV + c * 128, 128)], "copy", S["ckv"], c * 128, 128))
    chunks.append(([(OFF_KR, 64)], "copy", S["kr"], 0, 64))
    chunks.append(([(OFF_KR + 32, 32), (OFF_KR, 32)], "copy", S["krsw"], 0, 64))
    for c in range(8):
        chunks.append(([(OFF_RGG + c * 128, 128)], "gelu", S["rgg"], c * 128, 128))
    for c in range(24):
        chunks.append(([(OFF_GATE + c * 128, 128)], "sigmoid", S["gate"], c * 128, 128))
    with ExitStack() as es:
        sb = mk_sb(nc, es)
        pt = mk_pt(nc, es)
        mk_stage(P, sb)
        xg = [sb(f"w_xg{i}", [128, KD, 512], F32) for i in range(2)]
        hT = sb("w_hT", [128, KD, TG], BF16)
        sq = sb("w_sq", [128, KD, 512], BF16)
        lnv = sb("w_lnv", [128, 512], F32)
        rstd = sb("w_rstd", [128, 512], F32)
        NWB = 3
        wt = [sb(f"w_wt{i}", [128, KD, 128], BF16) for i in range(NWB)]
        wv = [sb(f"w_wv{i}", [128, KD, 512], BF16) for i in range(2)]
        NST = 3
        stf = [sb(f"w_stf{i}", [128, TG], F32) for i in range(NST)]
        stb = [sb(f"w_stb{i}", [128, TG], BF16) for i in range(NST)]
        stv = [sb(f"w_stv{i}", [128, 1024], BF16) for i in range(NST)]
        s_stv = [P.slot() for _ in range(NST)]
        pss = pt("w_pss", [128, 512])
        NPW = 4
        pw = [pt(f"w_pw{i}", [128, 512]) for i in range(NPW)]
        s_xg = [P.slot() for _ in range(2)]
        s_hT = [P.slot() for _ in range(NTI)]
        s_sq, s_ln, s_rstd, s_pss = P.slot(), P.slot(), P.slot(), P.slot()
        s_wt = [P.slot() for _ in range(NWB)]
        s_wv = [P.slot() for _ in range(2)]
        s_stf = [P.slot() for _ in range(NST)]
        s_stb = [P.slot() for _ in range(NST)]
        s_pw = [P.slot() for _ in range(NPW)]
        P.begin()
        wi = 0
        pi = 0
        si = 0
        xi = 0
        for g in range(T // TG):
            for ti in range(NTI):
                t0 = g * TG + ti * 512
                ts = slice(ti * 512, (ti + 1) * 512)
                xb = xi % 2
                xi += 1
                P.dma("sp", xg[xb][:], xT3[:, :, t0:t0 + 512], writes=[s_xg[xb]])
                rmsnorm_tile(P, C, xg[xb][:], s_xg[xb], hT[:, :, ts], s_hT[ti],
                             lambda k: C.par[:, l, PC_MIX + k:PC_MIX + k + 1], KD,
                             sq, s_sq, pss, s_pss, lnv, s_ln, rstd, s_rstd, 512, 1.0 / D)
            def load_w(ci):
                b_ = ci % NWB
                o = 0
                for (c0, w) in chunks[ci][0]:
                    P.dma("pool", wt[b_][:, :, o:o + w], w3[:, :, c0:c0 + w], writes=[s_wt[b_]])
                    o += w
            PF = NWB - 1
            for ci in range(min(PF, len(chunks))):
                load_w(ci)
            for ci, (pieces, evac, dst, row0, M) in enumerate(chunks):
                b = ci % NWB
                if ci + PF < len(chunks):
                    load_w(ci + PF)
                if ci == len(chunks) - 4:
                    for ch in range(2):
                        P.dma("pool", wv[ch][:], w3[:, :, OFF_V + ch * 512:OFF_V + (ch + 1) * 512], writes=[s_wv[ch]])
                sj = si % NST
                si += 1
                tg0 = tok0 + g * TG
                isb = (evac in ("copy", "qscale")) and dst.dtype == BF16
                for ti in range(NTI):
                    ts = slice(ti * 512, (ti + 1) * 512)
                    j = pi % NPW
                    pi += 1

                    def mm(e, b=b, ts=ts, j=j, M=M):
                        ins = None
                        for k in range(KD):
                            ins = e.matmul(pw[j][:M, :], wt[b][:, k, :M], hT[:, k, ts], start=(k == 0), stop=(k == KD - 1))
                        return ins
                    P.op("pe", mm, reads=[s_wt[b], s_hT[ti]], writes=[s_pw[j]])
                    if evac == "copy":
                        if isb:
                            P.op("dve", lambda e, j=j, sj=sj, M=M, ts=ts: e.tensor_copy(stb[sj][:M, ts], pw[j][:M, :]),
                                 reads=[s_pw[j]], writes=[s_stb[sj]])
                        else:
                            P.op("dve", lambda e, j=j, sj=sj, M=M, ts=ts: e.tensor_copy(stf[sj][:M, ts], pw[j][:M, :]),
                                 reads=[s_pw[j]], writes=[s_stf[sj]])
                    elif evac == "qscale":
                        P.op("dve", lambda e, j=j, sj=sj, M=M, ts=ts: e.tensor_scalar(stb[sj][:M, ts], pw[j][:M, :], 128 ** -0.5, None, ALU.mult),
                             reads=[s_pw[j]], writes=[s_stb[sj]])
                    else:
                        fn = AF.Gelu if evac == "gelu" else AF.Sigmoid
                        P.op("act", lambda e, j=j, sj=sj, M=M, fn=fn, ts=ts: e.activation(stf[sj][:M, ts], pw[j][:M, :], fn),
                             reads=[s_pw[j]], writes=[s_stf[sj]])
                if isb:
                    P.dma("sp", dst[row0:row0 + M, tg0:tg0 + TG], stb[sj][:M, :], reads=[s_stb[sj]])
                else:
                    P.dma("sp", dst[row0:row0 + M, tg0:tg0 + TG], stf[sj][:M, :], reads=[s_stf[sj]])
            for tb in range(TG // 128):
                t0 = tok0 + g * TG + tb * 128
                ti = tb // 4
                sj = si % NST
                si += 1
                for ch in range(2):
                    j = pi % NPW
                    pi += 1

                    def mmv(e, b=ch, tb=tb, j=j):
                        ins = None
                        for k in range(KD):
                            ins = e.matmul(pw[j][:], hT[:, k, tb * 128:(tb + 1) * 128], wv[b][:, k, :], start=(k == 0), stop=(k == KD - 1))
                        return ins
                    P.op("pe", mmv, reads=[s_wv[ch], s_hT[ti]], writes=[s_pw[j]])
                    P.op("dve", lambda e, j=j, sj=sj, ch=ch: e.tensor_copy(stv[sj][:, ch * 512:(ch + 1) * 512], pw[j][:]),
                         reads=[s_pw[j]], writes=[s_stv[sj]])
                P.dma("sp", S["v"][t0:t0 + 128, :], stv[sj][:], reads=[s_stv[sj]])
        P.end()


def alloc_scratch(nc, TA, pfx=""):
    S = {}
    S["rgx"] = nc.dram_tensor(pfx + "s_rgx", [1024, TA], F32).ap()
    S["rgg"] = nc.dram_tensor(pfx + "s_rgg", [1024, TA], F32).ap()
    S["qT"] = nc.dram_tensor(pfx + "s_qT", [1024, TA], BF16).ap()
    S["kT"] = nc.dram_tensor(pfx + "s_kT", [1024, TA], BF16).ap()
    S["v"] = nc.dram_tensor(pfx + "s_v", [TA, 1024], BF16).ap()
    S["cq"] = nc.dram_tensor(pfx + "s_cq", [256, TA], F32).ap()
    S["ckv"] = nc.dram_tensor(pfx + "s_ckv", [256, TA], F32).ap()
    S["kr"] = nc.dram_tensor(pfx + "s_kr", [64, TA], F32).ap()
    S["krsw"] = nc.dram_tensor(pfx + "s_krsw", [64, TA], F32).ap()
    S["gate"] = nc.dram_tensor(pfx + "s_gate", [3072, TA], F32).ap()
    S["yaT"] = nc.dram_tensor(pfx + "s_yaT", [1024, TA], BF16).ap()
    S["ybT"] = nc.dram_tensor(pfx + "s_ybT", [1024, TA], BF16).ap()
    S["ycT"] = nc.dram_tensor(pfx + "s_ycT", [1024, TA], BF16).ap()
    S["mqn"] = nc.dram_tensor(pfx + "s_mqn", [1024, TA], BF16).ap()
    S["mqr"] = nc.dram_tensor(pfx + "s_mqr", [512, TA], BF16).ap()
    S["mkn"] = nc.dram_tensor(pfx + "s_mkn", [1024, TA], BF16).ap()
    S["mkr"] = nc.dram_tensor(pfx + "s_mkr", [64, TA], BF16).ap()
    S["mv"] = nc.dram_tensor(pfx + "s_mv", [TA, 1024], BF16).ap()
    return S


def phase_rg(P, nc, C, rg_wa, rg_wx, l, S, TA, chunks=range(8)):
    NS = min(2048, TA)
    with ExitStack() as es:
        sb = mk_sb(nc, es)
        pt = mk_pt(nc, es)
        mk_stage(P, sb)
        wab = sb("r_wab", [128, 8, 2, 128], BF16)
        cc_t = sb("r_c", [128, 8, 4], F32)
        xin = [sb(f"r_xin{i}", [128, NS + 3], F32) for i in range(2)]
        gg = [sb(f"r_gg{i}", [128, NS], F32) for i in range(2)]
        u = sb("r_u", [128, NS], F32)
        ubf = sb("r_ubf", [128, NS], BF16)
        r_t = sb("r_r", [128, NS], F32)
        i_t = sb("r_i", [128, NS], F32)
        a_t = sb("r_a", [128, NS], F32)
        m_t = sb("r_m", [128, NS], F32)
        h_t = sb("r_h", [128, NS], F32)
        y_t = [sb(f"r_y{i}", [128, NS], BF16) for i in range(2)]
        hl = sb("r_hl", [128, 2], F32)
        pa = [pt(f"r_pa{i}", [128, 512]) for i in range(2)]
        px = [pt(f"r_px{i}", [128, 512]) for i in range(2)]
        s_wab, s_c = P.slot(), P.slot()
        s_xin = [P.slot() for _ in range(2)]
        s_gg = [P.slot() for _ in range(2)]
        s_u, s_ubf, s_r, s_i, s_a, s_m, s_h, s_hl = [P.slot() for _ in range(8)]
        s_y = [P.slot() for _ in range(2)]
        s_pa = [P.slot() for _ in range(2)]
        s_px = [P.slot() for _ in range(2)]
        par = C.par
        P.begin()
        P.op("pool", lambda e: e.memset(wab[:], 0.0), writes=[s_wab])
        for gi, wsrc in enumerate((rg_wa, rg_wx)):
            wr = wsrc.rearrange("(c two) j k -> two j c k", two=2)
            for half in range(2):
                P.dma("pool", wab[half * 64:(half + 1) * 64, :, gi, half * 64:(half + 1) * 64], wr[half],
                      writes=[s_wab], pbase=half * 64)
        P.op("act", lambda e: e.activation(cc_t[:, :, 2], par[:, l, PC_LAM:PC_LAM + 8], AF.Exp, scale=-1.0),
             reads=[C.s_par], writes=[s_c])
        P.op("act", lambda e: e.activation(cc_t[:, :, 3], cc_t[:, :, 2], AF.Ln, bias=C.eps[:, 1:2]),
             reads=[s_c, C.s_eps], writes=[s_c])

        def cfin(e):
            e.tensor_scalar(cc_t[:, :, 0], cc_t[:, :, 3], -8.0, None, ALU.mult)
            return e.tensor_scalar(cc_t[:, :, 1], cc_t[:, :, 3], -16.0, None, ALU.mult)
        P.op("dve", cfin, reads=[s_c], writes=[s_c])
        pi = 0
        items = [(cc, sg_) for cc in chunks for sg_ in range(TA // NS)]

        def load_in(ii):
            cc_, sg2 = items[ii]
            rows_ = slice(cc_ * 128, (cc_ + 1) * 128)
            t0_ = sg2 * NS
            b_ = ii % 2
            if t0_ == 0:
                P.op("pool", lambda e, b_=b_: e.memset(xin[b_][:, 0:3], 0.0), writes=[s_xin[b_]])
                P.dma("sp", xin[b_][:, 3:], S["rgx"][rows_, 0:NS], writes=[s_xin[b_]])
            else:
                P.dma("sp", xin[b_][:], S["rgx"][rows_, t0_ - 3:t0_ + NS], writes=[s_xin[b_]])
            P.dma("sp", gg[b_][:], S["rgg"][rows_, t0_:t0_ + NS], writes=[s_gg[b_]])
        load_in(0)
        for ii, (cc, sg_) in enumerate(items):
            if True:
                rows = slice(cc * 128, (cc + 1) * 128)
                t0 = sg_ * NS
                b = ii % 2
                if ii + 1 < len(items):
                    load_in(ii + 1)
                cw = lambda tap, cc=cc: par[:, l, PC_CONVW + tap * 8 + cc:PC_CONVW + tap * 8 + cc + 1]
                P.op("dve", lambda e, b=b, cc=cc, cw=cw: e.tensor_scalar(
                    u[:], xin[b][:, 0:NS], cw(0), par[:, l, PC_CONVB + cc:PC_CONVB + cc + 1], ALU.mult, ALU.add),
                    reads=[s_xin[b], C.s_par], writes=[s_u])
                for tap in range(1, 4):
                    P.op("dve", lambda e, b=b, tap=tap, cw=cw: e.scalar_tensor_tensor(
                        u[:], xin[b][:, tap:tap + NS], cw(tap), u[:], ALU.mult, ALU.add),
                        reads=[s_xin[b], C.s_par, s_u], writes=[s_u])
                P.op("pool", lambda e: e.tensor_copy(ubf[:], u[:]), reads=[s_u], writes=[s_ubf])
                for blk in range(NS // 512):
                    j = pi % 2
                    pi += 1
                    bs = slice(blk * 512, (blk + 1) * 512)

                    def mm(e, j=j, bs=bs, cc=cc):
                        e.matmul(pa[j][:], wab[:, cc, 0, :], ubf[:, bs], start=True, stop=True)
                        return e.matmul(px[j][:], wab[:, cc, 1, :], ubf[:, bs], start=True, stop=True)
                    P.op("pe", mm, reads=[s_wab, s_ubf], writes=[s_pa[j], s_px[j]])

                    def sig(e, j=j, bs=bs, cc=cc):
                        e.activation(r_t[:, bs], pa[j][:], AF.Sigmoid, bias=par[:, l, PC_RGBA + cc:PC_RGBA + cc + 1])
                        return e.activation(i_t[:, bs], px[j][:], AF.Sigmoid, bias=par[:, l, PC_RGBX + cc:PC_RGBX + cc + 1])
                    P.op("act", sig, reads=[s_pa[j], s_px[j], C.s_par], writes=[s_r, s_i])

                def aexp(e, cc=cc):
                    e.activation(a_t[:], r_t[:], AF.Exp, scale=cc_t[:, cc, 0:1])
                    return e.activation(m_t[:], r_t[:], AF.Exp, scale=cc_t[:, cc, 1:2])
                P.op("act", aexp, reads=[s_r, s_c], writes=[s_a, s_m])
                P.op("act", lambda e: e.activation(m_t[:], m_t[:], AF.Sqrt, bias=C.eps[:, 1:2], scale=-1.0),
                     reads=[s_m, C.s_eps], writes=[s_m])
                P.op("pool", lambda e: e.tensor_tensor(i_t[:], i_t[:], m_t[:], ALU.mult), reads=[s_i, s_m], writes=[s_i])
                P.op("dve", lambda e: e.tensor_tensor(i_t[:], i_t[:], u[:], ALU.mult), reads=[s_i, s_u], writes=[s_i])
                if t0 == 0:
                    P.op("dve", lambda e: e.tensor_tensor_scan(h_t[:], a_t[:], i_t[:], 0.0, ALU.mult, ALU.add),
                         reads=[s_a, s_i], writes=[s_h])
                else:
                    P.op("dve", lambda e: e.tensor_tensor_scan(h_t[:], a_t[:], i_t[:], hl[:, 0:1], ALU.mult, ALU.add),
                         reads=[s_a, s_i, s_hl], writes=[s_h])
                P.op("dve", lambda e: e.tensor_copy(hl[:, 0:1], h_t[:, NS - 1:NS]), reads=[s_h], writes=[s_hl])
                P.op("pool", lambda e, b=b: e.tensor_tensor(y_t[b][:], h_t[:], gg[b][:], ALU.mult),
                     reads=[s_h, s_gg[b]], writes=[s_y[b]])
                P.dma("sp", S["yaT"][rows, t0:t0 + NS], y_t[b][:], reads=[s_y[b]])
        P.end()


import math
TWO_PI = 2.0 * math.pi
CW1 = 6.28125
CW2 = TWO_PI - CW1
MLA_SCALE = 192 ** -0.5


def phase_rope(P, nc, C, pos_d, ropec_d, R, TA):
    N = min(2048, TA)
    with ExitStack() as es:
        sb = mk_sb(nc, es)
        rc = sb("rp_c", [64, 2], F32)
        pi_ = sb("rp_pi", [64, N], I32)
        ang = sb("rp_ang", [64, N], F32)
        kf = sb("rp_kf", [64, N], F32)
        ki = sb("rp_ki", [64, N], I32)
        r = sb("rp_r", [64, N], F32)
        m = sb("rp_m", [64, N], F32)
        o = [sb(f"rp_o{i}", [64, N], F32) for i in range(4)]
        s_rc, s_pi, s_w = P.slot(), P.slot(), P.slot()
        s_o = [P.slot() for _ in range(4)]
        P.begin()
        P.dma("sp", rc[:], ropec_d, writes=[s_rc])
        for sg_ in range(TA // N):
            t0 = sg_ * N
            P.dma("sp", pi_[:], pos_d[0:1, t0:t0 + N].broadcast_to([64, N]), writes=[s_pi])

            def D1(fn, reads=(), extra_w=()):
                P.op("dve", fn, reads=[s_w] + list(reads), writes=[s_w] + list(extra_w))

            def wrap():
                D1(lambda e: e.tensor_scalar(m[:], r[:], math.pi, -TWO_PI, ALU.is_gt, ALU.mult))
                D1(lambda e: e.tensor_tensor(r[:], r[:], m[:], ALU.add))
                D1(lambda e: e.tensor_scalar(m[:], r[:], -math.pi, TWO_PI, ALU.is_lt, ALU.mult))
                D1(lambda e: e.tensor_tensor(r[:], r[:], m[:], ALU.add))

            D1(lambda e: e.tensor_copy(ang[:], pi_[:]), reads=[s_pi])
            D1(lambda e: e.tensor_scalar(ang[:], ang[:], rc[:, 0:1], None, ALU.mult), reads=[s_rc])
            D1(lambda e: e.tensor_scalar(ki[:], ang[:], 1.0 / TWO_PI, None, ALU.mult))
            D1(lambda e: e.tensor_copy(kf[:], ki[:]))
            D1(lambda e: e.scalar_tensor_tensor(r[:], kf[:], -CW1, ang[:], ALU.mult, ALU.add), reads=[s_o[0], s_o[1]])
            D1(lambda e: e.scalar_tensor_tensor(r[:], kf[:], -CW2, r[:], ALU.mult, ALU.add))
            wrap()
            P.op("act", lambda e: e.activation(o[1][:], r[:], AF.Sin), reads=[s_w], writes=[s_o[1]])
            D1(lambda e: e.tensor_scalar(r[:], r[:], math.pi / 2, None, ALU.add), reads=[s_o[1]])
            wrap()
            P.op("act", lambda e: e.activation(o[0][:], r[:], AF.Sin), reads=[s_w], writes=[s_o[0]])
            P.op("dve", lambda e: e.tensor_scalar(o[1][:], o[1][:], rc[:, 1:2], None, ALU.mult), reads=[s_o[1], s_rc], writes=[s_o[1]])
            P.op("dve", lambda e: e.tensor_scalar(o[2][:], o[0][:], MLA_SCALE, None, ALU.mult), reads=[s_o[0]], writes=[s_o[2]])
            P.op("dve", lambda e: e.tensor_scalar(o[3][:], o[1][:], MLA_SCALE, None, ALU.mult), reads=[s_o[1]], writes=[s_o[3]])
            for i, nm in enumerate(("cos", "sin", "cosq", "sinq")):
                P.dma("sp", R[nm][:, t0:t0 + N], o[i][:], reads=[s_o[i]])
        P.end()


def alloc_rope(nc, TA):
    return {nm: nc.dram_tensor("r_" + nm, [64, TA], F32).ap() for nm in ("cos", "sin", "cosq", "sinq")}


def phase_mlaprep(P, nc, C, w_uq, w_ukv, l, S, R, TA, heads=range(8)):
    heads = list(heads)
    NH = len(heads)
    wq3 = w_uq.rearrange("(k p) n -> p k n", p=128)
    wq4 = w_uq.rearrange("(k p) (h c) -> p k h c", p=128, c=192)
    wkv4 = w_ukv.rearrange("(k p) (h c) -> p k h c", p=128, c=256)
    with ExitStack() as es:
        sb = mk_sb(nc, es)
        pt = mk_pt(nc, es)
        mk_stage(P, sb)
        wq = sb("m_wq", [128, 2, 8, 192], BF16)
        wqs = sb("m_wqs", [128, 2, 8, 64], BF16)
        wkv = sb("m_wkv", [128, 2, 8, 256], BF16)
        wv = sb("m_wv", [128, 2, NH, 128], BF16)
        cin = [sb(f"m_cin{i}", [128, 2, 512], F32) for i in range(2)]
        cn = [sb(f"m_cn{i}", [128, 2, 512], BF16) for i in range(2)]
        sq = sb("m_sq", [128, 2, 512], BF16)
        lnv = sb("m_lnv", [128, 512], F32)
        rstd = sb("m_rstd", [128, 512], F32)
        tab = [sb(f"m_tab{i}", [64, 512], F32) for i in range(4)]
        krt = [sb(f"m_kr{i}", [64, 512], F32) for i in range(2)]
        t1 = sb("m_t1", [64, 512], F32)
        t2 = sb("m_t2", [64, 512], F32)
        NST = 4
        stb = [sb(f"m_stb{i}", [128, 512], BF16) for i in range(NST)]
        pss = pt("m_pss", [128, 512])
        NPW = 4
        pw = [pt(f"m_pw{i}", [128, 512]) for i in range(NPW)]
        pr = [pt(f"m_pr{i}", [64, 512]) for i in range(2)]
        s_w = P.slot()
        s_cin = [P.slot() for _ in range(2)]
        s_cn = [P.slot() for _ in range(2)]
        s_sq, s_ln, s_rstd, s_pss = P.slot(), P.slot(), P.slot(), P.slot()
        s_tab = [P.slot() for _ in range(4)]
        s_kr = [P.slot() for _ in range(2)]
        s_t1, s_t2 = P.slot(), P.slot()
        s_stb = [P.slot() for _ in range(NST)]
        s_pw = [P.slot() for _ in range(NPW)]
        s_pr = [P.slot() for _ in range(2)]
        P.begin()
        for k in range(2):
            P.dma("pool", wq[:, k], wq4[:, k], writes=[s_w], part=(k > 0))
        for k in range(2):
            P.dma("pool", wqs[:, k, :, 0:32], wq4[:, k, :, 160:192], writes=[s_w], part=True)
            P.dma("pool", wqs[:, k, :, 32:64], wq4[:, k, :, 128:160], writes=[s_w], part=True)
        for k in range(2):
            P.dma("pool", wkv[:, k], wkv4[:, k], writes=[s_w], part=True)
        for hi, h in enumerate(heads):
            P.dma("pool", wv[:, :, hi, :], wkv4[:, :, h, 128:256], writes=[s_w], part=True)
        pi = 0
        si = 0

        def evac_store(src_ap, s_src, M, dst_ap, scale=None):
            nonlocal si
            sj = si % NST
            si += 1
            if scale is None:
                P.op("dve", lambda e: e.tensor_copy(stb[sj][:M, :], src_ap), reads=[s_src], writes=[s_stb[sj]])
            else:
                P.op("dve", lambda e: e.tensor_scalar(stb[sj][:M, :], src_ap, scale, None, ALU.mult), reads=[s_src], writes=[s_stb[sj]])
            P.dma("sp", dst_ap, stb[sj][:M, :], reads=[s_stb[sj]])

        for tix in range(TA // 512):
            t0 = tix * 512
            tsl = slice(t0, t0 + 512)
            P.dma("sp", cin[0][:], S["cq"].rearrange("(k p) t -> p k t", p=128)[:, :, tsl], writes=[s_cin[0]])
            P.dma("sp", cin[1][:], S["ckv"].rearrange("(k p) t -> p k t", p=128)[:, :, tsl], writes=[s_cin[1]])
            for i, nm in enumerate(("cos", "sin", "cosq", "sinq")):
                P.dma("sp", tab[i][:], R[nm][:, tsl], writes=[s_tab[i]])
            P.dma("sp", krt[0][:], S["kr"][:, tsl], writes=[s_kr[0]])
            P.dma("sp", krt[1][:], S["krsw"][:, tsl], writes=[s_kr[1]])
            for i, pc in enumerate((PC_QN, PC_KVN)):
                rmsnorm_tile(P, C, cin[i][:], s_cin[i], cn[i][:], s_cn[i],
                             lambda k, pc=pc: C.par[:, l, pc + k:pc + k + 1], 2,
                             sq, s_sq, pss, s_pss, lnv, s_ln, rstd, s_rstd, 512, 1.0 / 256)
            P.op("dve", lambda e: e.tensor_tensor(t1[:], krt[0][:], tab[0][:], ALU.mult), reads=[s_kr[0], s_tab[0]], writes=[s_t1])
            P.op("dve", lambda e: e.tensor_tensor(t2[:], krt[1][:], tab[1][:], ALU.mult), reads=[s_kr[1], s_tab[1]], writes=[s_t2])
            sj = si % NST
            si += 1
            P.op("dve", lambda e, sj=sj: e.tensor_tensor(stb[sj][:64, :], t1[:], t2[:], ALU.add), reads=[s_t1, s_t2], writes=[s_stb[sj]])
            P.dma("sp", S["mkr"][:, tsl], stb[sj][:64, :], reads=[s_stb[sj]])
            for hi, h in enumerate(heads):
                j = pi % NPW
                pi += 1

                def mm(e, j=j, h=h):
                    e.matmul(pw[j][:], wq[:, 0, h, 0:128], cn[0][:, 0, :], start=True, stop=False)
                    return e.matmul(pw[j][:], wq[:, 1, h, 0:128], cn[0][:, 1, :], start=False, stop=True)
                P.op("pe", mm, reads=[s_w, s_cn[0]], writes=[s_pw[j]])
                evac_store(pw[j][:], s_pw[j], 128, S["mqn"][h * 128:(h + 1) * 128, tsl], scale=MLA_SCALE)
                def mmr(e, h=h):
                    e.matmul(pr[0][:], wq[:, 0, h, 128:192], cn[0][:, 0, :], start=True, stop=False)
                    e.matmul(pr[0][:], wq[:, 1, h, 128:192], cn[0][:, 1, :], start=False, stop=True)
                    e.matmul(pr[1][:], wqs[:, 0, h, :], cn[0][:, 0, :], start=True, stop=False)
                    return e.matmul(pr[1][:], wqs[:, 1, h, :], cn[0][:, 1, :], start=False, stop=True)
                P.op("pe", mmr, reads=[s_w, s_cn[0]], writes=[s_pr[0], s_pr[1]])
                P.op("dve", lambda e: e.tensor_tensor(t1[:], pr[0][:], tab[2][:], ALU.mult), reads=[s_pr[0], s_tab[2]], writes=[s_t1])
                P.op("dve", lambda e: e.tensor_tensor(t2[:], pr[1][:], tab[3][:], ALU.mult), reads=[s_pr[1], s_tab[3]], writes=[s_t2])
                sj = si % NST
                si += 1
                P.op("dve", lambda e, sj=sj: e.tensor_tensor(stb[sj][:64, :], t1[:], t2[:], ALU.add), reads=[s_t1, s_t2], writes=[s_stb[sj]])
                P.dma("sp", S["mqr"][h * 64:(h + 1) * 64, tsl], stb[sj][:64, :], reads=[s_stb[sj]])
                j = pi % NPW
                pi += 1

                def mmk(e, j=j, h=h):
                    e.matmul(pw[j][:], wkv[:, 0, h, 0:128], cn[1][:, 0, :], start=True, stop=False)
                    return e.matmul(pw[j][:], wkv[:, 1, h, 0:128], cn[1][:, 1, :], start=False, stop=True)
                P.op("pe", mmk, reads=[s_w, s_cn[1]], writes=[s_pw[j]])
                evac_store(pw[j][:], s_pw[j], 128, S["mkn"][h * 128:(h + 1) * 128, tsl])
            for tb in range(4):
                for c0 in range(0, NH * 128, 512):
                    n = min(512, NH * 128 - c0)
                    j = pi % NPW
                    pi += 1

                    def mmv(e, j=j, tb=tb, c0=c0, n=n):
                        wv2 = wv[:].rearrange("p k h c -> p k (h c)")
                        e.matmul(pw[j][:, :n], cn[1][:, 0, tb * 128:(tb + 1) * 128], wv2[:, 0, c0:c0 + n], start=True, stop=False)
                        return e.matmul(pw[j][:, :n], cn[1][:, 1, tb * 128:(tb + 1) * 128], wv2[:, 1, c0:c0 + n], start=False, stop=True)
                    P.op("pe", mmv, reads=[s_w, s_cn[1]], writes=[s_pw[j]])
                    sj = si % NST
                    si += 1
                    P.op("dve", lambda e, j=j, sj=sj, n=n: e.tensor_copy(stb[sj][:, :n], pw[j][:, :n]), reads=[s_pw[j]], writes=[s_stb[sj]])
                    col0 = heads[0] * 128 + c0
                    P.dma("sp", S["mv"][t0 + tb * 128:t0 + (tb + 1) * 128, col0:col0 + n], stb[sj][:, :n], reads=[s_stb[sj]])
        P.end()


def setup_attn_consts(P, nc, C, es):
    C.negtri = es.enter_context(nc.sbuf_tensor("c_negtri", [128, 128], BF16))
    C.s_negtri = P.slot("negtri")
    P.begin()
    P.op("pool", lambda e: e.memset(C.negtri[:], -1.0), writes=[C.s_negtri])
    P.op("pool", lambda e: e.affine_select(C.negtri[:], C.negtri[:], [[-1, 128]], ALU.is_ge, 0.0, base=0, channel_multiplier=1),
         reads=[C.s_negtri], writes=[C.s_negtri])
    P.end()


def phase_sb(P, nc, C, S, TA, heads=range(8)):
    heads = list(heads)
    NB = TA // 128
    NG = TA // 512
    v3 = S["v"].rearrange("(b p) c -> p b c", p=128)
    with ExitStack() as es:
        sb = mk_sb(nc, es)
        pt = mk_pt(nc, es)
        mk_stage(P, sb)
        kT = [sb(f"a_kT{i}", [128, TA], BF16) for i in range(2)]
        qT = [sb(f"a_qT{i}", [128, TA], BF16) for i in range(2)]
        vt = [sb(f"a_v{i}", [128, NB, 128], BF16) for i in range(2)]
        ef = [sb(f"a_ef{i}", [128, 512], F32) for i in range(2)]
        spb = [sb(f"a_spb{i}", [128, 512], BF16) for i in range(2)]
        lw = [sb(f"a_lw{i}", [128, 512], F32) for i in range(2)]
        wb = [sb(f"a_wb{i}", [128, 512], BF16) for i in range(2)]
        cB = [sb(f"a_cB{i}", [128, 512], F32) for i in range(2)]
        yst = [sb(f"a_yst{i}", [128, 512], BF16) for i in range(2)]
        pA = [pt(f"a_pA{i}", [128, 512]) for i in range(2)]
        pB = [pt(f"a_pB{i}", [128, 512]) for i in range(2)]
        pC = [pt(f"a_pC{i}", [128, 512]) for i in range(2)]
        pY = [pt(f"a_pY{i}", [128, 512]) for i in range(2)]
        mk = lambda n: [P.slot() for _ in range(n)]
        s_kT, s_qT, s_vt, s_ef, s_spb, s_lw, s_wb, s_cB, s_yst = [mk(2) for _ in range(9)]
        s_pA, s_pB, s_pC, s_pY = [mk(2) for _ in range(4)]
        units = []
        for hi, h in enumerate(heads):
            for g in range(NG):
                nkb = 4 * g + 4
                for kb in reversed(range(nkb)):
                    units.append((hi, h, g, kb, kb == nkb - 1, kb == 0))
        gidx = {}
        hstart = {}
        for n_, u in enumerate(units):
            key = (u[0], u[2])
            if key not in gidx:
                gidx[key] = len(gidx)
            if u[0] not in hstart:
                hstart[u[0]] = n_
        P.begin()

        def load_head(hi, h):
            hb = hi % 2
            rows = slice(h * 128, (h + 1) * 128)
            P.dma("sp", kT[hb][:], S["kT"][rows, 0:TA], writes=[s_kT[hb]])
            P.dma("sp", qT[hb][:], S["qT"][rows, 0:TA], writes=[s_qT[hb]])
            P.dma("sp", vt[hb][:], v3[:, :, rows], writes=[s_vt[hb]])

        def st1(n):
            hi, h, g, kb, first, last = units[n]
            hb, j = hi % 2, n % 2
            if n == 0:
                load_head(0, heads[0])
            if n == hstart[hi] + 3 and hi + 1 < len(heads):
                load_head(hi + 1, heads[hi + 1])
            ks = slice(kb * 128, (kb + 1) * 128)
            qs = slice(g * 512, (g + 1) * 512)
            P.op("pe", lambda e: e.matmul(pA[j][:], kT[hb][:, ks], qT[hb][:, qs], start=True, stop=True),
                 reads=[s_kT[hb], s_qT[hb]], writes=[s_pA[j]])
            P.op("act", lambda e: e.activation(ef[j][:], pA[j][:], AF.Exp), reads=[s_pA[j]], writes=[s_ef[j]])
            P.op("act", lambda e: e.activation(spb[j][:], ef[j][:], AF.Ln, bias=C.eps[:, 1:2]),
                 reads=[s_ef[j], C.s_eps], writes=[s_spb[j]])
            i = kb - 4 * g
            if i >= 0:
                P.op("pool", lambda e: e.affine_select(spb[j][:], spb[j][:], [[1, 512]], ALU.is_gt, 0.0,
                                                       base=-128 * i, channel_multiplier=-1),
                     reads=[s_spb[j]], writes=[s_spb[j]])

        def st2(n):
            hi, h, g, kb, first, last = units[n]
            hb, j = hi % 2, n % 2
            gb = gidx[(hi, g)] % 2
            ks = slice(kb * 128, (kb + 1) * 128)
            qs = slice(g * 512, (g + 1) * 512)
            if first:
                P.op("pool", lambda e: e.memset(cB[gb][:], 0.0), writes=[s_cB[gb]])

            def mm(e):
                e.matmul(pB[j][:], kT[hb][:, ks], qT[hb][:, qs], start=True, stop=False)
                e.matmul(pB[j][:], C.negtri[:], spb[j][:], start=False, stop=True)
                return e.matmul(pC[j][:], C.ones[:], spb[j][:], start=True, stop=True)
            P.op("pe", mm, reads=[s_kT[hb], s_qT[hb], s_spb[j], C.s_negtri, C.s_ones], writes=[s_pB[j], s_pC[j]])
            P.op("dve", lambda e: e.tensor_tensor(lw[j][:], pB[j][:], cB[gb][:], ALU.subtract),
                 reads=[s_pB[j], s_cB[gb]], writes=[s_lw[j]])
            if not last:
                P.op("dve", lambda e: e.tensor_tensor(cB[gb][:], pC[j][:], cB[gb][:], ALU.add),
                     reads=[s_pC[j], s_cB[gb]], writes=[s_cB[gb]])
            P.op("act", lambda e: e.activation(wb[j][:], lw[j][:], AF.Exp), reads=[s_lw[j]], writes=[s_wb[j]])
            i = kb - 4 * g
            if i >= 0:
                P.op("pool", lambda e: e.affine_select(wb[j][:], wb[j][:], [[1, 512]], ALU.is_gt, 0.0,
                                                       base=-128 * i, channel_multiplier=-1),
                     reads=[s_wb[j]], writes=[s_wb[j]])

        def st3(n):
            hi, h, g, kb, first, last = units[n]
            hb, j = hi % 2, n % 2
            gb = gidx[(hi, g)] % 2
            P.op("pe", lambda e: e.matmul(pY[gb][:], vt[hb][:, kb, :], wb[j][:], start=first, stop=last),
                 reads=[s_vt[hb], s_wb[j]], writes=[s_pY[gb]])
            if last:
                P.op("dve", lambda e: e.tensor_copy(yst[gb][:], pY[gb][:]), reads=[s_pY[gb]], writes=[s_yst[gb]])
                P.dma("sp", S["ybT"][h * 128:(h + 1) * 128, g * 512:(g + 1) * 512], yst[gb][:], reads=[s_yst[gb]])

        NU = len(units)
        for step in range(NU + 2):
            if step < NU:
                st1(step)
            if 0 <= step - 1 < NU:
                st2(step - 1)
            if 0 <= step - 2 < NU:
                st3(step - 2)
        P.end()


def phase_mla(P, nc, C, S, TA, heads=range(8)):
    heads = list(heads)
    NB = TA // 128
    NG = TA // 512
    v3 = S["mv"].rearrange("(b p) c -> p b c", p=128)
    with ExitStack() as es:
        sb = mk_sb(nc, es)
        pt = mk_pt(nc, es)
        mk_stage(P, sb)
        kr = sb("b_kr", [64, TA], BF16)
        kn = [sb(f"b_kn{i}", [128, TA], BF16) for i in range(2)]
        qn = [sb(f"b_qn{i}", [128, TA], BF16) for i in range(2)]
        qr = [sb(f"b_qr{i}", [64, TA], BF16) for i in range(2)]
        vt = [sb(f"b_v{i}", [128, NB, 128], BF16) for i in range(2)]
        pb = [sb(f"b_pb{i}", [128, 512], BF16) for i in range(3)]
        lnd = sb("b_lnd", [128, 512], F32)
        rd = sb("b_rd", [128, 512], F32)
        yst = [sb(f"b_yst{i}", [128, 512], BF16) for i in range(2)]
        pS = [pt(f"b_pS{i}", [128, 512]) for i in range(3)]
        pN = [pt(f"b_pN{i}", [128, 512]) for i in range(2)]
        pD = [pt(f"b_pD{i}", [128, 512]) for i in range(2)]
        mk = lambda n: [P.slot() for _ in range(n)]
        s_kn, s_qn, s_qr, s_vt, s_yst, s_pN, s_pD = [mk(2) for _ in range(7)]
        s_pb, s_pS = mk(3), mk(3)
        s_kr, s_lnd, s_rd = P.slot(), P.slot(), P.slot()
        units = []
        for hi, h in enumerate(heads):
            for g in range(NG):
                nkb = 4 * g + 4
                for kb in range(nkb):
                    units.append((hi, h, g, kb, kb == 0, kb == nkb - 1))
        gidx = {}
        hstart = {}
        for n_, u in enumerate(units):
            key = (u[0], u[2])
            if key not in gidx:
                gidx[key] = len(gidx)
            if u[0] not in hstart:
                hstart[u[0]] = n_
        P.begin()
        P.dma("sp", kr[:], S["mkr"][:, 0:TA], writes=[s_kr])

        def load_head(hi, h):
            hb = hi % 2
            rows = slice(h * 128, (h + 1) * 128)
            P.dma("sp", kn[hb][:], S["mkn"][rows, 0:TA], writes=[s_kn[hb]])
            P.dma("sp", qn[hb][:], S["mqn"][rows, 0:TA], writes=[s_qn[hb]])
            P.dma("sp", qr[hb][:], S["mqr"][h * 64:(h + 1) * 64, 0:TA], writes=[s_qr[hb]])
            P.dma("sp", vt[hb][:], v3[:, :, rows], writes=[s_vt[hb]])

        def st1(n):
            hi, h, g, kb, first, last = units[n]
            hb, j = hi % 2, n % 3
            if n == 0:
                load_head(0, heads[0])
            if n == hstart[hi] + 3 and hi + 1 < len(heads):
                load_head(hi + 1, heads[hi + 1])
            ks = slice(kb * 128, (kb + 1) * 128)
            qs = slice(g * 512, (g + 1) * 512)

            def mm(e):
                e.matmul(pS[j][:], kn[hb][:, ks], qn[hb][:, qs], start=True, stop=False)
                return e.matmul(pS[j][:], kr[:, ks], qr[hb][:, qs], start=False, stop=True)
            P.op("pe", mm, reads=[s_kn[hb], s_qn[hb], s_kr, s_qr[hb]], writes=[s_pS[j]])
            P.op("act", lambda e: e.activation(pb[j][:], pS[j][:], AF.Exp), reads=[s_pS[j]], writes=[s_pb[j]])
            i = kb - 4 * g
            if i >= 0:
                def msk(e):
                    ins = e.memset(pb[j][64:128, 128 * i:128 * i + 64], 0.0)
                    if i > 0:
                        ins = e.memset(pb[j][:, 0:128 * i], 0.0)
                    return ins
                P.op("pool", msk, reads=[s_pb[j]], writes=[s_pb[j]])

        def st2(n):
            hi, h, g, kb, first, last = units[n]
            hb, j = hi % 2, n % 3
            gb = gidx[(hi, g)] % 2

            def mm(e):
                e.matmul(pN[gb][:], vt[hb][:, kb, :], pb[j][:], start=first, stop=last)
                return e.matmul(pD[gb][:], C.ones[:], pb[j][:], start=first, stop=last)
            P.op("pe", mm, reads=[s_vt[hb], s_pb[j], C.s_ones], writes=[s_pN[gb], s_pD[gb]])
            if last:
                P.op("act", lambda e: e.activation(lnd[:], pD[gb][:], AF.Ln), reads=[s_pD[gb]], writes=[s_lnd])
                P.op("act", lambda e: e.activation(rd[:], lnd[:], AF.Exp, scale=-1.0), reads=[s_lnd], writes=[s_rd])
                P.op("dve", lambda e: e.tensor_tensor(yst[gb][:], pN[gb][:], rd[:], ALU.mult),
                     reads=[s_pN[gb], s_rd], writes=[s_yst[gb]])
                P.dma("sp", S["ycT"][h * 128:(h + 1) * 128, g * 512:(g + 1) * 512], yst[gb][:], reads=[s_yst[gb]])

        NU = len(units)
        for step in range(NU + 1):
            if step < NU:
                st1(step)
            if 0 <= step - 1 < NU:
                st2(step - 1)
        P.end()


def phase_merge(P, nc, C, xT, w_a, w_b, w_c, w_o, S, T, tok0=0):
    xT3 = xT.rearrange("(k p) t -> p k t", p=128)
    g4 = S["gate"].rearrange("(b k p) t -> p b k t", p=128, k=8)
    with ExitStack() as es:
        sb = mk_sb(nc, es)
        pt = mk_pt(nc, es)
        mk_stage(P, sb)
        W = [sb(f"g_w{i}", [128, KD, 1024], BF16) for i in range(4)]
        y = [sb(f"g_y{i}", [128, KD, 512], BF16) for i in range(3)]
        gt = [sb(f"g_gt{i}", [128, 3, 512], F32) for i in range(2)]
        m = [sb(f"g_m{i}", [128, 512], F32) for i in range(3)]
        mg = sb("g_mg", [128, KD, 512], BF16)
        xg = sb("g_xg", [128, KD, 512], F32)
        pP = [[pt(f"g_p{b}{i}", [128, 512]) for i in range(2)] for b in range(3)]
        pO = [pt(f"g_po{i}", [128, 512]) for i in range(2)]
        mk = lambda n: [P.slot() for _ in range(n)]
        s_W, s_y, s_gt, s_m, s_pO = mk(4), mk(3), mk(2), mk(3), mk(2)
        s_pP = [mk(2) for _ in range(3)]
        s_mg, s_xg = P.slot(), P.slot()
        P.begin()
        for i, w in enumerate((w_a, w_b, w_c, w_o)):
            w3 = w.rearrange("(k p) n -> p k n", p=128)
            for k in range(KD):
                P.dma("pool", W[i][:, k, :], w3[:, k, :], writes=[s_W[i]], part=(k > 0))
        it = 0
        for tix in range(T // 512):
            t0 = tix * 512
            ta = tok0 + t0
            for i, nm in enumerate(("yaT", "ybT", "ycT")):
                P.dma("sp", y[i][:], S[nm].rearrange("(k p) t -> p k t", p=128)[:, :, ta:ta + 512], writes=[s_y[i]])
            P.dma("sp", xg[:], xT3[:, :, t0:t0 + 512], writes=[s_xg])
            for oc in range(KD):
                j = it % 2
                it += 1
                P.dma("sp", gt[j][:], g4[:, :, oc, ta:ta + 512], writes=[s_gt[j]])
                cs = slice(oc * 128, (oc + 1) * 128)

                def mm(e, j=j, cs=cs):
                    ins = None
                    for b in range(3):
                        for k in range(KD):
                            ins = e.matmul(pP[b][j][:], W[b][:, k, cs], y[b][:, k, :], start=(k == 0), stop=(k == KD - 1))
                    return ins
                P.op("pe", mm, reads=s_W[:3] + s_y, writes=[s_pP[b][j] for b in range(3)])
                for b in range(3):
                    P.op("dve", lambda e, b=b, j=j: e.tensor_tensor(m[b][:], pP[b][j][:], gt[j][:, b, :], ALU.mult),
                         reads=[s_pP[b][j], s_gt[j]], writes=[s_m[b]])
                P.op("pool", lambda e: e.tensor_tensor(m[0][:], m[0][:], m[1][:], ALU.add), reads=[s_m[0], s_m[1]], writes=[s_m[0]])
                P.op("pool", lambda e, oc=oc: e.tensor_tensor(mg[:, oc, :], m[0][:], m[2][:], ALU.add),
                     reads=[s_m[0], s_m[2]], writes=[s_mg])
            for oc in range(KD):
                j = oc % 2
                cs = slice(oc * 128, (oc + 1) * 128)

                def mm2(e, j=j, cs=cs):
                    ins = None
                    for k in range(KD):
                        ins = e.matmul(pO[j][:], W[3][:, k, cs], mg[:, k, :], start=(k == 0), stop=(k == KD - 1))
                    return ins
                P.op("pe", mm2, reads=[s_W[3], s_mg], writes=[s_pO[j]])
                P.op("dve", lambda e, j=j, oc=oc: e.tensor_tensor(xg[:, oc, :], pO[j][:], xg[:, oc, :], ALU.add),
                     reads=[s_pO[j], s_xg], writes=[s_xg])
            P.dma("sp", xT3[:, :, t0:t0 + 512], xg[:], reads=[s_xg])
        P.end()


def phase_final(P, nc, C, xT, outT, T):
    xT3 = xT.rearrange("(k p) t -> p k t", p=128)
    oT3 = outT.rearrange("(k p) t -> p k t", p=128)
    with ExitStack() as es:
        sb = mk_sb(nc, es)
        pt = mk_pt(nc, es)
        mk_stage(P, sb)
        xg = [sb(f"n_xg{i}", [128, KD, 512], F32) for i in range(2)]
        og = [sb(f"n_og{i}", [128, KD, 512], F32) for i in range(2)]
        sq = sb("n_sq", [128, KD, 512], BF16)
        lnv = sb("n_lnv", [128, 512], F32)
        rstd = sb("n_rstd", [128, 512], F32)
        pss = pt("n_pss", [128, 512])
        s_xg = [P.slot() for _ in range(2)]
        s_og = [P.slot() for _ in range(2)]
        s_sq, s_ln, s_rstd, s_pss = P.slot(), P.slot(), P.slot(), P.slot()
        P.begin()
        for tix in range(T // 512):
            t0 = tix * 512
            b = tix % 2
            P.dma("sp", xg[b][:], xT3[:, :, t0:t0 + 512], writes=[s_xg[b]])
            rmsnorm_tile(P, C, xg[b][:], s_xg[b], og[b][:], s_og[b],
                         lambda k: C.par[:, 0, PC_FIN + k:PC_FIN + k + 1], KD,
                         sq, s_sq, pss, s_pss, lnv, s_ln, rstd, s_rstd, 512, 1.0 / D)
            P.dma("sp", oT3[:, :, t0:t0 + 512], og[b][:], reads=[s_og[b]])
        P.end()


from concourse.bass_utils import run_bass_kernel_spmd

DEPTH = 4
SEQ = 4096
BATCH = 4
WNAMES = ["ffn1_w_gate_up", "ffn1_w_down", "w_in", "rg_w_a", "rg_w_x", "mla_w_uq", "mla_w_ukv",
          "w_branch_a", "w_branch_b", "w_branch_c", "w_out", "ffn2_w_gate_up", "ffn2_w_down"]


ALLPH = ("ffn1", "win", "rg", "prep", "sb", "mla", "merge", "ffn2")


def build_program(wshapes, depth=DEPTH, seq=SEQ, sel=ALLPH):
    nc = bass.Bass("TRN2", target_bir_lowering=False)
    TA = seq
    xT_in = nc.dram_tensor("xT_in", [D, TA], F32, kind="ExternalInput").ap()
    pos_d = nc.dram_tensor("pos", [1, TA], I32, kind="ExternalInput").ap()
    par_d = nc.dram_tensor("par", [DEPTH, 128, NPAR], F32, kind="ExternalInput").ap()
    ropec_d = nc.dram_tensor("ropec", [64, 2], F32, kind="ExternalInput").ap()
    Wd = {n: nc.dram_tensor(n, list(wshapes[n]), F32, kind="ExternalInput").ap() for n in WNAMES}
    outT = nc.dram_tensor("outT", [D, TA], F32, kind="ExternalOutput").ap()
    xs = nc.dram_tensor("xs", [D, TA], F32).ap()
    S = alloc_scratch(nc, TA)
    R = alloc_rope(nc, TA)
    with ExitStack() as es:
        P = Prog(nc)
        C = setup_consts(P, nc, es, par_d, DEPTH)
        setup_attn_consts(P, nc, C, es)
        P.begin()
        P.dma("sp", xs, xT_in)
        P.end()
        phase_rope(P, nc, C, pos_d, ropec_d, R, TA)
        for l in range(depth):
            if "ffn1" in sel:
                phase_ffn(P, nc, C, xs, Wd["ffn1_w_gate_up"][l], Wd["ffn1_w_down"][l], l, PC_FFN1, TA)
            if "win" in sel:
                phase_win(P, nc, C, xs, Wd["w_in"][l], l, S, TA)
            if "rg" in sel:
                phase_rg(P, nc, C, Wd["rg_w_a"][l], Wd["rg_w_x"][l], l, S, TA)
            if "prep" in sel:
                phase_mlaprep(P, nc, C, Wd["mla_w_uq"][l], Wd["mla_w_ukv"][l], l, S, R, TA)
            if "sb" in sel:
                phase_sb(P, nc, C, S, TA)
            if "mla" in sel:
                phase_mla(P, nc, C, S, TA)
            if "merge" in sel:
                phase_merge(P, nc, C, xs, Wd["w_branch_a"][l], Wd["w_branch_b"][l], Wd["w_branch_c"][l], Wd["w_out"][l], S, TA)
            if "ffn2" in sel:
                phase_ffn(P, nc, C, xs, Wd["ffn2_w_gate_up"][l], Wd["ffn2_w_down"][l], l, PC_FFN2, TA)
        phase_final(P, nc, C, xs, outT, TA)
    return nc


def pack_params(inp):
    par = np.zeros((DEPTH, 128, NPAR), np.float32)
    pm = lambda v: np.asarray(v, np.float32).reshape(-1, 128).T
    for l in range(DEPTH):
        par[l, :, PC_FFN1:PC_FFN1 + 8] = pm(inp["ffn1_norm"][l])
        par[l, :, PC_MIX:PC_MIX + 8] = pm(inp["mix_norm"][l])
        par[l, :, PC_FFN2:PC_FFN2 + 8] = pm(inp["ffn2_norm"][l])
        for tap in range(4):
            par[l, :, PC_CONVW + tap * 8:PC_CONVW + tap * 8 + 8] = pm(inp["conv_w"][l][tap])
        par[l, :, PC_CONVB:PC_CONVB + 8] = pm(inp["conv_b"][l])
        par[l, :, PC_RGBA:PC_RGBA + 8] = pm(inp["rg_b_a"][l])
        par[l, :, PC_RGBX:PC_RGBX + 8] = pm(inp["rg_b_x"][l])
        par[l, :, PC_LAM:PC_LAM + 8] = pm(inp["rg_lambda"][l])
        par[l, :, PC_QN:PC_QN + 2] = pm(inp["mla_q_norm"][l])
        par[l, :, PC_KVN:PC_KVN + 2] = pm(inp["mla_kv_norm"][l])
        par[l, :, PC_FIN:PC_FIN + 8] = pm(inp["final_norm"])
    return par


def rope_consts():
    inv = (np.float32(10000.0) ** (-np.arange(0, 64, 2, dtype=np.float32) / np.float32(64))).astype(np.float32)
    ropec = np.zeros((64, 2), np.float32)
    ropec[:, 0] = np.concatenate([inv, inv])
    ropec[:, 1] = np.concatenate([-np.ones(32, np.float32), np.ones(32, np.float32)])
    return ropec


def kernel(**inputs):
    inp = {k: np.asarray(v) for k, v in inputs.items()}
    x = inp["x"].astype(np.float32, copy=False)
    pos = inp["positions"].astype(np.int32, copy=False)
    par = pack_params(inp)
    ropec = rope_consts()
    W = {n: np.ascontiguousarray(inp[n], dtype=np.float32) for n in WNAMES}
    nc = build_program({n: W[n].shape for n in WNAMES})
    in_maps = []
    for c in range(8):
        b = c % BATCH
        m = {"xT_in": np.ascontiguousarray(x[b].T), "pos": np.ascontiguousarray(pos[b][None, :]),
             "par": par, "ropec": ropec}
        m.update(W)
        in_maps.append(m)
    res = run_bass_kernel_spmd(nc, in_maps, core_ids=list(range(8)))
    out = np.empty((BATCH, SEQ, D), np.float32)
    for b in range(BATCH):
        out[b] = np.asarray(res.results[b]["outT"]).T
    return out
```

```python
import numpy as np
import concourse.bass as bass
import concourse.mybir as mybir

F32 = mybir.dt.float32
BF16 = mybir.dt.bfloat16
I32 = mybir.dt.int32
AF = mybir.ActivationFunctionType
ALU = mybir.AluOpType

ENGS = ("sp", "pool", "act", "dve", "pe")
NDMA = 40


class Slot:
    def __init__(self, P, name=""):
        self.name = name
        self.last_w = None
        self.readers = []
        self.ld_sem = None
        self.st_sem = None
        P.slots.append(self)

    def reset(self):
        self.last_w = None
        self.readers = []
        self.ld_sem = None
        self.st_sem = None


class Prog:
    def __init__(self, nc):
        self.nc = nc
        self.slots = []
        self.sets = []
        for i in range(2):
            es = {e: nc.alloc_semaphore(name=f"s{i}_{e}") for e in ENGS}
            ds = [nc.alloc_semaphore(name=f"s{i}_d{j}") for j in range(NDMA)]
            self.sets.append((es, ds))
        self.phase_idx = 0
        self.in_phase = False

    def slot(self, name=""):
        return Slot(self, name)

    def begin(self):
        assert not self.in_phase
        self.in_phase = True
        self.es, ds = self.sets[self.phase_idx % 2]
        self.dma_free = list(ds)
        self.dma_cnt = {}
        self.ops = {e: [] for e in ENGS}
        self.cnt = {e: 0 for e in ENGS}
        self.waited = {e: {} for e in ENGS}
        for s in self.slots:
            s.reset()
        self.misc_sem = self._new_dma_sem()

    def _new_dma_sem(self):
        assert self.dma_free, "out of DMA semaphores"
        s = self.dma_free.pop()
        self.dma_cnt[s] = 0
        return s

    def _deps(self, eng, reads, writes, skip_sem=None):
        toks = []
        for s in reads:
            if s.last_w is not None:
                toks.append(s.last_w)
        for s in writes:
            if s.last_w is not None:
                toks.append(s.last_w)
            toks.extend(s.readers)
        need = {}
        for (sem, val, src) in toks:
            if src == "pe" and eng == "pe":
                continue
            if skip_sem is not None and sem is skip_sem:
                continue
            k = id(sem)
            if self.waited[eng].get(k, 0) >= val:
                continue
            if k not in need or need[k][1] < val:
                need[k] = (sem, val)
        waits = []
        for k, (sem, val) in need.items():
            self.waited[eng][k] = val
            waits.append((sem, val))
        return waits

    def op(self, eng, fn, reads=(), writes=()):
        waits = self._deps(eng, reads, writes)
        self.cnt[eng] += 1
        sem = self.es[eng]
        tok = (sem, self.cnt[eng], eng)
        wset = set(id(s) for s in writes)
        for s in reads:
            if id(s) not in wset:
                s.readers.append(tok)
        for s in writes:
            s.last_w = tok
            s.readers = []
        self.ops[eng].append((waits, fn, sem, 1))

    def set_stage(self, tiles, slots, size):
        self.stg = (tiles, slots, size)
        self.stg_i = 0

    def cast_load(self, out, in_, writes, pbase=0):
        shape = tuple(int(x) for x in out.shape)
        p, free = shape[0], shape[1:]
        n = 1
        for x in free:
            n *= x
        tiles, slots, size = self.stg
        if n > size:
            a = free[0]
            per = n // a
            step = max(1, size // per)
            for a0 in range(0, a, step):
                a1 = min(a, a0 + step)
                self.cast_load(out[:, a0:a1], in_[:, a0:a1], writes, pbase)
            return
        i = self.stg_i % len(tiles)
        self.stg_i += 1
        st = tiles[i][pbase:pbase + p, :n]
        if len(free) == 2:
            st = st.rearrange("p (a b) -> p a b", a=free[0])
        elif len(free) == 3:
            st = st.rearrange("p (a b c) -> p a b c", a=free[0], b=free[1])
        self.dma("sp", st, in_, writes=[slots[i]])
        self.op("pool", lambda e, out=out, st=st: e.tensor_copy(out, st), reads=[slots[i]], writes=list(writes))

    def dma(self, q, out, in_, reads=(), writes=(), part=False, **kw):
        if q == "pool":
            return self.cast_load(out, in_, writes, kw.get("pbase", 0))
        if writes:
            sl = writes[0]
            if sl.ld_sem is None:
                sl.ld_sem = self._new_dma_sem()
            sem = sl.ld_sem
        elif reads:
            sl = reads[0]
            if sl.st_sem is None:
                sl.st_sem = self._new_dma_sem()
            sem = sl.st_sem
        else:
            sem = self.misc_sem
        waits = self._deps(q, reads, writes, skip_sem=(sem if part else None))
        self.dma_cnt[sem] += 16
        tok = (sem, self.dma_cnt[sem], "dma")
        for s in reads:
            s.readers.append(tok)
        for s in writes:
            s.last_w = tok
            s.readers = []

        def fn(e, out=out, in_=in_, kw=kw):
            return e.dma_start(out=out, in_=in_, **kw)
        self.ops[q].append((waits, fn, sem, 16))

    def end(self):
        nc = self.nc
        other_es, other_ds = self.sets[(self.phase_idx + 1) % 2]
        final = [(s, c) for s, c in self.dma_cnt.items() if c > 0]
        with nc.Block() as block:
            decos = {"sp": block.sync, "pool": block.gpsimd, "act": block.scalar,
                     "dve": block.vector, "pe": block.tensor}
            for ename in ENGS:
                ops = self.ops[ename]

                def body(eng, ops=ops, ename=ename):
                    if ename == "pool":
                        for s in list(other_es.values()) + list(other_ds):
                            eng.sem_clear(s)
                    for waits, fn, sem, inc in ops:
                        for wsem, val in waits:
                            eng.wait_ge(wsem, val)
                        ins = fn(eng)
                        ins.then_inc(sem, inc)
                    if ename == "sp":
                        for s, c in final:
                            eng.wait_ge(s, c)
                decos[ename](body)
        self.phase_idx += 1
        self.in_phase = False

from contextlib import ExitStack

D = 1024
KD = 8
DFF = 2816
NFF = 22
EPS = 1e-6

PC_FFN1 = 0
PC_MIX = 8
PC_FFN2 = 16
PC_CONVW = 24
PC_CONVB = 56
PC_RGBA = 64
PC_RGBX = 72
PC_LAM = 80
PC_QN = 88
PC_KVN = 90
PC_FIN = 92
NPAR = 100


_UID = [0]


def mk_sb(nc, es):
    _UID[0] += 1
    u = _UID[0]
    return lambda name, shape, dt: es.enter_context(nc.sbuf_tensor(f"{name}_{u}", shape, dt))


def mk_pt(nc, es):
    _UID[0] += 1
    u = _UID[0]
    return lambda name, shape: es.enter_context(nc.psum_tensor(f"{name}_{u}", shape, F32))


def mk_stage(P, sb, n=3, size=2048):
    P.set_stage([sb(f"stg{i}", [128, size], F32) for i in range(n)], [P.slot() for _ in range(n)], size)


class Ctx:
    pass


def setup_consts(P, nc, es, par_d, L):
    C = Ctx()
    C.ones = es.enter_context(nc.sbuf_tensor("c_ones", [128, 128], BF16))
    C.par = es.enter_context(nc.sbuf_tensor("c_par", [128, L, NPAR], F32))
    C.eps = es.enter_context(nc.sbuf_tensor("c_eps", [128, 2], F32))
    C.s_ones = P.slot("ones")
    C.s_par = P.slot("par")
    C.s_eps = P.slot("eps")
    P.begin()
    P.op("pool", lambda e: e.memset(C.ones[:], 1.0), writes=[C.s_ones])

    def _e(e):
        e.memset(C.eps[:, 0:1], EPS)
        return e.memset(C.eps[:, 1:2], 1.0)
    P.op("pool", _e, writes=[C.s_eps])
    P.dma("sp", C.par[:], par_d.rearrange("l p n -> p l n"), writes=[C.s_par])
    P.end()
    return C


def rmsnorm_tile(P, C, x_ap3, s_x, h_ap3, s_h, gcol_ap, nk, sq, s_sq, ps, s_ps, lnv, s_ln, rstd, s_rstd, n, inv_dim):
    P.op("act", lambda e: e.activation(sq[:, :nk, :n], x_ap3, AF.Square), reads=[s_x], writes=[s_sq])

    def mm(e):
        ins = None
        for k in range(nk):
            ins = e.matmul(ps[:, :n], C.ones[:], sq[:, k, :n], start=(k == 0), stop=(k == nk - 1))
        return ins
    P.op("pe", mm, reads=[s_sq, C.s_ones], writes=[s_ps])
    P.op("act", lambda e: e.activation(lnv[:, :n], ps[:, :n], AF.Ln, bias=C.eps[:, 0:1], scale=inv_dim),
         reads=[s_ps, C.s_eps], writes=[s_ln])
    P.op("act", lambda e: e.activation(rstd[:, :n], lnv[:, :n], AF.Exp, scale=-0.5), reads=[s_ln], writes=[s_rstd])

    def nrm(e):
        ins = None
        for k in range(nk):
            ins = e.scalar_tensor_tensor(h_ap3[:, k, :], x_ap3[:, k, :], gcol_ap(k), rstd[:, :n], ALU.mult, ALU.mult)
        return ins
    P.op("dve", nrm, reads=[s_x, s_rstd, C.s_par], writes=[s_h])


def phase_ffn(P, nc, C, xT, w_gu, w_dn, l, gcol, T):
    TG = min(1024, T)
    NTI = TG // 512
    xT3 = xT.rearrange("(k p) t -> p k t", p=128)
    wgu3 = w_gu.rearrange("(k p) n -> p k n", p=128)
    wdn3 = w_dn.rearrange("(c p) n -> p c n", p=128)
    with ExitStack() as es:
        sb = mk_sb(nc, es)
        pt = mk_pt(nc, es)
        mk_stage(P, sb)
        xg = sb("f_xg", [128, KD, TG], F32)
        hT = sb("f_hT", [128, KD, TG], BF16)
        aT = sb("f_aT", [128, NFF, TG], BF16)
        sq = sb("f_sq", [128, KD, 512], BF16)
        lnv = sb("f_lnv", [128, 512], F32)
        rstd = sb("f_rstd", [128, 512], F32)
        sg = [sb(f"f_sg{i}", [128, 512], F32) for i in range(2)]
        NWB = 3
        wgu = [sb(f"f_wgu{i}", [128, KD, 2, 128], BF16) for i in range(NWB)]
        wd = [sb(f"f_wd{i}", [128, NFF, 128], BF16) for i in range(2)]
        pss = pt("f_pss", [128, 512])
        pg = [pt(f"f_pg{i}", [128, 512]) for i in range(2)]
        pu = [pt(f"f_pu{i}", [128, 512]) for i in range(2)]
        po = [pt(f"f_po{i}", [128, 512]) for i in range(2)]
        s_xg = [P.slot() for _ in range(NTI)]
        s_hT = [P.slot() for _ in range(NTI)]
        s_aT = [P.slot() for _ in range(NTI)]
        s_sq, s_ln, s_rstd, s_pss = P.slot(), P.slot(), P.slot(), P.slot()
        s_sg = [P.slot() for _ in range(2)]
        s_wgu = [P.slot() for _ in range(NWB)]
        s_wd = [P.slot() for _ in range(2)]
        s_pg = [P.slot() for _ in range(2)]
        s_pu = [P.slot() for _ in range(2)]
        s_po = [P.slot() for _ in range(2)]
        P.begin()
        it = 0
        wi = 0
        di = 0
        for g in range(T // TG):
            for ti in range(NTI):
                t0 = g * TG + ti * 512
                ts = slice(ti * 512, (ti + 1) * 512)
                P.dma("sp", xg[:, :, ts], xT3[:, :, t0:t0 + 512], writes=[s_xg[ti]])
                rmsnorm_tile(P, C, xg[:, :, ts], s_xg[ti], hT[:, :, ts], s_hT[ti],
                             lambda k: C.par[:, l, gcol + k:gcol + k + 1], KD,
                             sq, s_sq, pss, s_pss, lnv, s_ln, rstd, s_rstd, 512, 1.0 / D)
            for c in range(NFF):
                b = wi % NWB
                wi += 1
                P.dma("pool", wgu[b][:, :, 0, :], wgu3[:, :, c * 128:(c + 1) * 128], writes=[s_wgu[b]])
                P.dma("pool", wgu[b][:, :, 1, :], wgu3[:, :, DFF + c * 128:DFF + (c + 1) * 128],
                      writes=[s_wgu[b]], part=True)
                for ti in range(NTI):
                    ts = slice(ti * 512, (ti + 1) * 512)
                    j = it % 2
                    it += 1

                    def mm(e, b=b, ts=ts, j=j):
                        ins = None
                        for k in range(KD):
                            ins = e.matmul(pg[j][:], wgu[b][:, k, 0, :], hT[:, k, ts], start=(k == 0), stop=(k == KD - 1))
                        for k in range(KD):
                            ins = e.matmul(pu[j][:], wgu[b][:, k, 1, :], hT[:, k, ts], start=(k == 0), stop=(k == KD - 1))
                        return ins
                    P.op("pe", mm, reads=[s_wgu[b], s_hT[ti]], writes=[s_pg[j], s_pu[j]])
                    P.op("act", lambda e, j=j: e.activation(sg[j][:], pg[j][:], AF.Silu), reads=[s_pg[j]], writes=[s_sg[j]])
                    P.op("dve", lambda e, j=j, c=c, ts=ts: e.tensor_tensor(aT[:, c, ts], sg[j][:], pu[j][:], ALU.mult),
                         reads=[s_sg[j], s_pu[j]], writes=[s_aT[ti]])
            for oc in range(KD):
                b = di % 2
                P.dma("pool", wd[b][:], wdn3[:, :, oc * 128:(oc + 1) * 128], writes=[s_wd[b]])
                for ti in range(NTI):
                    ts = slice(ti * 512, (ti + 1) * 512)
                    j = di % 2
                    di2 = (di * NTI + ti) % 2

                    def mm2(e, b=b, ts=ts, j=di2):
                        ins = None
                        for c in range(NFF):
                            ins = e.matmul(po[j][:], wd[b][:, c, :], aT[:, c, ts], start=(c == 0), stop=(c == NFF - 1))
                        return ins
                    P.op("pe", mm2, reads=[s_wd[b], s_aT[ti]], writes=[s_po[di2]])
                    P.op("dve", lambda e, j=di2, oc=oc, ts=ts: e.scalar_tensor_tensor(
                        xg[:, oc, ts], po[j][:], 0.5, xg[:, oc, ts], ALU.mult, ALU.add),
                        reads=[s_po[di2], s_xg[ti]], writes=[s_xg[ti]])
                di += 1
            for ti in range(NTI):
                t0 = g * TG + ti * 512
                ts = slice(ti * 512, (ti + 1) * 512)
                P.dma("sp", xT3[:, :, t0:t0 + 512], xg[:, :, ts], reads=[s_xg[ti]])
        P.end()


SBW = 1024
OFF_RGX = 0
OFF_RGG = 1024
OFF_Q = 2048
OFF_K = 3072
OFF_V = 4096
OFF_CQ = 5120
OFF_CKV = 5376
OFF_KR = 5632
OFF_GATE = 5696
NIN = 8768


def phase_win(P, nc, C, xT, w_in, l, S, T, tok0=0):
    TG = min(2048, T)
    NTI = TG // 512
    xT3 = xT.rearrange("(k p) t -> p k t", p=128)
    w3 = w_in.rearrange("(k p) n -> p k n", p=128)
    chunks = []
    for c in range(8):
        chunks.append(([(OFF_RGX + c * 128, 128)], "copy", S["rgx"], c * 128, 128))
    for c in range(8):
        chunks.append(([(OFF_Q + c * 128, 128)], "qscale", S["qT"], c * 128, 128))
    for c in range(8):
        chunks.append(([(OFF_K + c * 128, 128)], "copy", S["kT"], c * 128, 128))
    for c in range(2):
        chunks.append(([(OFF_CQ + c * 128, 128)], "copy", S["cq"], c * 128, 128))
    for c in range(2):
        chunks.append(([(OFF_CKV + c * 128, 128)], "copy", S["ckv"], c * 128, 128))
    chunks.append(([(OFF_KR, 64)], "copy", S["kr"], 0, 64))
    chunks.append(([(OFF_KR + 32, 32), (OFF_KR, 32)], "copy", S["krsw"], 0, 64))
    for c in range(8):
        chunks.append(([(OFF_RGG + c * 128, 128)], "gelu", S["rgg"], c * 128, 128))
    for c in range(24):
        chunks.append(([(OFF_GATE + c * 128, 128)], "sigmoid", S["gate"], c * 128, 128))
    with ExitStack() as es:
        sb = mk_sb(nc, es)
        pt = mk_pt(nc, es)
        mk_stage(P, sb)
        xg = [sb(f"w_xg{i}", [128, KD, 512], F32) for i in range(2)]
        hT = sb("w_hT", [128, KD, TG], BF16)
        sq = sb("w_sq", [128, KD, 512], BF16)
        lnv = sb("w_lnv", [128, 512], F32)
        rstd = sb("w_rstd", [128, 512], F32)
        NWB = 3
        wt = [sb(f"w_wt{i}", [128, KD, 128], BF16) for i in range(NWB)]
        wv = [sb(f"w_wv{i}", [128, KD, 512], BF16) for i in range(2)]
        NST = 3
        stf = [sb(f"w_stf{i}", [128, TG], F32) for i in range(NST)]
        stb = [sb(f"w_stb{i}", [128, TG], BF16) for i in range(NST)]
        stv = [sb(f"w_stv{i}", [128, 1024], BF16) for i in range(NST)]
        s_stv = [P.slot() for _ in range(NST)]
        pss = pt("w_pss", [128, 512])
        NPW = 4
        pw = [pt(f"w_pw{i}", [128, 512]) for i in range(NPW)]
        s_xg = [P.slot() for _ in range(2)]
        s_hT = [P.slot() for _ in range(NTI)]
        s_sq, s_ln, s_rstd, s_pss = P.slot(), P.slot(), P.slot(), P.slot()
        s_wt = [P.slot() for _ in range(NWB)]
        s_wv = [P.slot() for _ in range(2)]
        s_stf = [P.slot() for _ in range(NST)]
        s_stb = [P.slot() for _ in range(NST)]
        s_pw = [P.slot() for _ in range(NPW)]
        P.begin()
        wi = 0
        pi = 0
        si = 0
        xi = 0
        for g in range(T // TG):
            for ti in range(NTI):
                t0 = g * TG + ti * 512
                ts = slice(ti * 512, (ti + 1) * 512)
                xb = xi % 2
                xi += 1
                P.dma("sp", xg[xb][:], xT3[:, :, t0:t0 + 512], writes=[s_xg[xb]])
                rmsnorm_tile(P, C, xg[xb][:], s_xg[xb], hT[:, :, ts], s_hT[ti],
                             lambda k: C.par[:, l, PC_MIX + k:PC_MIX + k + 1], KD,
                             sq, s_sq, pss, s_pss, lnv, s_ln, rstd, s_rstd, 512, 1.0 / D)
            def load_w(ci):
                b_ = ci % NWB
                o = 0
                for (c0, w) in chunks[ci][0]:
                    P.dma("pool", wt[b_][:, :, o:o + w], w3[:, :, c0:c0 + w], writes=[s_wt[b_]])
                    o += w
            PF = NWB - 1
            for ci in range(min(PF, len(chunks))):
                load_w(ci)
            for ci, (pieces, evac, dst, row0, M) in enumerate(chunks):
                b = ci % NWB
                if ci + PF < len(chunks):
                    load_w(ci + PF)
                if ci == len(chunks) - 4:
                    for ch in range(2):
                        P.dma("pool", wv[ch][:], w3[:, :, OFF_V + ch * 512:OFF_V + (ch + 1) * 512], writes=[s_wv[ch]])
                sj = si % NST
                si += 1
                tg0 = tok0 + g * TG
                isb = (evac in ("copy", "qscale")) and dst.dtype == BF16
                for ti in range(NTI):
                    ts = slice(ti * 512, (ti + 1) * 512)
                    j = pi % NPW
                    pi += 1

                    def mm(e, b=b, ts=ts, j=j, M=M):
                        ins = None
                        for k in range(KD):
                            ins = e.matmul(pw[j][:M, :], wt[b][:, k, :M], hT[:, k, ts], start=(k == 0), stop=(k == KD - 1))
                        return ins
                    P.op("pe", mm, reads=[s_wt[b], s_hT[ti]], writes=[s_pw[j]])
                    if evac == "copy":
                        if isb:
                            P.op("dve", lambda e, j=j, sj=sj, M=M, ts=ts: e.tensor_copy(stb[sj][:M, ts], pw[j][:M, :]),
                                 reads=[s_pw[j]], writes=[s_stb[sj]])
                        else:
                            P.op("dve", lambda e, j=j, sj=sj, M=M, ts=ts: e.tensor_copy(stf[sj][:M, ts], pw[j][:M, :]),
                                 reads=[s_pw[j]], writes=[s_stf[sj]])
                    elif evac == "qscale":
                        P.op("dve", lambda e, j=j, sj=sj, M=M, ts=ts: e.tensor_scalar(stb[sj][:M, ts], pw[j][:M, :], 128 ** -0.5, None, ALU.mult),
                             reads=[s_pw[j]], writes=[s_stb[sj]])
                    else:
                        fn = AF.Gelu if evac == "gelu" else AF.Sigmoid
                        P.op("act", lambda e, j=j, sj=sj, M=M, fn=fn, ts=ts: e.activation(stf[sj][:M, ts], pw[j][:M, :], fn),
                             reads=[s_pw[j]], writes=[s_stf[sj]])
                if isb:
                    P.dma("sp", dst[row0:row0 + M, tg0:tg0 + TG], stb[sj][:M, :], reads=[s_stb[sj]])
                else:
                    P.dma("sp", dst[row0:row0 + M, tg0:tg0 + TG], stf[sj][:M, :], reads=[s_stf[sj]])
            for tb in range(TG // 128):
                t0 = tok0 + g * TG + tb * 128
                ti = tb // 4
                sj = si % NST
                si += 1
                for ch in range(2):
                    j = pi % NPW
                    pi += 1

                    def mmv(e, b=ch, tb=tb, j=j):
                        ins = None
                        for k in range(KD):
                            ins = e.matmul(pw[j][:], hT[:, k, tb * 128:(tb + 1) * 128], wv[b][:, k, :], start=(k == 0), stop=(k == KD - 1))
                        return ins
                    P.op("pe", mmv, reads=[s_wv[ch], s_hT[ti]], writes=[s_pw[j]])
                    P.op("dve", lambda e, j=j, sj=sj, ch=ch: e.tensor_copy(stv[sj][:, ch * 512:(ch + 1) * 512], pw[j][:]),
                         reads=[s_pw[j]], writes=[s_stv[sj]])
                P.dma("sp", S["v"][t0:t0 + 128, :], stv[sj][:], reads=[s_stv[sj]])
        P.end()


def alloc_scratch(nc, TA, pfx=""):
    S = {}
    S["rgx"] = nc.dram_tensor(pfx + "s_rgx", [1024, TA], F32).ap()
    S["rgg"] = nc.dram_tensor(pfx + "s_rgg", [1024, TA], F32).ap()
    S["qT"] = nc.dram_tensor(pfx + "s_qT", [1024, TA], BF16).ap()
    S["kT"] = nc.dram_tensor(pfx + "s_kT", [1024, TA], BF16).ap()
    S["v"] = nc.dram_tensor(pfx + "s_v", [TA, 1024], BF16).ap()
    S["cq"] = nc.dram_tensor(pfx + "s_cq", [256, TA], F32).ap()
    S["ckv"] = nc.dram_tensor(pfx + "s_ckv", [256, TA], F32).ap()
    S["kr"] = nc.dram_tensor(pfx + "s_kr", [64, TA], F32).ap()
    S["krsw"] = nc.dram_tensor(pfx + "s_krsw", [64, TA], F32).ap()
    S["gate"] = nc.dram_tensor(pfx + "s_gate", [3072, TA], F32).ap()
    S["yaT"] = nc.dram_tensor(pfx + "s_yaT", [1024, TA], BF16).ap()
    S["ybT"] = nc.dram_tensor(pfx + "s_ybT", [1024, TA], BF16).ap()
    S["ycT"] = nc.dram_tensor(pfx + "s_ycT", [1024, TA], BF16).ap()
    S["mqn"] = nc.dram_tensor(pfx + "s_mqn", [1024, TA], BF16).ap()
    S["mqr"] = nc.dram_tensor(pfx + "s_mqr", [512, TA], BF16).ap()
    S["mkn"] = nc.dram_tensor(pfx + "s_mkn", [1024, TA], BF16).ap()
    S["mkr"] = nc.dram_tensor(pfx + "s_mkr", [64, TA], BF16).ap()
    S["mv"] = nc.dram_tensor(pfx + "s_mv", [TA, 1024], BF16).ap()
    return S


def phase_rg(P, nc, C, rg_wa, rg_wx, l, S, TA, chunks=range(8)):
    NS = min(2048, TA)
    with ExitStack() as es:
        sb = mk_sb(nc, es)
        pt = mk_pt(nc, es)
        mk_stage(P, sb)
        wab = sb("r_wab", [128, 8, 2, 128], BF16)
        cc_t = sb("r_c", [128, 8, 4], F32)
        xin = [sb(f"r_xin{i}", [128, NS + 3], F32) for i in range(2)]
        gg = [sb(f"r_gg{i}", [128, NS], F32) for i in range(3)]
        u2 = [sb(f"r_u{i}", [128, NS], F32) for i in range(2)]
        ubf2 = [sb(f"r_ubf{i}", [128, NS], BF16) for i in range(2)]
        r2_ = [sb(f"r_r{i}", [128, NS], F32) for i in range(2)]
        i2_ = [sb(f"r_i{i}", [128, NS], F32) for i in range(2)]
        a2_ = [sb(f"r_a{i}", [128, NS], F32) for i in range(2)]
        m2_ = [sb(f"r_m{i}", [128, NS], F32) for i in range(2)]
        h_t = sb("r_h", [128, NS], F32)
        y_t = [sb(f"r_y{i}", [128, NS], BF16) for i in range(2)]
        hl = sb("r_hl", [128, 2], F32)
        pa = [pt(f"r_pa{i}", [128, 512]) for i in range(2)]
        px = [pt(f"r_px{i}", [128, 512]) for i in range(2)]
        s_wab, s_c = P.slot(), P.slot()
        s_xin = [P.slot() for _ in range(2)]
        s_gg = [P.slot() for _ in range(3)]
        s_u2, s_ubf2, s_r2, s_i2, s_a2, s_m2 = [[P.slot() for _ in range(2)] for _ in range(6)]
        s_h, s_hl = P.slot(), P.slot()
        s_y = [P.slot() for _ in range(2)]
        s_pa = [P.slot() for _ in range(2)]
        s_px = [P.slot() for _ in range(2)]
        par = C.par
        P.begin()
        P.op("pool", lambda e: e.memset(wab[:], 0.0), writes=[s_wab])
        for gi, wsrc in enumerate((rg_wa, rg_wx)):
            wr = wsrc.rearrange("(c two) j k -> two j c k", two=2)
            for half in range(2):
                P.dma("pool", wab[half * 64:(half + 1) * 64, :, gi, half * 64:(half + 1) * 64], wr[half],
                      writes=[s_wab], pbase=half * 64)
        P.op("act", lambda e: e.activation(cc_t[:, :, 2], par[:, l, PC_LAM:PC_LAM + 8], AF.Exp, scale=-1.0),
             reads=[C.s_par], writes=[s_c])
        P.op("act", lambda e: e.activation(cc_t[:, :, 3], cc_t[:, :, 2], AF.Ln, bias=C.eps[:, 1:2]),
             reads=[s_c, C.s_eps], writes=[s_c])

        def cfin(e):
            e.tensor_scalar(cc_t[:, :, 0], cc_t[:, :, 3], -8.0, None, ALU.mult)
            return e.tensor_scalar(cc_t[:, :, 1], cc_t[:, :, 3], -16.0, None, ALU.mult)
        P.op("dve", cfin, reads=[s_c], writes=[s_c])
        pi = 0
        items = [(cc, sg_) for cc in chunks for sg_ in range(TA // NS)]

        def load_in(ii):
            cc_, sg2 = items[ii]
            rows_ = slice(cc_ * 128, (cc_ + 1) * 128)
            t0_ = sg2 * NS
            b_ = ii % 2
            if t0_ == 0:
                P.op("pool", lambda e, b_=b_: e.memset(xin[b_][:, 0:3], 0.0), writes=[s_xin[b_]])
                P.dma("sp", xin[b_][:, 3:], S["rgx"][rows_, 0:NS], writes=[s_xin[b_]])
            else:
                P.dma("sp", xin[b_][:], S["rgx"][rows_, t0_ - 3:t0_ + NS], writes=[s_xin[b_]])
            P.dma("sp", gg[ii % 3][:], S["rgg"][rows_, t0_:t0_ + NS], writes=[s_gg[ii % 3]])
        load_in(0)
        pi_box = [0]

        def stage_a(ii):
            cc, sg_ = items[ii]
            b = ii % 2
            u, ubf, r_t, i_t, a_t, m_t = u2[b], ubf2[b], r2_[b], i2_[b], a2_[b], m2_[b]
            s_u, s_ubf, s_r, s_i, s_a, s_m = s_u2[b], s_ubf2[b], s_r2[b], s_i2[b], s_a2[b], s_m2[b]
            if ii + 1 < len(items):
                load_in(ii + 1)
            cw = lambda tap: par[:, l, PC_CONVW + tap * 8 + cc:PC_CONVW + tap * 8 + cc + 1]
            P.op("dve", lambda e: e.tensor_scalar(
                u[:], xin[b][:, 0:NS], cw(0), par[:, l, PC_CONVB + cc:PC_CONVB + cc + 1], ALU.mult, ALU.add),
                reads=[s_xin[b], C.s_par], writes=[s_u])
            for tap in range(1, 4):
                P.op("dve", lambda e, tap=tap: e.scalar_tensor_tensor(
                    u[:], xin[b][:, tap:tap + NS], cw(tap), u[:], ALU.mult, ALU.add),
                    reads=[s_xin[b], C.s_par, s_u], writes=[s_u])
            P.op("pool", lambda e: e.tensor_copy(ubf[:], u[:]), reads=[s_u], writes=[s_ubf])
            for blk in range(NS // 512):
                j = pi_box[0] % 2
                pi_box[0] += 1
                bs = slice(blk * 512, (blk + 1) * 512)

                def mm(e, j=j, bs=bs):
                    e.matmul(pa[j][:], wab[:, cc, 0, :], ubf[:, bs], start=True, stop=True)
                    return e.matmul(px[j][:], wab[:, cc, 1, :], ubf[:, bs], start=True, stop=True)
                P.op("pe", mm, reads=[s_wab, s_ubf], writes=[s_pa[j], s_px[j]])

                def sig(e, j=j, bs=bs):
                    e.activation(r_t[:, bs], pa[j][:], AF.Sigmoid, bias=par[:, l, PC_RGBA + cc:PC_RGBA + cc + 1])
                    return e.activation(i_t[:, bs], px[j][:], AF.Sigmoid, bias=par[:, l, PC_RGBX + cc:PC_RGBX + cc + 1])
                P.op("act", sig, reads=[s_pa[j], s_px[j], C.s_par], writes=[s_r, s_i])

            def aexp(e):
                e.activation(a_t[:], r_t[:], AF.Exp, scale=cc_t[:, cc, 0:1])
                return e.activation(m_t[:], r_t[:], AF.Exp, scale=cc_t[:, cc, 1:2])
            P.op("act", aexp, reads=[s_r, s_c], writes=[s_a, s_m])
            P.op("act", lambda e: e.activation(m_t[:], m_t[:], AF.Sqrt, bias=C.eps[:, 1:2], scale=-1.0),
                 reads=[s_m, C.s_eps], writes=[s_m])
            P.op("pool", lambda e: e.tensor_tensor(i_t[:], i_t[:], m_t[:], ALU.mult), reads=[s_i, s_m], writes=[s_i])

        def stage_b(ii):
            cc, sg_ = items[ii]
            b = ii % 2
            rows = slice(cc * 128, (cc + 1) * 128)
            t0 = sg_ * NS
            u, i_t, a_t = u2[b], i2_[b], a2_[b]
            s_u, s_i, s_a = s_u2[b], s_i2[b], s_a2[b]
            P.op("dve", lambda e: e.tensor_tensor(i_t[:], i_t[:], u[:], ALU.mult), reads=[s_i, s_u], writes=[s_i])
            if t0 == 0:
                P.op("dve", lambda e: e.tensor_tensor_scan(h_t[:], a_t[:], i_t[:], 0.0, ALU.mult, ALU.add),
                     reads=[s_a, s_i], writes=[s_h])
            else:
                P.op("dve", lambda e: e.tensor_tensor_scan(h_t[:], a_t[:], i_t[:], hl[:, 0:1], ALU.mult, ALU.add),
                     reads=[s_a, s_i, s_hl], writes=[s_h])
            P.op("dve", lambda e: e.tensor_copy(hl[:, 0:1], h_t[:, NS - 1:NS]), reads=[s_h], writes=[s_hl])
            P.op("pool", lambda e: e.tensor_tensor(y_t[b][:], h_t[:], gg[ii % 3][:], ALU.mult),
                 reads=[s_h, s_gg[ii % 3]], writes=[s_y[b]])
            P.dma("sp", S["yaT"][rows, t0:t0 + NS], y_t[b][:], reads=[s_y[b]])

        stage_a(0)
        for ii in range(len(items)):
            if ii + 1 < len(items):
                stage_a(ii + 1)
            stage_b(ii)
        P.end()


import math
TWO_PI = 2.0 * math.pi
CW1 = 6.28125
CW2 = TWO_PI - CW1
MLA_SCALE = 192 ** -0.5


def phase_rope(P, nc, C, pos_d, ropec_d, R, TA):
    N = min(2048, TA)
    with ExitStack() as es:
        sb = mk_sb(nc, es)
        rc = sb("rp_c", [64, 2], F32)
        pi_ = sb("rp_pi", [64, N], I32)
        ang = sb("rp_ang", [64, N], F32)
        kf = sb("rp_kf", [64, N], F32)
        ki = sb("rp_ki", [64, N], I32)
        r = sb("rp_r", [64, N], F32)
        m = sb("rp_m", [64, N], F32)
        o = [sb(f"rp_o{i}", [64, N], F32) for i in range(4)]
        s_rc, s_pi, s_w = P.slot(), P.slot(), P.slot()
        s_o = [P.slot() for _ in range(4)]
        P.begin()
        P.dma("sp", rc[:], ropec_d, writes=[s_rc])
        for sg_ in range(TA // N):
            t0 = sg_ * N
            P.dma("sp", pi_[:], pos_d[0:1, t0:t0 + N].broadcast_to([64, N]), writes=[s_pi])

            def D1(fn, reads=(), extra_w=()):
                P.op("dve", fn, reads=[s_w] + list(reads), writes=[s_w] + list(extra_w))

            def wrap():
                D1(lambda e: e.tensor_scalar(m[:], r[:], math.pi, -TWO_PI, ALU.is_gt, ALU.mult))
                D1(lambda e: e.tensor_tensor(r[:], r[:], m[:], ALU.add))
                D1(lambda e: e.tensor_scalar(m[:], r[:], -math.pi, TWO_PI, ALU.is_lt, ALU.mult))
                D1(lambda e: e.tensor_tensor(r[:], r[:], m[:], ALU.add))

            D1(lambda e: e.tensor_copy(ang[:], pi_[:]), reads=[s_pi])
            D1(lambda e: e.tensor_scalar(ang[:], ang[:], rc[:, 0:1], None, ALU.mult), reads=[s_rc])
            D1(lambda e: e.tensor_scalar(ki[:], ang[:], 1.0 / TWO_PI, None, ALU.mult))
            D1(lambda e: e.tensor_copy(kf[:], ki[:]))
            D1(lambda e: e.scalar_tensor_tensor(r[:], kf[:], -CW1, ang[:], ALU.mult, ALU.add), reads=[s_o[0], s_o[1]])
            D1(lambda e: e.scalar_tensor_tensor(r[:], kf[:], -CW2, r[:], ALU.mult, ALU.add))
            wrap()
            P.op("act", lambda e: e.activation(o[1][:], r[:], AF.Sin), reads=[s_w], writes=[s_o[1]])
            D1(lambda e: e.tensor_scalar(r[:], r[:], math.pi / 2, None, ALU.add), reads=[s_o[1]])
            wrap()
            P.op("act", lambda e: e.activation(o[0][:], r[:], AF.Sin), reads=[s_w], writes=[s_o[0]])
            P.op("dve", lambda e: e.tensor_scalar(o[1][:], o[1][:], rc[:, 1:2], None, ALU.mult), reads=[s_o[1], s_rc], writes=[s_o[1]])
            P.op("dve", lambda e: e.tensor_scalar(o[2][:], o[0][:], MLA_SCALE, None, ALU.mult), reads=[s_o[0]], writes=[s_o[2]])
            P.op("dve", lambda e: e.tensor_scalar(o[3][:], o[1][:], MLA_SCALE, None, ALU.mult), reads=[s_o[1]], writes=[s_o[3]])
            for i, nm in enumerate(("cos", "sin", "cosq", "sinq")):
                P.dma("sp", R[nm][:, t0:t0 + N], o[i][:], reads=[s_o[i]])
        P.end()


def alloc_rope(nc, TA):
    return {nm: nc.dram_tensor("r_" + nm, [64, TA], F32).ap() for nm in ("cos", "sin", "cosq", "sinq")}


def phase_mlaprep(P, nc, C, w_uq, w_ukv, l, S, R, TA, heads=range(8)):
    heads = list(heads)
    NH = len(heads)
    wq3 = w_uq.rearrange("(k p) n -> p k n", p=128)
    wq4 = w_uq.rearrange("(k p) (h c) -> p k h c", p=128, c=192)
    wkv4 = w_ukv.rearrange("(k p) (h c) -> p k h c", p=128, c=256)
    with ExitStack() as es:
        sb = mk_sb(nc, es)
        pt = mk_pt(nc, es)
        mk_stage(P, sb)
        wq = sb("m_wq", [128, 2, 8, 192], BF16)
        wqs = sb("m_wqs", [128, 2, 8, 64], BF16)
        wkv = sb("m_wkv", [128, 2, 8, 256], BF16)
        wv = sb("m_wv", [128, 2, NH, 128], BF16)
        cin = [sb(f"m_cin{i}", [128, 2, 512], F32) for i in range(2)]
        cn = [sb(f"m_cn{i}", [128, 2, 512], BF16) for i in range(2)]
        sq = sb("m_sq", [128, 2, 512], BF16)
        lnv = sb("m_lnv", [128, 512], F32)
        rstd = sb("m_rstd", [128, 512], F32)
        tab = [sb(f"m_tab{i}", [64, 512], F32) for i in range(4)]
        krt = [sb(f"m_kr{i}", [64, 512], F32) for i in range(2)]
        t1 = sb("m_t1", [64, 512], F32)
        t2 = sb("m_t2", [64, 512], F32)
        NST = 4
        stb = [sb(f"m_stb{i}", [128, 512], BF16) for i in range(NST)]
        pss = pt("m_pss", [128, 512])
        NPW = 4
        pw = [pt(f"m_pw{i}", [128, 512]) for i in range(NPW)]
        pr = [pt(f"m_pr{i}", [64, 512]) for i in range(2)]
        s_w = P.slot()
        s_cin = [P.slot() for _ in range(2)]
        s_cn = [P.slot() for _ in range(2)]
        s_sq, s_ln, s_rstd, s_pss = P.slot(), P.slot(), P.slot(), P.slot()
        s_tab = [P.slot() for _ in range(4)]
        s_kr = [P.slot() for _ in range(2)]
        s_t1, s_t2 = P.slot(), P.slot()
        s_stb = [P.slot() for _ in range(NST)]
        s_pw = [P.slot() for _ in range(NPW)]
        s_pr = [P.slot() for _ in range(2)]
        P.begin()
        for k in range(2):
            P.dma("pool", wq[:, k], wq4[:, k], writes=[s_w], part=(k > 0))
        for k in range(2):
            P.dma("pool", wqs[:, k, :, 0:32], wq4[:, k, :, 160:192], writes=[s_w], part=True)
            P.dma("pool", wqs[:, k, :, 32:64], wq4[:, k, :, 128:160], writes=[s_w], part=True)
        for k in range(2):
            P.dma("pool", wkv[:, k], wkv4[:, k], writes=[s_w], part=True)
        for hi, h in enumerate(heads):
            P.dma("pool", wv[:, :, hi, :], wkv4[:, :, h, 128:256], writes=[s_w], part=True)
        pi = 0
        si = 0

        def evac_store(src_ap, s_src, M, dst_ap, scale=None):
            nonlocal si
            sj = si % NST
            si += 1
            if scale is None:
                P.op("dve", lambda e: e.tensor_copy(stb[sj][:M, :], src_ap), reads=[s_src], writes=[s_stb[sj]])
            else:
                P.op("dve", lambda e: e.tensor_scalar(stb[sj][:M, :], src_ap, scale, None, ALU.mult), reads=[s_src], writes=[s_stb[sj]])
            P.dma("sp", dst_ap, stb[sj][:M, :], reads=[s_stb[sj]])

        for tix in range(TA // 512):
            t0 = tix * 512
            tsl = slice(t0, t0 + 512)
            P.dma("sp", cin[0][:], S["cq"].rearrange("(k p) t -> p k t", p=128)[:, :, tsl], writes=[s_cin[0]])
            P.dma("sp", cin[1][:], S["ckv"].rearrange("(k p) t -> p k t", p=128)[:, :, tsl], writes=[s_cin[1]])
            for i, nm in enumerate(("cos", "sin", "cosq", "sinq")):
                P.dma("sp", tab[i][:], R[nm][:, tsl], writes=[s_tab[i]])
            P.dma("sp", krt[0][:], S["kr"][:, tsl], writes=[s_kr[0]])
            P.dma("sp", krt[1][:], S["krsw"][:, tsl], writes=[s_kr[1]])
            for i, pc in enumerate((PC_QN, PC_KVN)):
                rmsnorm_tile(P, C, cin[i][:], s_cin[i], cn[i][:], s_cn[i],
                             lambda k, pc=pc: C.par[:, l, pc + k:pc + k + 1], 2,
                             sq, s_sq, pss, s_pss, lnv, s_ln, rstd, s_rstd, 512, 1.0 / 256)
            P.op("dve", lambda e: e.tensor_tensor(t1[:], krt[0][:], tab[0][:], ALU.mult), reads=[s_kr[0], s_tab[0]], writes=[s_t1])
            P.op("dve", lambda e: e.tensor_tensor(t2[:], krt[1][:], tab[1][:], ALU.mult), reads=[s_kr[1], s_tab[1]], writes=[s_t2])
            sj = si % NST
            si += 1
            P.op("dve", lambda e, sj=sj: e.tensor_tensor(stb[sj][:64, :], t1[:], t2[:], ALU.add), reads=[s_t1, s_t2], writes=[s_stb[sj]])
            P.dma("sp", S["mkr"][:, tsl], stb[sj][:64, :], reads=[s_stb[sj]])
            for hi, h in enumerate(heads):
                j = pi % NPW
                pi += 1

                def mm(e, j=j, h=h):
                    e.matmul(pw[j][:], wq[:, 0, h, 0:128], cn[0][:, 0, :], start=True, stop=False)
                    return e.matmul(pw[j][:], wq[:, 1, h, 0:128], cn[0][:, 1, :], start=False, stop=True)
                P.op("pe", mm, reads=[s_w, s_cn[0]], writes=[s_pw[j]])
                evac_store(pw[j][:], s_pw[j], 128, S["mqn"][h * 128:(h + 1) * 128, tsl], scale=MLA_SCALE)
                def mmr(e, h=h):
                    e.matmul(pr[0][:], wq[:, 0, h, 128:192], cn[0][:, 0, :], start=True, stop=False)
                    e.matmul(pr[0][:], wq[:, 1, h, 128:192], cn[0][:, 1, :], start=False, stop=True)
                    e.matmul(pr[1][:], wqs[:, 0, h, :], cn[0][:, 0, :], start=True, stop=False)
                    return e.matmul(pr[1][:], wqs[:, 1, h, :], cn[0][:, 1, :], start=False, stop=True)
                P.op("pe", mmr, reads=[s_w, s_cn[0]], writes=[s_pr[0], s_pr[1]])
                P.op("dve", lambda e: e.tensor_tensor(t1[:], pr[0][:], tab[2][:], ALU.mult), reads=[s_pr[0], s_tab[2]], writes=[s_t1])
                P.op("dve", lambda e: e.tensor_tensor(t2[:], pr[1][:], tab[3][:], ALU.mult), reads=[s_pr[1], s_tab[3]], writes=[s_t2])
                sj = si % NST
                si += 1
                P.op("dve", lambda e, sj=sj: e.tensor_tensor(stb[sj][:64, :], t1[:], t2[:], ALU.add), reads=[s_t1, s_t2], writes=[s_stb[sj]])
                P.dma("sp", S["mqr"][h * 64:(h + 1) * 64, tsl], stb[sj][:64, :], reads=[s_stb[sj]])
                j = pi % NPW
                pi += 1

                def mmk(e, j=j, h=h):
                    e.matmul(pw[j][:], wkv[:, 0, h, 0:128], cn[1][:, 0, :], start=True, stop=False)
                    return e.matmul(pw[j][:], wkv[:, 1, h, 0:128], cn[1][:, 1, :], start=False, stop=True)
                P.op("pe", mmk, reads=[s_w, s_cn[1]], writes=[s_pw[j]])
                evac_store(pw[j][:], s_pw[j], 128, S["mkn"][h * 128:(h + 1) * 128, tsl])
            for tb in range(4):
                for c0 in range(0, NH * 128, 512):
                    n = min(512, NH * 128 - c0)
                    j = pi % NPW
                    pi += 1

                    def mmv(e, j=j, tb=tb, c0=c0, n=n):
                        wv2 = wv[:].rearrange("p k h c -> p k (h c)")
                        e.matmul(pw[j][:, :n], cn[1][:, 0, tb * 128:(tb + 1) * 128], wv2[:, 0, c0:c0 + n], start=True, stop=False)
                        return e.matmul(pw[j][:, :n], cn[1][:, 1, tb * 128:(tb + 1) * 128], wv2[:, 1, c0:c0 + n], start=False, stop=True)
                    P.op("pe", mmv, reads=[s_w, s_cn[1]], writes=[s_pw[j]])
                    sj = si % NST
                    si += 1
                    P.op("dve", lambda e, j=j, sj=sj, n=n: e.tensor_copy(stb[sj][:, :n], pw[j][:, :n]), reads=[s_pw[j]], writes=[s_stb[sj]])
                    col0 = heads[0] * 128 + c0
                    P.dma("sp", S["mv"][t0 + tb * 128:t0 + (tb + 1) * 128, col0:col0 + n], stb[sj][:, :n], reads=[s_stb[sj]])
        P.end()


def setup_attn_consts(P, nc, C, es):
    C.negtri = es.enter_context(nc.sbuf_tensor("c_negtri", [128, 128], BF16))
    C.s_negtri = P.slot("negtri")
    C.negones = es.enter_context(nc.sbuf_tensor("c_negones", [128, 128], BF16))
    C.s_negones = P.slot("negones")
    P.begin()
    P.op("pool", lambda e: e.memset(C.negones[:], -1.0), writes=[C.s_negones])
    P.op("pool", lambda e: e.memset(C.negtri[:], -1.0), writes=[C.s_negtri])
    P.op("pool", lambda e: e.affine_select(C.negtri[:], C.negtri[:], [[-1, 128]], ALU.is_ge, 0.0, base=0, channel_multiplier=1),
         reads=[C.s_negtri], writes=[C.s_negtri])
    P.end()


def phase_sb(P, nc, C, S, TA, heads=range(8)):
    heads = list(heads)
    NB = TA // 128
    NG = TA // 512
    v3 = S["v"].rearrange("(b p) c -> p b c", p=128)
    with ExitStack() as es:
        sb = mk_sb(nc, es)
        pt = mk_pt(nc, es)
        mk_stage(P, sb)
        kT = [sb(f"a_kT{i}", [128, TA], BF16) for i in range(2)]
        qT = [sb(f"a_qT{i}", [128, TA], BF16) for i in range(2)]
        vt = [sb(f"a_v{i}", [128, NB, 128], BF16) for i in range(2)]
        ef = [sb(f"a_ef{i}", [128, 512], F32) for i in range(2)]
        spb = [sb(f"a_spb{i}", [128, 512], BF16) for i in range(2)]
        lw = [sb(f"a_lw{i}", [128, 512], F32) for i in range(2)]
        wb = [sb(f"a_wb{i}", [128, 512], BF16) for i in range(2)]
        cB = [sb(f"a_cB{i}", [128, 512], F32) for i in range(2)]
        yst = [sb(f"a_yst{i}", [128, 512], BF16) for i in range(2)]
        pA = [pt(f"a_pA{i}", [128, 512]) for i in range(2)]
        pB = [pt(f"a_pB{i}", [128, 512]) for i in range(2)]
        pC = [pt(f"a_pC{i}", [128, 512]) for i in range(2)]
        pY = [pt(f"a_pY{i}", [128, 512]) for i in range(2)]
        mk = lambda n: [P.slot() for _ in range(n)]
        s_kT, s_qT, s_vt, s_ef, s_spb, s_lw, s_wb, s_cB, s_yst = [mk(2) for _ in range(9)]
        s_pA, s_pB, s_pC, s_pY = [mk(2) for _ in range(4)]
        units = []
        for hi, h in enumerate(heads):
            for g in range(NG):
                nkb = 4 * g + 4
                for kb in reversed(range(nkb)):
                    units.append((hi, h, g, kb, kb == nkb - 1, kb == 0))
        gidx = {}
        hstart = {}
        for n_, u in enumerate(units):
            key = (u[0], u[2])
            if key not in gidx:
                gidx[key] = len(gidx)
            if u[0] not in hstart:
                hstart[u[0]] = n_
        P.begin()

        def load_head(hi, h):
            hb = hi % 2
            rows = slice(h * 128, (h + 1) * 128)
            P.dma("sp", kT[hb][:], S["kT"][rows, 0:TA], writes=[s_kT[hb]])
            P.dma("sp", qT[hb][:], S["qT"][rows, 0:TA], writes=[s_qT[hb]])
            P.dma("sp", vt[hb][:], v3[:, :, rows], writes=[s_vt[hb]])

        def st1(n):
            hi, h, g, kb, first, last = units[n]
            hb, j = hi % 2, n % 2
            if n == 0:
                load_head(0, heads[0])
            if n == hstart[hi] + 3 and hi + 1 < len(heads):
                load_head(hi + 1, heads[hi + 1])
            ks = slice(kb * 128, (kb + 1) * 128)
            qs = slice(g * 512, (g + 1) * 512)
            P.op("pe", lambda e: e.matmul(pA[j][:], kT[hb][:, ks], qT[hb][:, qs], start=True, stop=True),
                 reads=[s_kT[hb], s_qT[hb]], writes=[s_pA[j]])
            P.op("act", lambda e: e.activation(ef[j][:], pA[j][:], AF.Exp), reads=[s_pA[j]], writes=[s_ef[j]])
            P.op("act", lambda e: e.activation(spb[j][:], ef[j][:], AF.Ln, bias=C.eps[:, 1:2]),
                 reads=[s_ef[j], C.s_eps], writes=[s_spb[j]])
            i = kb - 4 * g
            if i >= 0:
                P.op("pool", lambda e: e.affine_select(spb[j][:], spb[j][:], [[1, 512]], ALU.is_gt, 0.0,
                                                       base=-128 * i, channel_multiplier=-1),
                     reads=[s_spb[j]], writes=[s_spb[j]])

        def st2(n):
            hi, h, g, kb, first, last = units[n]
            hb, j = hi % 2, n % 2
            gb = gidx[(hi, g)] % 2
            ks = slice(kb * 128, (kb + 1) * 128)
            qs = slice(g * 512, (g + 1) * 512)
            if first:
                P.op("pool", lambda e: e.memset(cB[gb][:], 0.0), writes=[s_cB[gb]])

            def mm(e):
                e.matmul(pB[j][:], kT[hb][:, ks], qT[hb][:, qs], start=True, stop=False)
                e.matmul(pB[j][:], C.negtri[:], spb[j][:], start=False, stop=True)
                return e.matmul(pC[j][:], C.ones[:], spb[j][:], start=True, stop=True)
            P.op("pe", mm, reads=[s_kT[hb], s_qT[hb], s_spb[j], C.s_negtri, C.s_ones], writes=[s_pB[j], s_pC[j]])
            P.op("dve", lambda e: e.tensor_tensor(lw[j][:], pB[j][:], cB[gb][:], ALU.subtract),
                 reads=[s_pB[j], s_cB[gb]], writes=[s_lw[j]])
            if not last:
                P.op("dve", lambda e: e.tensor_tensor(cB[gb][:], pC[j][:], cB[gb][:], ALU.add),
                     reads=[s_pC[j], s_cB[gb]], writes=[s_cB[gb]])
            P.op("act", lambda e: e.activation(wb[j][:], lw[j][:], AF.Exp), reads=[s_lw[j]], writes=[s_wb[j]])
            i = kb - 4 * g
            if i >= 0:
                P.op("pool", lambda e: e.affine_select(wb[j][:], wb[j][:], [[1, 512]], ALU.is_gt, 0.0,
                                                       base=-128 * i, channel_multiplier=-1),
                     reads=[s_wb[j]], writes=[s_wb[j]])

        def st3(n):
            hi, h, g, kb, first, last = units[n]
            hb, j = hi % 2, n % 2
            gb = gidx[(hi, g)] % 2
            P.op("pe", lambda e: e.matmul(pY[gb][:], vt[hb][:, kb, :], wb[j][:], start=first, stop=last),
                 reads=[s_vt[hb], s_wb[j]], writes=[s_pY[gb]])
            if last:
                P.op("dve", lambda e: e.tensor_copy(yst[gb][:], pY[gb][:]), reads=[s_pY[gb]], writes=[s_yst[gb]])
                P.dma("sp", S["ybT"][h * 128:(h + 1) * 128, g * 512:(g + 1) * 512], yst[gb][:], reads=[s_yst[gb]])

        NU = len(units)
        for step in range(NU + 2):
            if step < NU:
                st1(step)
            if 0 <= step - 1 < NU:
                st2(step - 1)
            if 0 <= step - 2 < NU:
                st3(step - 2)
        P.end()


def phase_mla(P, nc, C, S, TA, heads=range(8)):
    heads = list(heads)
    NB = TA // 128
    NG = TA // 512
    v3 = S["mv"].rearrange("(b p) c -> p b c", p=128)
    with ExitStack() as es:
        sb = mk_sb(nc, es)
        pt = mk_pt(nc, es)
        mk_stage(P, sb)
        kr = sb("b_kr", [64, TA], BF16)
        kn = [sb(f"b_kn{i}", [128, TA], BF16) for i in range(2)]
        qn = [sb(f"b_qn{i}", [128, TA], BF16) for i in range(2)]
        qr = [sb(f"b_qr{i}", [64, TA], BF16) for i in range(2)]
        vt = [sb(f"b_v{i}", [128, NB, 128], BF16) for i in range(2)]
        pb = [sb(f"b_pb{i}", [128, 512], BF16) for i in range(3)]
        lnd = sb("b_lnd", [128, 512], F32)
        rd = sb("b_rd", [128, 512], F32)
        yst = [sb(f"b_yst{i}", [128, 512], BF16) for i in range(2)]
        pS = [pt(f"b_pS{i}", [128, 512]) for i in range(3)]
        pN = [pt(f"b_pN{i}", [128, 512]) for i in range(2)]
        pD = [pt(f"b_pD{i}", [128, 512]) for i in range(2)]
        mk = lambda n: [P.slot() for _ in range(n)]
        s_kn, s_qn, s_qr, s_vt, s_yst, s_pN, s_pD = [mk(2) for _ in range(7)]
        s_pb, s_pS = mk(3), mk(3)
        s_kr, s_lnd, s_rd = P.slot(), P.slot(), P.slot()
        units = []
        for hi, h in enumerate(heads):
            for g in range(NG):
                nkb = 4 * g + 4
                for kb in range(nkb):
                    units.append((hi, h, g, kb, kb == 0, kb == nkb - 1))
        gidx = {}
        hstart = {}
        for n_, u in enumerate(units):
            key = (u[0], u[2])
            if key not in gidx:
                gidx[key] = len(gidx)
            if u[0] not in hstart:
                hstart[u[0]] = n_
        P.begin()
        P.dma("sp", kr[:], S["mkr"][:, 0:TA], writes=[s_kr])

        def load_head(hi, h):
            hb = hi % 2
            rows = slice(h * 128, (h + 1) * 128)
            P.dma("sp", kn[hb][:], S["mkn"][rows, 0:TA], writes=[s_kn[hb]])
            P.dma("sp", qn[hb][:], S["mqn"][rows, 0:TA], writes=[s_qn[hb]])
            P.dma("sp", qr[hb][:], S["mqr"][h * 64:(h + 1) * 64, 0:TA], writes=[s_qr[hb]])
            P.dma("sp", vt[hb][:], v3[:, :, rows], writes=[s_vt[hb]])

        def st1(n):
            hi, h, g, kb, first, last = units[n]
            hb, j = hi % 2, n % 3
            if n == 0:
                load_head(0, heads[0])
            if n == hstart[hi] + 3 and hi + 1 < len(heads):
                load_head(hi + 1, heads[hi + 1])
            ks = slice(kb * 128, (kb + 1) * 128)
            qs = slice(g * 512, (g + 1) * 512)

            def mm(e):
                e.matmul(pS[j][:], kn[hb][:, ks], qn[hb][:, qs], start=True, stop=False)
                return e.matmul(pS[j][:], kr[:, ks], qr[hb][:, qs], start=False, stop=True)
            P.op("pe", mm, reads=[s_kn[hb], s_qn[hb], s_kr, s_qr[hb]], writes=[s_pS[j]])
            P.op("act", lambda e: e.activation(pb[j][:], pS[j][:], AF.Exp), reads=[s_pS[j]], writes=[s_pb[j]])
            i = kb - 4 * g
            if i >= 0:
                def msk(e):
                    ins = e.memset(pb[j][64:128, 128 * i:128 * i + 64], 0.0)
                    if i > 0:
                        ins = e.memset(pb[j][:, 0:128 * i], 0.0)
                    return ins
                P.op("pool", msk, reads=[s_pb[j]], writes=[s_pb[j]])

        def st2(n):
            hi, h, g, kb, first, last = units[n]
            hb, j = hi % 2, n % 3
            gb = gidx[(hi, g)] % 2

            def mm(e):
                e.matmul(pN[gb][:], vt[hb][:, kb, :], pb[j][:], start=first, stop=last)
                return e.matmul(pD[gb][:], C.ones[:], pb[j][:], start=first, stop=last)
            P.op("pe", mm, reads=[s_vt[hb], s_pb[j], C.s_ones], writes=[s_pN[gb], s_pD[gb]])
            if last:
                P.op("act", lambda e: e.activation(lnd[:], pD[gb][:], AF.Ln), reads=[s_pD[gb]], writes=[s_lnd])
                P.op("act", lambda e: e.activation(rd[:], lnd[:], AF.Exp, scale=-1.0), reads=[s_lnd], writes=[s_rd])
                P.op("dve", lambda e: e.tensor_tensor(yst[gb][:], pN[gb][:], rd[:], ALU.mult),
                     reads=[s_pN[gb], s_rd], writes=[s_yst[gb]])
                P.dma("sp", S["ycT"][h * 128:(h + 1) * 128, g * 512:(g + 1) * 512], yst[gb][:], reads=[s_yst[gb]])

        NU = len(units)
        for step in range(NU + 1):
            if step < NU:
                st1(step)
            if 0 <= step - 1 < NU:
                st2(step - 1)
        P.end()


def phase_merge(P, nc, C, xT, w_a, w_b, w_c, w_o, S, T, tok0=0):
    xT3 = xT.rearrange("(k p) t -> p k t", p=128)
    g4 = S["gate"].rearrange("(b k p) t -> p b k t", p=128, k=8)
    with ExitStack() as es:
        sb = mk_sb(nc, es)
        pt = mk_pt(nc, es)
        mk_stage(P, sb)
        W = [sb(f"g_w{i}", [128, KD, 1024], BF16) for i in range(4)]
        y = [sb(f"g_y{i}", [128, KD, 512], BF16) for i in range(3)]
        gt = [sb(f"g_gt{i}", [128, 3, 512], F32) for i in range(2)]
        m = [sb(f"g_m{i}", [128, 512], F32) for i in range(3)]
        mg = sb("g_mg", [128, KD, 512], BF16)
        xg = sb("g_xg", [128, KD, 512], F32)
        pP = [[pt(f"g_p{b}{i}", [128, 512]) for i in range(2)] for b in range(3)]
        pO = [pt(f"g_po{i}", [128, 512]) for i in range(2)]
        mk = lambda n: [P.slot() for _ in range(n)]
        s_W, s_y, s_gt, s_m, s_pO = mk(4), mk(3), mk(2), mk(3), mk(2)
        s_pP = [mk(2) for _ in range(3)]
        s_mg, s_xg = P.slot(), P.slot()
        P.begin()
        for i, w in enumerate((w_a, w_b, w_c, w_o)):
            w3 = w.rearrange("(k p) n -> p k n", p=128)
            for k in range(KD):
                P.dma("pool", W[i][:, k, :], w3[:, k, :], writes=[s_W[i]], part=(k > 0))
        it = 0
        for tix in range(T // 512):
            t0 = tix * 512
            ta = tok0 + t0
            for i, nm in enumerate(("yaT", "ybT", "ycT")):
                P.dma("sp", y[i][:], S[nm].rearrange("(k p) t -> p k t", p=128)[:, :, ta:ta + 512], writes=[s_y[i]])
            P.dma("sp", xg[:], xT3[:, :, t0:t0 + 512], writes=[s_xg])
            for oc in range(KD):
                j = it % 2
                it += 1
                P.dma("sp", gt[j][:], g4[:, :, oc, ta:ta + 512], writes=[s_gt[j]])
                cs = slice(oc * 128, (oc + 1) * 128)

                def mm(e, j=j, cs=cs):
                    ins = None
                    for b in range(3):
                        for k in range(KD):
                            ins = e.matmul(pP[b][j][:], W[b][:, k, cs], y[b][:, k, :], start=(k == 0), stop=(k == KD - 1))
                    return ins
                P.op("pe", mm, reads=s_W[:3] + s_y, writes=[s_pP[b][j] for b in range(3)])
                for b in range(3):
                    P.op("dve", lambda e, b=b, j=j: e.tensor_tensor(m[b][:], pP[b][j][:], gt[j][:, b, :], ALU.mult),
                         reads=[s_pP[b][j], s_gt[j]], writes=[s_m[b]])
                P.op("pool", lambda e: e.tensor_tensor(m[0][:], m[0][:], m[1][:], ALU.add), reads=[s_m[0], s_m[1]], writes=[s_m[0]])
                P.op("pool", lambda e, oc=oc: e.tensor_tensor(mg[:, oc, :], m[0][:], m[2][:], ALU.add),
                     reads=[s_m[0], s_m[2]], writes=[s_mg])
            for oc in range(KD):
                j = oc % 2
                cs = slice(oc * 128, (oc + 1) * 128)

                def mm2(e, j=j, cs=cs):
                    ins = None
                    for k in range(KD):
                        ins = e.matmul(pO[j][:], W[3][:, k, cs], mg[:, k, :], start=(k == 0), stop=(k == KD - 1))
                    return ins
                P.op("pe", mm2, reads=[s_W[3], s_mg], writes=[s_pO[j]])
                P.op("dve", lambda e, j=j, oc=oc: e.tensor_tensor(xg[:, oc, :], pO[j][:], xg[:, oc, :], ALU.add),
                     reads=[s_pO[j], s_xg], writes=[s_xg])
            P.dma("sp", xT3[:, :, t0:t0 + 512], xg[:], reads=[s_xg])
        P.end()


def phase_final(P, nc, C, xT, outT, T):
    xT3 = xT.rearrange("(k p) t -> p k t", p=128)
    oT3 = outT.rearrange("(k p) t -> p k t", p=128)
    with ExitStack() as es:
        sb = mk_sb(nc, es)
        pt = mk_pt(nc, es)
        mk_stage(P, sb)
        xg = [sb(f"n_xg{i}", [128, KD, 512], F32) for i in range(2)]
        og = [sb(f"n_og{i}", [128, KD, 512], F32) for i in range(2)]
        sq = sb("n_sq", [128, KD, 512], BF16)
        lnv = sb("n_lnv", [128, 512], F32)
        rstd = sb("n_rstd", [128, 512], F32)
        pss = pt("n_pss", [128, 512])
        s_xg = [P.slot() for _ in range(2)]
        s_og = [P.slot() for _ in range(2)]
        s_sq, s_ln, s_rstd, s_pss = P.slot(), P.slot(), P.slot(), P.slot()
        P.begin()
        for tix in range(T // 512):
            t0 = tix * 512
            b = tix % 2
            P.dma("sp", xg[b][:], xT3[:, :, t0:t0 + 512], writes=[s_xg[b]])
            rmsnorm_tile(P, C, xg[b][:], s_xg[b], og[b][:], s_og[b],
                         lambda k: C.par[:, 0, PC_FIN + k:PC_FIN + k + 1], KD,
                         sq, s_sq, pss, s_pss, lnv, s_ln, rstd, s_rstd, 512, 1.0 / D)
            P.dma("sp", oT3[:, :, t0:t0 + 512], og[b][:], reads=[s_og[b]])
        P.end()


from concourse.bass_utils import run_bass_kernel_spmd

DEPTH = 4
SEQ = 4096
BATCH = 4
WNAMES = ["ffn1_w_gate_up", "ffn1_w_down", "w_in", "rg_w_a", "rg_w_x", "mla_w_uq", "mla_w_ukv",
          "w_branch_a", "w_branch_b", "w_branch_c", "w_out", "ffn2_w_gate_up", "ffn2_w_down"]


ALLPH = ("ffn1", "win", "rg", "prep", "sb", "mla", "merge", "ffn2")


def build_program(wshapes, depth=DEPTH, seq=SEQ, sel=ALLPH):
    nc = bass.Bass("TRN2", target_bir_lowering=False)
    TA = seq
    xT_in = nc.dram_tensor("xT_in", [D, TA], F32, kind="ExternalInput").ap()
    pos_d = nc.dram_tensor("pos", [1, TA], I32, kind="ExternalInput").ap()
    par_d = nc.dram_tensor("par", [DEPTH, 128, NPAR], F32, kind="ExternalInput").ap()
    ropec_d = nc.dram_tensor("ropec", [64, 2], F32, kind="ExternalInput").ap()
    Wd = {n: nc.dram_tensor(n, list(wshapes[n]), F32, kind="ExternalInput").ap() for n in WNAMES}
    outT = nc.dram_tensor("outT", [D, TA], F32, kind="ExternalOutput").ap()
    xs = nc.dram_tensor("xs", [D, TA], F32).ap()
    S = alloc_scratch(nc, TA)
    R = alloc_rope(nc, TA)
    with ExitStack() as es:
        P = Prog(nc)
        C = setup_consts(P, nc, es, par_d, DEPTH)
        setup_attn_consts(P, nc, C, es)
        P.begin()
        P.dma("sp", xs, xT_in)
        P.end()
        phase_rope(P, nc, C, pos_d, ropec_d, R, TA)
        for l in range(depth):
            if "ffn1" in sel:
                phase_ffn(P, nc, C, xs, Wd["ffn1_w_gate_up"][l], Wd["ffn1_w_down"][l], l, PC_FFN1, TA)
            if "win" in sel:
                phase_win(P, nc, C, xs, Wd["w_in"][l], l, S, TA)
            if "rg" in sel:
                phase_rg(P, nc, C, Wd["rg_w_a"][l], Wd["rg_w_x"][l], l, S, TA)
            if "prep" in sel:
                phase_mlaprep(P, nc, C, Wd["mla_w_uq"][l], Wd["mla_w_ukv"][l], l, S, R, TA)
            if "sb" in sel:
                phase_sb(P, nc, C, S, TA)
            if "mla" in sel:
                phase_mla(P, nc, C, S, TA)
            if "merge" in sel:
                phase_merge(P, nc, C, xs, Wd["w_branch_a"][l], Wd["w_branch_b"][l], Wd["w_branch_c"][l], Wd["w_out"][l], S, TA)
            if "ffn2" in sel:
                phase_ffn(P, nc, C, xs, Wd["ffn2_w_gate_up"][l], Wd["ffn2_w_down"][l], l, PC_FFN2, TA)
        phase_final(P, nc, C, xs, outT, TA)
    return nc


def pack_params(inp):
    par = np.zeros((DEPTH, 128, NPAR), np.float32)
    pm = lambda v: np.asarray(v, np.float32).reshape(-1, 128).T
    for l in range(DEPTH):
        par[l, :, PC_FFN1:PC_FFN1 + 8] = pm(inp["ffn1_norm"][l])
        par[l, :, PC_MIX:PC_MIX + 8] = pm(inp["mix_norm"][l])
        par[l, :, PC_FFN2:PC_FFN2 + 8] = pm(inp["ffn2_norm"][l])
        for tap in range(4):
            par[l, :, PC_CONVW + tap * 8:PC_CONVW + tap * 8 + 8] = pm(inp["conv_w"][l][tap])
        par[l, :, PC_CONVB:PC_CONVB + 8] = pm(inp["conv_b"][l])
        par[l, :, PC_RGBA:PC_RGBA + 8] = pm(inp["rg_b_a"][l])
        par[l, :, PC_RGBX:PC_RGBX + 8] = pm(inp["rg_b_x"][l])
        par[l, :, PC_LAM:PC_LAM + 8] = pm(inp["rg_lambda"][l])
        par[l, :, PC_QN:PC_QN + 2] = pm(inp["mla_q_norm"][l])
        par[l, :, PC_KVN:PC_KVN + 2] = pm(inp["mla_kv_norm"][l])
        par[l, :, PC_FIN:PC_FIN + 8] = pm(inp["final_norm"])
    return par


def rope_consts():
    inv = (np.float32(10000.0) ** (-np.arange(0, 64, 2, dtype=np.float32) / np.float32(64))).astype(np.float32)
    ropec = np.zeros((64, 2), np.float32)
    ropec[:, 0] = np.concatenate([inv, inv])
    ropec[:, 1] = np.concatenate([-np.ones(32, np.float32), np.ones(32, np.float32)])
    return ropec


def kernel(**inputs):
    inp = {k: np.asarray(v) for k, v in inputs.items()}
    x = inp["x"].astype(np.float32, copy=False)
    pos = inp["positions"].astype(np.int32, copy=False)
    par = pack_params(inp)
    ropec = rope_consts()
    W = {n: np.ascontiguousarray(inp[n], dtype=np.float32) for n in WNAMES}
    nc = build_program({n: W[n].shape for n in WNAMES})
    ACTIVE = [0, 1, 4, 5]
    zW = {n: np.zeros_like(W[n]) for n in WNAMES}
    zmap = {"xT_in": np.zeros((D, SEQ), np.float32), "pos": np.zeros((1, SEQ), np.int32),
            "par": np.zeros_like(par), "ropec": ropec}
    zmap.update(zW)
    in_maps = []
    for c in range(8):
        if c in ACTIVE:
            b = ACTIVE.index(c)
            m = {"xT_in": np.ascontiguousarray(x[b].T), "pos": np.ascontiguousarray(pos[b][None, :]),
                 "par": par, "ropec": ropec}
            m.update(W)
            in_maps.append(m)
        else:
            in_maps.append(zmap)
    res = run_bass_kernel_spmd(nc, in_maps, core_ids=list(range(8)))
    out = np.empty((BATCH, SEQ, D), np.float32)
    for b in range(BATCH):
        out[b] = np.asarray(res.results[ACTIVE[b]]["outT"]).T
    return out
```

```python
import numpy as np
import concourse.bass as bass
import concourse.mybir as mybir

F32 = mybir.dt.float32
BF16 = mybir.dt.bfloat16
I32 = mybir.dt.int32
AF = mybir.ActivationFunctionType
ALU = mybir.AluOpType

ENGS = ("sp", "pool", "act", "dve", "pe")
NDMA = 40


class Slot:
    def __init__(self, P, name=""):
        self.name = name
        self.last_w = None
        self.readers = []
        self.ld_sem = None
        self.st_sem = None
        P.slots.append(self)

    def reset(self):
        self.last_w = None
        self.readers = []
        self.ld_sem = None
        self.st_sem = None


class Prog:
    def __init__(self, nc):
        self.nc = nc
        self.slots = []
        self.sets = []
        for i in range(2):
            es = {e: nc.alloc_semaphore(name=f"s{i}_{e}") for e in ENGS}
            ds = [nc.alloc_semaphore(name=f"s{i}_d{j}") for j in range(NDMA)]
            self.sets.append((es, ds))
        self.phase_idx = 0
        self.in_phase = False

    def slot(self, name=""):
        return Slot(self, name)

    def begin(self):
        assert not self.in_phase
        self.in_phase = True
        self.es, ds = self.sets[self.phase_idx % 2]
        self.dma_free = list(ds)
        self.dma_cnt = {}
        self.ops = {e: [] for e in ENGS}
        self.cnt = {e: 0 for e in ENGS}
        self.waited = {e: {} for e in ENGS}
        for s in self.slots:
            s.reset()
        self.misc_sem = self._new_dma_sem()

    def _new_dma_sem(self):
        assert self.dma_free, "out of DMA semaphores"
        s = self.dma_free.pop()
        self.dma_cnt[s] = 0
        return s

    def _deps(self, eng, reads, writes, skip_sem=None):
        toks = []
        for s in reads:
            if s.last_w is not None:
                toks.append(s.last_w)
        for s in writes:
            if s.last_w is not None:
                toks.append(s.last_w)
            toks.extend(s.readers)
        need = {}
        for (sem, val, src) in toks:
            if src == "pe" and eng == "pe":
                continue
            if skip_sem is not None and sem is skip_sem:
                continue
            k = id(sem)
            if self.waited[eng].get(k, 0) >= val:
                continue
            if k not in need or need[k][1] < val:
                need[k] = (sem, val)
        waits = []
        for k, (sem, val) in need.items():
            self.waited[eng][k] = val
            waits.append((sem, val))
        return waits

    def op(self, eng, fn, reads=(), writes=()):
        waits = self._deps(eng, reads, writes)
        self.cnt[eng] += 1
        sem = self.es[eng]
        tok = (sem, self.cnt[eng], eng)
        wset = set(id(s) for s in writes)
        for s in reads:
            if id(s) not in wset:
                s.readers.append(tok)
        for s in writes:
            s.last_w = tok
            s.readers = []
        self.ops[eng].append((waits, fn, sem, 1))

    def set_stage(self, tiles, slots, size):
        self.stg = (tiles, slots, size)
        self.stg_i = 0

    def cast_load(self, out, in_, writes, pbase=0):
        shape = tuple(int(x) for x in out.shape)
        p, free = shape[0], shape[1:]
        n = 1
        for x in free:
            n *= x
        tiles, slots, size = self.stg
        if n > size:
            a = free[0]
            per = n // a
            step = max(1, size // per)
            for a0 in range(0, a, step):
                a1 = min(a, a0 + step)
                self.cast_load(out[:, a0:a1], in_[:, a0:a1], writes, pbase)
            return
        i = self.stg_i % len(tiles)
        self.stg_i += 1
        st = tiles[i][pbase:pbase + p, :n]
        if len(free) == 2:
            st = st.rearrange("p (a b) -> p a b", a=free[0])
        elif len(free) == 3:
            st = st.rearrange("p (a b c) -> p a b c", a=free[0], b=free[1])
        self.dma("sp", st, in_, writes=[slots[i]])
        self.op("pool", lambda e, out=out, st=st: e.tensor_copy(out, st), reads=[slots[i]], writes=list(writes))

    def dma(self, q, out, in_, reads=(), writes=(), part=False, **kw):
        if q == "pool":
            return self.cast_load(out, in_, writes, kw.get("pbase", 0))
        if writes:
            sl = writes[0]
            if sl.ld_sem is None:
                sl.ld_sem = self._new_dma_sem()
            sem = sl.ld_sem
        elif reads:
            sl = reads[0]
            if sl.st_sem is None:
                sl.st_sem = self._new_dma_sem()
            sem = sl.st_sem
        else:
            sem = self.misc_sem
        waits = self._deps(q, reads, writes, skip_sem=(sem if part else None))
        self.dma_cnt[sem] += 16
        tok = (sem, self.dma_cnt[sem], "dma")
        for s in reads:
            s.readers.append(tok)
        for s in writes:
            s.last_w = tok
            s.readers = []

        def fn(e, out=out, in_=in_, kw=kw):
            return e.dma_start(out=out, in_=in_, **kw)
        self.ops[q].append((waits, fn, sem, 16))

    def end(self):
        nc = self.nc
        other_es, other_ds = self.sets[(self.phase_idx + 1) % 2]
        final = [(s, c) for s, c in self.dma_cnt.items() if c > 0]
        with nc.Block() as block:
            decos = {"sp": block.sync, "pool": block.gpsimd, "act": block.scalar,
                     "dve": block.vector, "pe": block.tensor}
            for ename in ENGS:
                ops = self.ops[ename]

                def body(eng, ops=ops, ename=ename):
                    if ename == "pool":
                        for s in list(other_es.values()) + list(other_ds):
                            eng.sem_clear(s)
                    for waits, fn, sem, inc in ops:
                        for wsem, val in waits:
                            eng.wait_ge(wsem, val)
                        ins = fn(eng)
                        ins.then_inc(sem, inc)
                    if ename == "sp":
                        for s, c in final:
                            eng.wait_ge(s, c)
                decos[ename](body)
        self.phase_idx += 1
        self.in_phase = False

from contextlib import ExitStack

D = 1024
KD = 8
DFF = 2816
NFF = 22
EPS = 1e-6

PC_FFN1 = 0
PC_MIX = 8
PC_FFN2 = 16
PC_CONVW = 24
PC_CONVB = 56
PC_RGBA = 64
PC_RGBX = 72
PC_LAM = 80
PC_QN = 88
PC_KVN = 90
PC_FIN = 92
NPAR = 100


_UID = [0]


def mk_sb(nc, es):
    _UID[0] += 1
    u = _UID[0]
    return lambda name, shape, dt: es.enter_context(nc.sbuf_tensor(f"{name}_{u}", shape, dt))


def mk_pt(nc, es):
    _UID[0] += 1
    u = _UID[0]
    return lambda name, shape: es.enter_context(nc.psum_tensor(f"{name}_{u}", shape, F32))


def mk_stage(P, sb, n=3, size=2048):
    P.set_stage([sb(f"stg{i}", [128, size], F32) for i in range(n)], [P.slot() for _ in range(n)], size)


class Ctx:
    pass


def setup_consts(P, nc, es, par_d, L):
    C = Ctx()
    C.ones = es.enter_context(nc.sbuf_tensor("c_ones", [128, 128], BF16))
    C.par = es.enter_context(nc.sbuf_tensor("c_par", [128, L, NPAR], F32))
    C.eps = es.enter_context(nc.sbuf_tensor("c_eps", [128, 2], F32))
    C.s_ones = P.slot("ones")
    C.s_par = P.slot("par")
    C.s_eps = P.slot("eps")
    P.begin()
    P.op("pool", lambda e: e.memset(C.ones[:], 1.0), writes=[C.s_ones])

    def _e(e):
        e.memset(C.eps[:, 0:1], EPS)
        return e.memset(C.eps[:, 1:2], 1.0)
    P.op("pool", _e, writes=[C.s_eps])
    P.dma("sp", C.par[:], par_d.rearrange("l p n -> p l n"), writes=[C.s_par])
    P.end()
    return C


def rmsnorm_tile(P, C, x_ap3, s_x, h_ap3, s_h, gcol_ap, nk, sq, s_sq, ps, s_ps, lnv, s_ln, rstd, s_rstd, n, inv_dim):
    P.op("act", lambda e: e.activation(sq[:, :nk, :n], x_ap3, AF.Square), reads=[s_x], writes=[s_sq])

    def mm(e):
        ins = None
        for k in range(nk):
            ins = e.matmul(ps[:, :n], C.ones[:], sq[:, k, :n], start=(k == 0), stop=(k == nk - 1))
        return ins
    P.op("pe", mm, reads=[s_sq, C.s_ones], writes=[s_ps])
    P.op("act", lambda e: e.activation(lnv[:, :n], ps[:, :n], AF.Ln, bias=C.eps[:, 0:1], scale=inv_dim),
         reads=[s_ps, C.s_eps], writes=[s_ln])
    P.op("act", lambda e: e.activation(rstd[:, :n], lnv[:, :n], AF.Exp, scale=-0.5), reads=[s_ln], writes=[s_rstd])

    def nrm(e):
        ins = None
        for k in range(nk):
            ins = e.scalar_tensor_tensor(h_ap3[:, k, :], x_ap3[:, k, :], gcol_ap(k), rstd[:, :n], ALU.mult, ALU.mult)
        return ins
    P.op("dve", nrm, reads=[s_x, s_rstd, C.s_par], writes=[s_h])


def phase_ffn(P, nc, C, xT, w_gu, w_dn, l, gcol, T):
    TG = min(1024, T)
    NTI = TG // 512
    xT3 = xT.rearrange("(k p) t -> p k t", p=128)
    wgu3 = w_gu.rearrange("(k p) n -> p k n", p=128)
    wdn3 = w_dn.rearrange("(c p) n -> p c n", p=128)
    with ExitStack() as es:
        sb = mk_sb(nc, es)
        pt = mk_pt(nc, es)
        mk_stage(P, sb)
        xg = sb("f_xg", [128, KD, TG], F32)
        hT = sb("f_hT", [128, KD, TG], BF16)
        aT = sb("f_aT", [128, NFF, TG], BF16)
        sq = sb("f_sq", [128, KD, 512], BF16)
        lnv = sb("f_lnv", [128, 512], F32)
        rstd = sb("f_rstd", [128, 512], F32)
        sg = [sb(f"f_sg{i}", [128, 512], F32) for i in range(2)]
        NWB = 3
        wgu = [sb(f"f_wgu{i}", [128, KD, 2, 128], BF16) for i in range(NWB)]
        wd = [sb(f"f_wd{i}", [128, NFF, 128], BF16) for i in range(2)]
        pss = pt("f_pss", [128, 512])
        pg = [pt(f"f_pg{i}", [128, 512]) for i in range(2)]
        pu = [pt(f"f_pu{i}", [128, 512]) for i in range(2)]
        po = [pt(f"f_po{i}", [128, 512]) for i in range(2)]
        s_xg = [P.slot() for _ in range(NTI)]
        s_hT = [P.slot() for _ in range(NTI)]
        s_aT = [P.slot() for _ in range(NTI)]
        s_sq, s_ln, s_rstd, s_pss = P.slot(), P.slot(), P.slot(), P.slot()
        s_sg = [P.slot() for _ in range(2)]
        s_wgu = [P.slot() for _ in range(NWB)]
        s_wd = [P.slot() for _ in range(2)]
        s_pg = [P.slot() for _ in range(2)]
        s_pu = [P.slot() for _ in range(2)]
        s_po = [P.slot() for _ in range(2)]
        P.begin()
        it = 0
        wi = 0
        di = 0
        for g in range(T // TG):
            for ti in range(NTI):
                t0 = g * TG + ti * 512
                ts = slice(ti * 512, (ti + 1) * 512)
                P.dma("sp", xg[:, :, ts], xT3[:, :, t0:t0 + 512], writes=[s_xg[ti]])
                rmsnorm_tile(P, C, xg[:, :, ts], s_xg[ti], hT[:, :, ts], s_hT[ti],
                             lambda k: C.par[:, l, gcol + k:gcol + k + 1], KD,
                             sq, s_sq, pss, s_pss, lnv, s_ln, rstd, s_rstd, 512, 1.0 / D)
            for c in range(NFF):
                b = wi % NWB
                wi += 1
                P.dma("pool", wgu[b][:, :, 0, :], wgu3[:, :, c * 128:(c + 1) * 128], writes=[s_wgu[b]])
                P.dma("pool", wgu[b][:, :, 1, :], wgu3[:, :, DFF + c * 128:DFF + (c + 1) * 128],
                      writes=[s_wgu[b]], part=True)
                for ti in range(NTI):
                    ts = slice(ti * 512, (ti + 1) * 512)
                    j = it % 2
                    it += 1

                    def mm(e, b=b, ts=ts, j=j):
                        ins = None
                        for k in range(KD):
                            ins = e.matmul(pg[j][:], wgu[b][:, k, 0, :], hT[:, k, ts], start=(k == 0), stop=(k == KD - 1))
                        for k in range(KD):
                            ins = e.matmul(pu[j][:], wgu[b][:, k, 1, :], hT[:, k, ts], start=(k == 0), stop=(k == KD - 1))
                        return ins
                    P.op("pe", mm, reads=[s_wgu[b], s_hT[ti]], writes=[s_pg[j], s_pu[j]])
                    P.op("act", lambda e, j=j: e.activation(sg[j][:], pg[j][:], AF.Silu), reads=[s_pg[j]], writes=[s_sg[j]])
                    P.op("dve", lambda e, j=j, c=c, ts=ts: e.tensor_tensor(aT[:, c, ts], sg[j][:], pu[j][:], ALU.mult),
                         reads=[s_sg[j], s_pu[j]], writes=[s_aT[ti]])
            for oc in range(KD):
                b = di % 2
                P.dma("pool", wd[b][:], wdn3[:, :, oc * 128:(oc + 1) * 128], writes=[s_wd[b]])
                for ti in range(NTI):
                    ts = slice(ti * 512, (ti + 1) * 512)
                    j = di % 2
                    di2 = (di * NTI + ti) % 2

                    def mm2(e, b=b, ts=ts, j=di2):
                        ins = None
                        for c in range(NFF):
                            ins = e.matmul(po[j][:], wd[b][:, c, :], aT[:, c, ts], start=(c == 0), stop=(c == NFF - 1))
                        return ins
                    P.op("pe", mm2, reads=[s_wd[b], s_aT[ti]], writes=[s_po[di2]])
                    P.op("dve", lambda e, j=di2, oc=oc, ts=ts: e.scalar_tensor_tensor(
                        xg[:, oc, ts], po[j][:], 0.5, xg[:, oc, ts], ALU.mult, ALU.add),
                        reads=[s_po[di2], s_xg[ti]], writes=[s_xg[ti]])
                di += 1
            for ti in range(NTI):
                t0 = g * TG + ti * 512
                ts = slice(ti * 512, (ti + 1) * 512)
                P.dma("sp", xT3[:, :, t0:t0 + 512], xg[:, :, ts], reads=[s_xg[ti]])
        P.end()


SBW = 1024
OFF_RGX = 0
OFF_RGG = 1024
OFF_Q = 2048
OFF_K = 3072
OFF_V = 4096
OFF_CQ = 5120
OFF_CKV = 5376
OFF_KR = 5632
OFF_GATE = 5696
NIN = 8768


def phase_win(P, nc, C, xT, w_in, l, S, T, tok0=0):
    TG = min(2048, T)
    NTI = TG // 512
    xT3 = xT.rearrange("(k p) t -> p k t", p=128)
    w3 = w_in.rearrange("(k p) n -> p k n", p=128)
    chunks = []
    for c in range(8):
        chunks.append(([(OFF_RGX + c * 128, 128)], "copy", S["rgx"], c * 128, 128))
    for c in range(8):
        chunks.append(([(OFF_Q + c * 128, 128)], "qscale", S["qT"], c * 128, 128))
    for c in range(8):
        chunks.append(([(OFF_K + c * 128, 128)], "copy", S["kT"], c * 128, 128))
    for c in range(2):
        chunks.append(([(OFF_CQ + c * 128, 128)], "copy", S["cq"], c * 128, 128))
    for c in range(2):
        chunks.append(([(OFF_CKV + c * 128, 128)], "copy", S["ckv"], c * 128, 128))
    chunks.append(([(OFF_KR, 64)], "copy", S["kr"], 0, 64))
    chunks.append(([(OFF_KR + 32, 32), (OFF_KR, 32)], "copy", S["krsw"], 0, 64))
    for c in range(8):
        chunks.append(([(OFF_RGG + c * 128, 128)], "gelu", S["rgg"], c * 128, 128))
    for c in range(24):
        chunks.append(([(OFF_GATE + c * 128, 128)], "sigmoid", S["gate"], c * 128, 128))
    with ExitStack() as es:
        sb = mk_sb(nc, es)
        pt = mk_pt(nc, es)
        mk_stage(P, sb)
        xg = [sb(f"w_xg{i}", [128, KD, 512], F32) for i in range(2)]
        hT = sb("w_hT", [128, KD, TG], BF16)
        sq = sb("w_sq", [128, KD, 512], BF16)
        lnv = sb("w_lnv", [128, 512], F32)
        rstd = sb("w_rstd", [128, 512], F32)
        NWB = 3
        wt = [sb(f"w_wt{i}", [128, KD, 128], BF16) for i in range(NWB)]
        wv = [sb(f"w_wv{i}", [128, KD, 512], BF16) for i in range(2)]
        NST = 3
        stf = [sb(f"w_stf{i}", [128, TG], F32) for i in range(NST)]
        stb = [sb(f"w_stb{i}", [128, TG], BF16) for i in range(NST)]
        stv = [sb(f"w_stv{i}", [128, 1024], BF16) for i in range(NST)]
        s_stv = [P.slot() for _ in range(NST)]
        pss = pt("w_pss", [128, 512])
        NPW = 4
        pw = [pt(f"w_pw{i}", [128, 512]) for i in range(NPW)]
        s_xg = [P.slot() for _ in range(2)]
        s_hT = [P.slot() for _ in range(NTI)]
        s_sq, s_ln, s_rstd, s_pss = P.slot(), P.slot(), P.slot(), P.slot()
        s_wt = [P.slot() for _ in range(NWB)]
        s_wv = [P.slot() for _ in range(2)]
        s_stf = [P.slot() for _ in range(NST)]
        s_stb = [P.slot() for _ in range(NST)]
        s_pw = [P.slot() for _ in range(NPW)]
        P.begin()
        wi = 0
        pi = 0
        si = 0
        xi = 0
        for g in range(T // TG):
            for ti in range(NTI):
                t0 = g * TG + ti * 512
                ts = slice(ti * 512, (ti + 1) * 512)
                xb = xi % 2
                xi += 1
                P.dma("sp", xg[xb][:], xT3[:, :, t0:t0 + 512], writes=[s_xg[xb]])
                rmsnorm_tile(P, C, xg[xb][:], s_xg[xb], hT[:, :, ts], s_hT[ti],
                             lambda k: C.par[:, l, PC_MIX + k:PC_MIX + k + 1], KD,
                             sq, s_sq, pss, s_pss, lnv, s_ln, rstd, s_rstd, 512, 1.0 / D)
            def load_w(ci):
                b_ = ci % NWB
                o = 0
                for (c0, w) in chunks[ci][0]:
                    P.dma("pool", wt[b_][:, :, o:o + w], w3[:, :, c0:c0 + w], writes=[s_wt[b_]])
                    o += w
            PF = NWB - 1
            for ci in range(min(PF, len(chunks))):
                load_w(ci)
            for ci, (pieces, evac, dst, row0, M) in enumerate(chunks):
                b = ci % NWB
                if ci + PF < len(chunks):
                    load_w(ci + PF)
                if ci == len(chunks) - 4:
                    for ch in range(2):
                        P.dma("pool", wv[ch][:], w3[:, :, OFF_V + ch * 512:OFF_V + (ch + 1) * 512], writes=[s_wv[ch]])
                sj = si % NST
                si += 1
                tg0 = tok0 + g * TG
                isb = (evac in ("copy", "qscale")) and dst.dtype == BF16
                for ti in range(NTI):
                    ts = slice(ti * 512, (ti + 1) * 512)
                    j = pi % NPW
                    pi += 1

                    def mm(e, b=b, ts=ts, j=j, M=M):
                        ins = None
                        for k in range(KD):
                            ins = e.matmul(pw[j][:M, :], wt[b][:, k, :M], hT[:, k, ts], start=(k == 0), stop=(k == KD - 1))
                        return ins
                    P.op("pe", mm, reads=[s_wt[b], s_hT[ti]], writes=[s_pw[j]])
                    if evac == "copy":
                        if isb:
                            P.op("dve", lambda e, j=j, sj=sj, M=M, ts=ts: e.tensor_copy(stb[sj][:M, ts], pw[j][:M, :]),
                                 reads=[s_pw[j]], writes=[s_stb[sj]])
                        else:
                            P.op("dve", lambda e, j=j, sj=sj, M=M, ts=ts: e.tensor_copy(stf[sj][:M, ts], pw[j][:M, :]),
                                 reads=[s_pw[j]], writes=[s_stf[sj]])
                    elif evac == "qscale":
                        P.op("dve", lambda e, j=j, sj=sj, M=M, ts=ts: e.tensor_scalar(stb[sj][:M, ts], pw[j][:M, :], 128 ** -0.5, None, ALU.mult),
                             reads=[s_pw[j]], writes=[s_stb[sj]])
                    else:
                        fn = AF.Gelu if evac == "gelu" else AF.Sigmoid
                        P.op("act", lambda e, j=j, sj=sj, M=M, fn=fn, ts=ts: e.activation(stf[sj][:M, ts], pw[j][:M, :], fn),
                             reads=[s_pw[j]], writes=[s_stf[sj]])
                if isb:
                    P.dma("sp", dst[row0:row0 + M, tg0:tg0 + TG], stb[sj][:M, :], reads=[s_stb[sj]])
                else:
                    P.dma("sp", dst[row0:row0 + M, tg0:tg0 + TG], stf[sj][:M, :], reads=[s_stf[sj]])
            for tb in range(TG // 128):
                t0 = tok0 + g * TG + tb * 128
                ti = tb // 4
                sj = si % NST
                si += 1
                for ch in range(2):
                    j = pi % NPW
                    pi += 1

                    def mmv(e, b=ch, tb=tb, j=j):
                        ins = None
                        for k in range(KD):
                            ins = e.matmul(pw[j][:], hT[:, k, tb * 128:(tb + 1) * 128], wv[b][:, k, :], start=(k == 0), stop=(k == KD - 1))
                        return ins
                    P.op("pe", mmv, reads=[s_wv[ch], s_hT[ti]], writes=[s_pw[j]])
                    P.op("dve", lambda e, j=j, sj=sj, ch=ch: e.tensor_copy(stv[sj][:, ch * 512:(ch + 1) * 512], pw[j][:]),
                         reads=[s_pw[j]], writes=[s_stv[sj]])
                P.dma("sp", S["v"][t0:t0 + 128, :], stv[sj][:], reads=[s_stv[sj]])
        P.end()


def alloc_scratch(nc, TA, pfx=""):
    S = {}
    S["rgx"] = nc.dram_tensor(pfx + "s_rgx", [1024, TA], F32).ap()
    S["rgg"] = nc.dram_tensor(pfx + "s_rgg", [1024, TA], F32).ap()
    S["qT"] = nc.dram_tensor(pfx + "s_qT", [1024, TA], BF16).ap()
    S["kT"] = nc.dram_tensor(pfx + "s_kT", [1024, TA], BF16).ap()
    S["v"] = nc.dram_tensor(pfx + "s_v", [TA, 1024], BF16).ap()
    S["cq"] = nc.dram_tensor(pfx + "s_cq", [256, TA], F32).ap()
    S["ckv"] = nc.dram_tensor(pfx + "s_ckv", [256, TA], F32).ap()
    S["kr"] = nc.dram_tensor(pfx + "s_kr", [64, TA], F32).ap()
    S["krsw"] = nc.dram_tensor(pfx + "s_krsw", [64, TA], F32).ap()
    S["gate"] = nc.dram_tensor(pfx + "s_gate", [3072, TA], F32).ap()
    S["yaT"] = nc.dram_tensor(pfx + "s_yaT", [1024, TA], BF16).ap()
    S["ybT"] = nc.dram_tensor(pfx + "s_ybT", [1024, TA], BF16).ap()
    S["ycT"] = nc.dram_tensor(pfx + "s_ycT", [1024, TA], BF16).ap()
    S["mqn"] = nc.dram_tensor(pfx + "s_mqn", [1024, TA], BF16).ap()
    S["mqr"] = nc.dram_tensor(pfx + "s_mqr", [512, TA], BF16).ap()
    S["mkn"] = nc.dram_tensor(pfx + "s_mkn", [1024, TA], BF16).ap()
    S["mkr"] = nc.dram_tensor(pfx + "s_mkr", [64, TA], BF16).ap()
    S["mv"] = nc.dram_tensor(pfx + "s_mv", [TA, 1024], BF16).ap()
    return S


def phase_rg(P, nc, C, rg_wa, rg_wx, l, S, TA, chunks=range(8)):
    NS = min(2048, TA)
    with ExitStack() as es:
        sb = mk_sb(nc, es)
        pt = mk_pt(nc, es)
        mk_stage(P, sb)
        wab = sb("r_wab", [128, 8, 2, 128], BF16)
        cc_t = sb("r_c", [128, 8, 4], F32)
        xin = [sb(f"r_xin{i}", [128, NS + 3], F32) for i in range(2)]
        gg = [sb(f"r_gg{i}", [128, NS], F32) for i in range(3)]
        u2 = [sb(f"r_u{i}", [128, NS], F32) for i in range(2)]
        ubf2 = [sb(f"r_ubf{i}", [128, NS], BF16) for i in range(2)]
        r2_ = [sb(f"r_r{i}", [128, NS], F32) for i in range(2)]
        i2_ = [sb(f"r_i{i}", [128, NS], F32) for i in range(2)]
        a2_ = [sb(f"r_a{i}", [128, NS], F32) for i in range(2)]
        m2_ = [sb(f"r_m{i}", [128, NS], F32) for i in range(2)]
        h_t = sb("r_h", [128, NS], F32)
        y_t = [sb(f"r_y{i}", [128, NS], BF16) for i in range(2)]
        hl = sb("r_hl", [128, 2], F32)
        pa = [pt(f"r_pa{i}", [128, 512]) for i in range(2)]
        px = [pt(f"r_px{i}", [128, 512]) for i in range(2)]
        s_wab, s_c = P.slot(), P.slot()
        s_xin = [P.slot() for _ in range(2)]
        s_gg = [P.slot() for _ in range(3)]
        s_u2, s_ubf2, s_r2, s_i2, s_a2, s_m2 = [[P.slot() for _ in range(2)] for _ in range(6)]
        s_h, s_hl = P.slot(), P.slot()
        s_y = [P.slot() for _ in range(2)]
        s_pa = [P.slot() for _ in range(2)]
        s_px = [P.slot() for _ in range(2)]
        par = C.par
        P.begin()
        P.op("pool", lambda e: e.memset(wab[:], 0.0), writes=[s_wab])
        for gi, wsrc in enumerate((rg_wa, rg_wx)):
            wr = wsrc.rearrange("(c two) j k -> two j c k", two=2)
            for half in range(2):
                P.dma("pool", wab[half * 64:(half + 1) * 64, :, gi, half * 64:(half + 1) * 64], wr[half],
                      writes=[s_wab], pbase=half * 64)
        P.op("act", lambda e: e.activation(cc_t[:, :, 2], par[:, l, PC_LAM:PC_LAM + 8], AF.Exp, scale=-1.0),
             reads=[C.s_par], writes=[s_c])
        P.op("act", lambda e: e.activation(cc_t[:, :, 3], cc_t[:, :, 2], AF.Ln, bias=C.eps[:, 1:2]),
             reads=[s_c, C.s_eps], writes=[s_c])

        def cfin(e):
            e.tensor_scalar(cc_t[:, :, 0], cc_t[:, :, 3], -8.0, None, ALU.mult)
            return e.tensor_scalar(cc_t[:, :, 1], cc_t[:, :, 3], -16.0, None, ALU.mult)
        P.op("dve", cfin, reads=[s_c], writes=[s_c])
        pi = 0
        items = [(cc, sg_) for cc in chunks for sg_ in range(TA // NS)]

        def load_in(ii):
            cc_, sg2 = items[ii]
            rows_ = slice(cc_ * 128, (cc_ + 1) * 128)
            t0_ = sg2 * NS
            b_ = ii % 2
            if t0_ == 0:
                P.op("pool", lambda e, b_=b_: e.memset(xin[b_][:, 0:3], 0.0), writes=[s_xin[b_]])
                P.dma("sp", xin[b_][:, 3:], S["rgx"][rows_, 0:NS], writes=[s_xin[b_]])
            else:
                P.dma("sp", xin[b_][:], S["rgx"][rows_, t0_ - 3:t0_ + NS], writes=[s_xin[b_]])
            P.dma("sp", gg[ii % 3][:], S["rgg"][rows_, t0_:t0_ + NS], writes=[s_gg[ii % 3]])
        load_in(0)
        pi_box = [0]

        def stage_a(ii):
            cc, sg_ = items[ii]
            b = ii % 2
            u, ubf, r_t, i_t, a_t, m_t = u2[b], ubf2[b], r2_[b], i2_[b], a2_[b], m2_[b]
            s_u, s_ubf, s_r, s_i, s_a, s_m = s_u2[b], s_ubf2[b], s_r2[b], s_i2[b], s_a2[b], s_m2[b]
            if ii + 1 < len(items):
                load_in(ii + 1)
            cw = lambda tap: par[:, l, PC_CONVW + tap * 8 + cc:PC_CONVW + tap * 8 + cc + 1]
            P.op("dve", lambda e: e.tensor_scalar(
                u[:], xin[b][:, 0:NS], cw(0), par[:, l, PC_CONVB + cc:PC_CONVB + cc + 1], ALU.mult, ALU.add),
                reads=[s_xin[b], C.s_par], writes=[s_u])
            for tap in range(1, 4):
                P.op("dve", lambda e, tap=tap: e.scalar_tensor_tensor(
                    u[:], xin[b][:, tap:tap + NS], cw(tap), u[:], ALU.mult, ALU.add),
                    reads=[s_xin[b], C.s_par, s_u], writes=[s_u])
            P.op("pool", lambda e: e.tensor_copy(ubf[:], u[:]), reads=[s_u], writes=[s_ubf])
            for blk in range(NS // 512):
                j = pi_box[0] % 2
                pi_box[0] += 1
                bs = slice(blk * 512, (blk + 1) * 512)

                def mm(e, j=j, bs=bs):
                    e.matmul(pa[j][:], wab[:, cc, 0, :], ubf[:, bs], start=True, stop=True)
                    return e.matmul(px[j][:], wab[:, cc, 1, :], ubf[:, bs], start=True, stop=True)
                P.op("pe", mm, reads=[s_wab, s_ubf], writes=[s_pa[j], s_px[j]])

                def sig(e, j=j, bs=bs):
                    e.activation(r_t[:, bs], pa[j][:], AF.Sigmoid, bias=par[:, l, PC_RGBA + cc:PC_RGBA + cc + 1])
                    return e.activation(i_t[:, bs], px[j][:], AF.Sigmoid, bias=par[:, l, PC_RGBX + cc:PC_RGBX + cc + 1])
                P.op("act", sig, reads=[s_pa[j], s_px[j], C.s_par], writes=[s_r, s_i])

            def aexp(e):
                e.activation(a_t[:], r_t[:], AF.Exp, scale=cc_t[:, cc, 0:1])
                return e.activation(m_t[:], r_t[:], AF.Exp, scale=cc_t[:, cc, 1:2])
            P.op("act", aexp, reads=[s_r, s_c], writes=[s_a, s_m])
            P.op("act", lambda e: e.activation(m_t[:], m_t[:], AF.Sqrt, bias=C.eps[:, 1:2], scale=-1.0),
                 reads=[s_m, C.s_eps], writes=[s_m])
            P.op("pool", lambda e: e.tensor_tensor(i_t[:], i_t[:], m_t[:], ALU.mult), reads=[s_i, s_m], writes=[s_i])

        def stage_b(ii):
            cc, sg_ = items[ii]
            b = ii % 2
            rows = slice(cc * 128, (cc + 1) * 128)
            t0 = sg_ * NS
            u, i_t, a_t = u2[b], i2_[b], a2_[b]
            s_u, s_i, s_a = s_u2[b], s_i2[b], s_a2[b]
            P.op("dve", lambda e: e.tensor_tensor(i_t[:], i_t[:], u[:], ALU.mult), reads=[s_i, s_u], writes=[s_i])
            if t0 == 0:
                P.op("dve", lambda e: e.tensor_tensor_scan(h_t[:], a_t[:], i_t[:], 0.0, ALU.mult, ALU.add),
                     reads=[s_a, s_i], writes=[s_h])
            else:
                P.op("dve", lambda e: e.tensor_tensor_scan(h_t[:], a_t[:], i_t[:], hl[:, 0:1], ALU.mult, ALU.add),
                     reads=[s_a, s_i, s_hl], writes=[s_h])
            P.op("dve", lambda e: e.tensor_copy(hl[:, 0:1], h_t[:, NS - 1:NS]), reads=[s_h], writes=[s_hl])
            P.op("pool", lambda e: e.tensor_tensor(y_t[b][:], h_t[:], gg[ii % 3][:], ALU.mult),
                 reads=[s_h, s_gg[ii % 3]], writes=[s_y[b]])
            P.dma("sp", S["yaT"][rows, t0:t0 + NS], y_t[b][:], reads=[s_y[b]])

        stage_a(0)
        for ii in range(len(items)):
            if ii + 1 < len(items):
                stage_a(ii + 1)
            stage_b(ii)
        P.end()


import math
TWO_PI = 2.0 * math.pi
CW1 = 6.28125
CW2 = TWO_PI - CW1
MLA_SCALE = 192 ** -0.5


def phase_rope(P, nc, C, pos_d, ropec_d, R, TA):
    N = min(2048, TA)
    with ExitStack() as es:
        sb = mk_sb(nc, es)
        rc = sb("rp_c", [64, 2], F32)
        pi_ = sb("rp_pi", [64, N], I32)
        ang = sb("rp_ang", [64, N], F32)
        kf = sb("rp_kf", [64, N], F32)
        ki = sb("rp_ki", [64, N], I32)
        r = sb("rp_r", [64, N], F32)
        m = sb("rp_m", [64, N], F32)
        o = [sb(f"rp_o{i}", [64, N], F32) for i in range(4)]
        s_rc, s_pi, s_w = P.slot(), P.slot(), P.slot()
        s_o = [P.slot() for _ in range(4)]
        P.begin()
        P.dma("sp", rc[:], ropec_d, writes=[s_rc])
        for sg_ in range(TA // N):
            t0 = sg_ * N
            P.dma("sp", pi_[:], pos_d[0:1, t0:t0 + N].broadcast_to([64, N]), writes=[s_pi])

            def D1(fn, reads=(), extra_w=()):
                P.op("dve", fn, reads=[s_w] + list(reads), writes=[s_w] + list(extra_w))

            def wrap():
                D1(lambda e: e.tensor_scalar(m[:], r[:], math.pi, -TWO_PI, ALU.is_gt, ALU.mult))
                D1(lambda e: e.tensor_tensor(r[:], r[:], m[:], ALU.add))
                D1(lambda e: e.tensor_scalar(m[:], r[:], -math.pi, TWO_PI, ALU.is_lt, ALU.mult))
                D1(lambda e: e.tensor_tensor(r[:], r[:], m[:], ALU.add))

            D1(lambda e: e.tensor_copy(ang[:], pi_[:]), reads=[s_pi])
            D1(lambda e: e.tensor_scalar(ang[:], ang[:], rc[:, 0:1], None, ALU.mult), reads=[s_rc])
            D1(lambda e: e.tensor_scalar(ki[:], ang[:], 1.0 / TWO_PI, None, ALU.mult))
            D1(lambda e: e.tensor_copy(kf[:], ki[:]))
            D1(lambda e: e.scalar_tensor_tensor(r[:], kf[:], -CW1, ang[:], ALU.mult, ALU.add), reads=[s_o[0], s_o[1]])
            D1(lambda e: e.scalar_tensor_tensor(r[:], kf[:], -CW2, r[:], ALU.mult, ALU.add))
            wrap()
            P.op("act", lambda e: e.activation(o[1][:], r[:], AF.Sin), reads=[s_w], writes=[s_o[1]])
            D1(lambda e: e.tensor_scalar(r[:], r[:], math.pi / 2, None, ALU.add), reads=[s_o[1]])
            wrap()
            P.op("act", lambda e: e.activation(o[0][:], r[:], AF.Sin), reads=[s_w], writes=[s_o[0]])
            P.op("dve", lambda e: e.tensor_scalar(o[1][:], o[1][:], rc[:, 1:2], None, ALU.mult), reads=[s_o[1], s_rc], writes=[s_o[1]])
            P.op("dve", lambda e: e.tensor_scalar(o[2][:], o[0][:], MLA_SCALE, None, ALU.mult), reads=[s_o[0]], writes=[s_o[2]])
            P.op("dve", lambda e: e.tensor_scalar(o[3][:], o[1][:], MLA_SCALE, None, ALU.mult), reads=[s_o[1]], writes=[s_o[3]])
            for i, nm in enumerate(("cos", "sin", "cosq", "sinq")):
                P.dma("sp", R[nm][:, t0:t0 + N], o[i][:], reads=[s_o[i]])
        P.end()


def alloc_rope(nc, TA):
    return {nm: nc.dram_tensor("r_" + nm, [64, TA], F32).ap() for nm in ("cos", "sin", "cosq", "sinq")}


def phase_mlaprep(P, nc, C, w_uq, w_ukv, l, S, R, TA, heads=range(8)):
    heads = list(heads)
    NH = len(heads)
    wq3 = w_uq.rearrange("(k p) n -> p k n", p=128)
    wq4 = w_uq.rearrange("(k p) (h c) -> p k h c", p=128, c=192)
    wkv4 = w_ukv.rearrange("(k p) (h c) -> p k h c", p=128, c=256)
    with ExitStack() as es:
        sb = mk_sb(nc, es)
        pt = mk_pt(nc, es)
        mk_stage(P, sb)
        wq = sb("m_wq", [128, 2, 8, 192], BF16)
        wqs = sb("m_wqs", [128, 2, 8, 64], BF16)
        wkv = sb("m_wkv", [128, 2, 8, 256], BF16)
        wv = sb("m_wv", [128, 2, NH, 128], BF16)
        cin = [sb(f"m_cin{i}", [128, 2, 512], F32) for i in range(2)]
        cn = [sb(f"m_cn{i}", [128, 2, 512], BF16) for i in range(2)]
        sq = sb("m_sq", [128, 2, 512], BF16)
        lnv = sb("m_lnv", [128, 512], F32)
        rstd = sb("m_rstd", [128, 512], F32)
        tab = [sb(f"m_tab{i}", [64, 512], F32) for i in range(4)]
        krt = [sb(f"m_kr{i}", [64, 512], F32) for i in range(2)]
        t1 = sb("m_t1", [64, 512], F32)
        t2 = sb("m_t2", [64, 512], F32)
        NST = 4
        stb = [sb(f"m_stb{i}", [128, 512], BF16) for i in range(NST)]
        pss = pt("m_pss", [128, 512])
        NPW = 4
        pw = [pt(f"m_pw{i}", [128, 512]) for i in range(NPW)]
        pr = [pt(f"m_pr{i}", [64, 512]) for i in range(2)]
        s_w = P.slot()
        s_cin = [P.slot() for _ in range(2)]
        s_cn = [P.slot() for _ in range(2)]
        s_sq, s_ln, s_rstd, s_pss = P.slot(), P.slot(), P.slot(), P.slot()
        s_tab = [P.slot() for _ in range(4)]
        s_kr = [P.slot() for _ in range(2)]
        s_t1, s_t2 = P.slot(), P.slot()
        s_stb = [P.slot() for _ in range(NST)]
        s_pw = [P.slot() for _ in range(NPW)]
        s_pr = [P.slot() for _ in range(2)]
        P.begin()
        for k in range(2):
            P.dma("pool", wq[:, k], wq4[:, k], writes=[s_w], part=(k > 0))
        for k in range(2):
            P.dma("pool", wqs[:, k, :, 0:32], wq4[:, k, :, 160:192], writes=[s_w], part=True)
            P.dma("pool", wqs[:, k, :, 32:64], wq4[:, k, :, 128:160], writes=[s_w], part=True)
        for k in range(2):
            P.dma("pool", wkv[:, k], wkv4[:, k], writes=[s_w], part=True)
        for hi, h in enumerate(heads):
            P.dma("pool", wv[:, :, hi, :], wkv4[:, :, h, 128:256], writes=[s_w], part=True)
        pi = 0
        si = 0

        def evac_store(src_ap, s_src, M, dst_ap, scale=None):
            nonlocal si
            sj = si % NST
            si += 1
            if scale is None:
                P.op("dve", lambda e: e.tensor_copy(stb[sj][:M, :], src_ap), reads=[s_src], writes=[s_stb[sj]])
            else:
                P.op("dve", lambda e: e.tensor_scalar(stb[sj][:M, :], src_ap, scale, None, ALU.mult), reads=[s_src], writes=[s_stb[sj]])
            P.dma("sp", dst_ap, stb[sj][:M, :], reads=[s_stb[sj]])

        for tix in range(TA // 512):
            t0 = tix * 512
            tsl = slice(t0, t0 + 512)
            P.dma("sp", cin[0][:], S["cq"].rearrange("(k p) t -> p k t", p=128)[:, :, tsl], writes=[s_cin[0]])
            P.dma("sp", cin[1][:], S["ckv"].rearrange("(k p) t -> p k t", p=128)[:, :, tsl], writes=[s_cin[1]])
            for i, nm in enumerate(("cos", "sin", "cosq", "sinq")):
                P.dma("sp", tab[i][:], R[nm][:, tsl], writes=[s_tab[i]])
            P.dma("sp", krt[0][:], S["kr"][:, tsl], writes=[s_kr[0]])
            P.dma("sp", krt[1][:], S["krsw"][:, tsl], writes=[s_kr[1]])
            for i, pc in enumerate((PC_QN, PC_KVN)):
                rmsnorm_tile(P, C, cin[i][:], s_cin[i], cn[i][:], s_cn[i],
                             lambda k, pc=pc: C.par[:, l, pc + k:pc + k + 1], 2,
                             sq, s_sq, pss, s_pss, lnv, s_ln, rstd, s_rstd, 512, 1.0 / 256)
            P.op("dve", lambda e: e.tensor_tensor(t1[:], krt[0][:], tab[0][:], ALU.mult), reads=[s_kr[0], s_tab[0]], writes=[s_t1])
            P.op("dve", lambda e: e.tensor_tensor(t2[:], krt[1][:], tab[1][:], ALU.mult), reads=[s_kr[1], s_tab[1]], writes=[s_t2])
            sj = si % NST
            si += 1
            P.op("dve", lambda e, sj=sj: e.tensor_tensor(stb[sj][:64, :], t1[:], t2[:], ALU.add), reads=[s_t1, s_t2], writes=[s_stb[sj]])
            P.dma("sp", S["mkr"][:, tsl], stb[sj][:64, :], reads=[s_stb[sj]])
            for hi, h in enumerate(heads):
                j = pi % NPW
                pi += 1

                def mm(e, j=j, h=h):
                    e.matmul(pw[j][:], wq[:, 0, h, 0:128], cn[0][:, 0, :], start=True, stop=False)
                    return e.matmul(pw[j][:], wq[:, 1, h, 0:128], cn[0][:, 1, :], start=False, stop=True)
                P.op("pe", mm, reads=[s_w, s_cn[0]], writes=[s_pw[j]])
                evac_store(pw[j][:], s_pw[j], 128, S["mqn"][h * 128:(h + 1) * 128, tsl], scale=MLA_SCALE)
                def mmr(e, h=h):
                    e.matmul(pr[0][:], wq[:, 0, h, 128:192], cn[0][:, 0, :], start=True, stop=False)
                    e.matmul(pr[0][:], wq[:, 1, h, 128:192], cn[0][:, 1, :], start=False, stop=True)
                    e.matmul(pr[1][:], wqs[:, 0, h, :], cn[0][:, 0, :], start=True, stop=False)
                    return e.matmul(pr[1][:], wqs[:, 1, h, :], cn[0][:, 1, :], start=False, stop=True)
                P.op("pe", mmr, reads=[s_w, s_cn[0]], writes=[s_pr[0], s_pr[1]])
                P.op("dve", lambda e: e.tensor_tensor(t1[:], pr[0][:], tab[2][:], ALU.mult), reads=[s_pr[0], s_tab[2]], writes=[s_t1])
                P.op("dve", lambda e: e.tensor_tensor(t2[:], pr[1][:], tab[3][:], ALU.mult), reads=[s_pr[1], s_tab[3]], writes=[s_t2])
                sj = si % NST
                si += 1
                P.op("dve", lambda e, sj=sj: e.tensor_tensor(stb[sj][:64, :], t1[:], t2[:], ALU.add), reads=[s_t1, s_t2], writes=[s_stb[sj]])
                P.dma("sp", S["mqr"][h * 64:(h + 1) * 64, tsl], stb[sj][:64, :], reads=[s_stb[sj]])
                j = pi % NPW
                pi += 1

                def mmk(e, j=j, h=h):
                    e.matmul(pw[j][:], wkv[:, 0, h, 0:128], cn[1][:, 0, :], start=True, stop=False)
                    return e.matmul(pw[j][:], wkv[:, 1, h, 0:128], cn[1][:, 1, :], start=False, stop=True)
                P.op("pe", mmk, reads=[s_w, s_cn[1]], writes=[s_pw[j]])
                evac_store(pw[j][:], s_pw[j], 128, S["mkn"][h * 128:(h + 1) * 128, tsl])
            for tb in range(4):
                for c0 in range(0, NH * 128, 512):
                    n = min(512, NH * 128 - c0)
                    j = pi % NPW
                    pi += 1

                    def mmv(e, j=j, tb=tb, c0=c0, n=n):
                        wv2 = wv[:].rearrange("p k h c -> p k (h c)")
                        e.matmul(pw[j][:, :n], cn[1][:, 0, tb * 128:(tb + 1) * 128], wv2[:, 0, c0:c0 + n], start=True, stop=False)
                        return e.matmul(pw[j][:, :n], cn[1][:, 1, tb * 128:(tb + 1) * 128], wv2[:, 1, c0:c0 + n], start=False, stop=True)
                    P.op("pe", mmv, reads=[s_w, s_cn[1]], writes=[s_pw[j]])
                    sj = si % NST
                    si += 1
                    P.op("dve", lambda e, j=j, sj=sj, n=n: e.tensor_copy(stb[sj][:, :n], pw[j][:, :n]), reads=[s_pw[j]], writes=[s_stb[sj]])
                    col0 = heads[0] * 128 + c0
                    P.dma("sp", S["mv"][t0 + tb * 128:t0 + (tb + 1) * 128, col0:col0 + n], stb[sj][:, :n], reads=[s_stb[sj]])
        P.end()


def setup_attn_consts(P, nc, C, es):
    C.negtri = es.enter_context(nc.sbuf_tensor("c_negtri", [128, 128], BF16))
    C.s_negtri = P.slot("negtri")
    C.negones = es.enter_context(nc.sbuf_tensor("c_negones", [128, 128], BF16))
    C.s_negones = P.slot("negones")
    P.begin()
    P.op("pool", lambda e: e.memset(C.negones[:], -1.0), writes=[C.s_negones])
    P.op("pool", lambda e: e.memset(C.negtri[:], -1.0), writes=[C.s_negtri])
    P.op("pool", lambda e: e.affine_select(C.negtri[:], C.negtri[:], [[-1, 128]], ALU.is_ge, 0.0, base=0, channel_multiplier=1),
         reads=[C.s_negtri], writes=[C.s_negtri])
    P.end()


def phase_sb(P, nc, C, S, TA, heads=range(8)):
    heads = list(heads)
    NB = TA // 128
    NG = TA // 512
    v3 = S["v"].rearrange("(b p) c -> p b c", p=128)
    with ExitStack() as es:
        sb = mk_sb(nc, es)
        pt = mk_pt(nc, es)
        mk_stage(P, sb)
        kT = [sb(f"a_kT{i}", [128, TA], BF16) for i in range(2)]
        qT = [sb(f"a_qT{i}", [128, TA], BF16) for i in range(2)]
        vt = [sb(f"a_v{i}", [128, NB, 128], BF16) for i in range(2)]
        ef = [sb(f"a_ef{i}", [128, 512], F32) for i in range(2)]
        spb = [sb(f"a_spb{i}", [128, 512], BF16) for i in range(2)]
        lw = [sb(f"a_lw{i}", [128, 512], F32) for i in range(2)]
        wb = [sb(f"a_wb{i}", [128, 512], BF16) for i in range(2)]
        cB = [sb(f"a_cB{i}", [128, 512], F32) for i in range(2)]
        yst = [sb(f"a_yst{i}", [128, 512], BF16) for i in range(2)]
        pA = [pt(f"a_pA{i}", [128, 512]) for i in range(2)]
        pB = [pt(f"a_pB{i}", [128, 512]) for i in range(2)]
        pC = [pt(f"a_pC{i}", [128, 512]) for i in range(2)]
        pY = [pt(f"a_pY{i}", [128, 512]) for i in range(2)]
        mk = lambda n: [P.slot() for _ in range(n)]
        s_kT, s_qT, s_vt, s_ef, s_spb, s_lw, s_wb, s_cB, s_yst = [mk(2) for _ in range(9)]
        s_pA, s_pB, s_pC, s_pY = [mk(2) for _ in range(4)]
        units = []
        for hi, h in enumerate(heads):
            for g in range(NG):
                nkb = 4 * g + 4
                for kb in reversed(range(nkb)):
                    units.append((hi, h, g, kb, kb == nkb - 1, kb == 0))
        gidx = {}
        hstart = {}
        for n_, u in enumerate(units):
            key = (u[0], u[2])
            if key not in gidx:
                gidx[key] = len(gidx)
            if u[0] not in hstart:
                hstart[u[0]] = n_
        P.begin()

        def load_head(hi, h):
            hb = hi % 2
            rows = slice(h * 128, (h + 1) * 128)
            P.dma("sp", kT[hb][:], S["kT"][rows, 0:TA], writes=[s_kT[hb]])
            P.dma("sp", qT[hb][:], S["qT"][rows, 0:TA], writes=[s_qT[hb]])
            P.dma("sp", vt[hb][:], v3[:, :, rows], writes=[s_vt[hb]])

        def st1(n):
            hi, h, g, kb, first, last = units[n]
            hb, j = hi % 2, n % 2
            if n == 0:
                load_head(0, heads[0])
            if n == hstart[hi] + 3 and hi + 1 < len(heads):
                load_head(hi + 1, heads[hi + 1])
            ks = slice(kb * 128, (kb + 1) * 128)
            i = kb - 4 * g
            c0 = 128 * i if i > 0 else 0
            qs = slice(g * 512 + c0, (g + 1) * 512)
            cs = slice(c0, 512)
            P.op("pe", lambda e: e.matmul(pA[j][:, cs], kT[hb][:, ks], qT[hb][:, qs], start=True, stop=True),
                 reads=[s_kT[hb], s_qT[hb]], writes=[s_pA[j]])
            P.op("act", lambda e: e.activation(ef[j][:, cs], pA[j][:, cs], AF.Exp), reads=[s_pA[j]], writes=[s_ef[j]])
            P.op("act", lambda e: e.activation(spb[j][:, cs], ef[j][:, cs], AF.Ln, bias=C.eps[:, 1:2]),
                 reads=[s_ef[j], C.s_eps], writes=[s_spb[j]])
            if i >= 0:
                P.op("pool", lambda e: e.affine_select(spb[j][:, cs], spb[j][:, cs], [[1, 512 - c0]], ALU.is_gt, 0.0,
                                                       base=c0 - 128 * i, channel_multiplier=-1),
                     reads=[s_spb[j]], writes=[s_spb[j]])

        def st2(n):
            hi, h, g, kb, first, last = units[n]
            hb, j = hi % 2, n % 2
            gb = gidx[(hi, g)] % 2
            ks = slice(kb * 128, (kb + 1) * 128)
            i = kb - 4 * g
            c0 = 128 * i if i > 0 else 0
            qs = slice(g * 512 + c0, (g + 1) * 512)
            cs = slice(c0, 512)
            if first:
                P.op("pool", lambda e: e.memset(cB[gb][:], 0.0), writes=[s_cB[gb]])

            def mm(e):
                e.matmul(pB[j][:, cs], kT[hb][:, ks], qT[hb][:, qs], start=True, stop=False)
                e.matmul(pB[j][:, cs], C.negtri[:], spb[j][:, cs], start=False, stop=True)
                return e.matmul(pC[j][:, cs], C.ones[:], spb[j][:, cs], start=True, stop=True)
            P.op("pe", mm, reads=[s_kT[hb], s_qT[hb], s_spb[j], C.s_negtri, C.s_ones], writes=[s_pB[j], s_pC[j]])
            P.op("dve", lambda e: e.tensor_tensor(lw[j][:, cs], pB[j][:, cs], cB[gb][:, cs], ALU.subtract),
                 reads=[s_pB[j], s_cB[gb]], writes=[s_lw[j]])
            if not last:
                P.op("dve", lambda e: e.tensor_tensor(cB[gb][:, cs], pC[j][:, cs], cB[gb][:, cs], ALU.add),
                     reads=[s_pC[j], s_cB[gb]], writes=[s_cB[gb]])
            P.op("act", lambda e: e.activation(wb[j][:, cs], lw[j][:, cs], AF.Exp), reads=[s_lw[j]], writes=[s_wb[j]])
            if i >= 0:
                def msk(e):
                    if c0 > 0:
                        e.memset(wb[j][:, 0:c0], 0.0)
                    return e.affine_select(wb[j][:, cs], wb[j][:, cs], [[1, 512 - c0]], ALU.is_gt, 0.0,
                                           base=c0 - 128 * i, channel_multiplier=-1)
                P.op("pool", msk, reads=[s_wb[j]], writes=[s_wb[j]])

        def st3(n):
            hi, h, g, kb, first, last = units[n]
            hb, j = hi % 2, n % 2
            gb = gidx[(hi, g)] % 2
            P.op("pe", lambda e: e.matmul(pY[gb][:], vt[hb][:, kb, :], wb[j][:], start=first, stop=last),
                 reads=[s_vt[hb], s_wb[j]], writes=[s_pY[gb]])
            if last:
                P.op("dve", lambda e: e.tensor_copy(yst[gb][:], pY[gb][:]), reads=[s_pY[gb]], writes=[s_yst[gb]])
                P.dma("sp", S["ybT"][h * 128:(h + 1) * 128, g * 512:(g + 1) * 512], yst[gb][:], reads=[s_yst[gb]])

        NU = len(units)
        for step in range(NU + 2):
            if step < NU:
                st1(step)
            if 0 <= step - 1 < NU:
                st2(step - 1)
            if 0 <= step - 2 < NU:
                st3(step - 2)
        P.end()


def phase_mla(P, nc, C, S, TA, heads=range(8)):
    heads = list(heads)
    NB = TA // 128
    NG = TA // 512
    v3 = S["mv"].rearrange("(b p) c -> p b c", p=128)
    with ExitStack() as es:
        sb = mk_sb(nc, es)
        pt = mk_pt(nc, es)
        mk_stage(P, sb)
        kr = sb("b_kr", [64, TA], BF16)
        kn = [sb(f"b_kn{i}", [128, TA], BF16) for i in range(2)]
        qn = [sb(f"b_qn{i}", [128, TA], BF16) for i in range(2)]
        qr = [sb(f"b_qr{i}", [64, TA], BF16) for i in range(2)]
        vt = [sb(f"b_v{i}", [128, NB, 128], BF16) for i in range(2)]
        pb = [sb(f"b_pb{i}", [128, 512], BF16) for i in range(3)]
        lnd = sb("b_lnd", [128, 512], F32)
        rd = sb("b_rd", [128, 512], F32)
        yst = [sb(f"b_yst{i}", [128, 512], BF16) for i in range(2)]
        pS = [pt(f"b_pS{i}", [128, 512]) for i in range(3)]
        pN = [pt(f"b_pN{i}", [128, 512]) for i in range(2)]
        pD = [pt(f"b_pD{i}", [128, 512]) for i in range(2)]
        mk = lambda n: [P.slot() for _ in range(n)]
        s_kn, s_qn, s_qr, s_vt, s_yst, s_pN, s_pD = [mk(2) for _ in range(7)]
        s_pb, s_pS = mk(3), mk(3)
        s_kr, s_lnd, s_rd = P.slot(), P.slot(), P.slot()
        units = []
        for hi, h in enumerate(heads):
            for g in range(NG):
                nkb = 4 * g + 4
                for kb in range(nkb):
                    units.append((hi, h, g, kb, kb == 0, kb == nkb - 1))
        gidx = {}
        hstart = {}
        for n_, u in enumerate(units):
            key = (u[0], u[2])
            if key not in gidx:
                gidx[key] = len(gidx)
            if u[0] not in hstart:
                hstart[u[0]] = n_
        P.begin()
        P.dma("sp", kr[:], S["mkr"][:, 0:TA], writes=[s_kr])

        def load_head(hi, h):
            hb = hi % 2
            rows = slice(h * 128, (h + 1) * 128)
            P.dma("sp", kn[hb][:], S["mkn"][rows, 0:TA], writes=[s_kn[hb]])
            P.dma("sp", qn[hb][:], S["mqn"][rows, 0:TA], writes=[s_qn[hb]])
            P.dma("sp", qr[hb][:], S["mqr"][h * 64:(h + 1) * 64, 0:TA], writes=[s_qr[hb]])
            P.dma("sp", vt[hb][:], v3[:, :, rows], writes=[s_vt[hb]])

        def st1(n):
            hi, h, g, kb, first, last = units[n]
            hb, j = hi % 2, n % 3
            if n == 0:
                load_head(0, heads[0])
            if n == hstart[hi] + 3 and hi + 1 < len(heads):
                load_head(hi + 1, heads[hi + 1])
            ks = slice(kb * 128, (kb + 1) * 128)
            i = kb - 4 * g
            c0 = 128 * i if i > 0 else 0
            qs = slice(g * 512 + c0, (g + 1) * 512)
            cs = slice(c0, 512)

            def mm(e):
                e.matmul(pS[j][:, cs], kn[hb][:, ks], qn[hb][:, qs], start=True, stop=False)
                return e.matmul(pS[j][:, cs], kr[:, ks], qr[hb][:, qs], start=False, stop=True)
            P.op("pe", mm, reads=[s_kn[hb], s_qn[hb], s_kr, s_qr[hb]], writes=[s_pS[j]])
            P.op("act", lambda e: e.activation(pb[j][:, cs], pS[j][:, cs], AF.Exp), reads=[s_pS[j]], writes=[s_pb[j]])
            if i >= 0:
                P.op("pool", lambda e: e.memset(pb[j][64:128, c0:c0 + 64], 0.0), reads=[s_pb[j]], writes=[s_pb[j]])

        def st2(n):
            hi, h, g, kb, first, last = units[n]
            hb, j = hi % 2, n % 3
            gb = gidx[(hi, g)] % 2

            i = kb - 4 * g
            c0 = 128 * i if i > 0 else 0
            cs = slice(c0, 512)

            def mm(e):
                e.matmul(pN[gb][:, cs], vt[hb][:, kb, :], pb[j][:, cs], start=first, stop=last)
                return e.matmul(pD[gb][:, cs], C.ones[:], pb[j][:, cs], start=first, stop=last)
            P.op("pe", mm, reads=[s_vt[hb], s_pb[j], C.s_ones], writes=[s_pN[gb], s_pD[gb]])
            if last:
                P.op("act", lambda e: e.activation(lnd[:], pD[gb][:], AF.Ln), reads=[s_pD[gb]], writes=[s_lnd])
                P.op("act", lambda e: e.activation(rd[:], lnd[:], AF.Exp, scale=-1.0), reads=[s_lnd], writes=[s_rd])
                P.op("dve", lambda e: e.tensor_tensor(yst[gb][:], pN[gb][:], rd[:], ALU.mult),
                     reads=[s_pN[gb], s_rd], writes=[s_yst[gb]])
                P.dma("sp", S["ycT"][h * 128:(h + 1) * 128, g * 512:(g + 1) * 512], yst[gb][:], reads=[s_yst[gb]])

        NU = len(units)
        for step in range(NU + 1):
            if step < NU:
                st1(step)
            if 0 <= step - 1 < NU:
                st2(step - 1)
        P.end()


def phase_merge(P, nc, C, xT, w_a, w_b, w_c, w_o, S, T, tok0=0):
    xT3 = xT.rearrange("(k p) t -> p k t", p=128)
    g4 = S["gate"].rearrange("(b k p) t -> p b k t", p=128, k=8)
    with ExitStack() as es:
        sb = mk_sb(nc, es)
        pt = mk_pt(nc, es)
        mk_stage(P, sb)
        W = [sb(f"g_w{i}", [128, KD, 1024], BF16) for i in range(4)]
        y = [sb(f"g_y{i}", [128, KD, 512], BF16) for i in range(3)]
        gt = [sb(f"g_gt{i}", [128, 3, 512], F32) for i in range(2)]
        m = [sb(f"g_m{i}", [128, 512], F32) for i in range(3)]
        mg = sb("g_mg", [128, KD, 512], BF16)
        xg = sb("g_xg", [128, KD, 512], F32)
        pP = [[pt(f"g_p{b}{i}", [128, 512]) for i in range(2)] for b in range(3)]
        pO = [pt(f"g_po{i}", [128, 512]) for i in range(2)]
        mk = lambda n: [P.slot() for _ in range(n)]
        s_W, s_y, s_gt, s_m, s_pO = mk(4), mk(3), mk(2), mk(3), mk(2)
        s_pP = [mk(2) for _ in range(3)]
        s_mg, s_xg = P.slot(), P.slot()
        P.begin()
        for i, w in enumerate((w_a, w_b, w_c, w_o)):
            w3 = w.rearrange("(k p) n -> p k n", p=128)
            for k in range(KD):
                P.dma("pool", W[i][:, k, :], w3[:, k, :], writes=[s_W[i]], part=(k > 0))
        it = 0
        for tix in range(T // 512):
            t0 = tix * 512
            ta = tok0 + t0
            for i, nm in enumerate(("yaT", "ybT", "ycT")):
                P.dma("sp", y[i][:], S[nm].rearrange("(k p) t -> p k t", p=128)[:, :, ta:ta + 512], writes=[s_y[i]])
            P.dma("sp", xg[:], xT3[:, :, t0:t0 + 512], writes=[s_xg])
            for oc in range(KD):
                j = it % 2
                it += 1
                P.dma("sp", gt[j][:], g4[:, :, oc, ta:ta + 512], writes=[s_gt[j]])
                cs = slice(oc * 128, (oc + 1) * 128)

                def mm(e, j=j, cs=cs):
                    ins = None
                    for b in range(3):
                        for k in range(KD):
                            ins = e.matmul(pP[b][j][:], W[b][:, k, cs], y[b][:, k, :], start=(k == 0), stop=(k == KD - 1))
                    return ins
                P.op("pe", mm, reads=s_W[:3] + s_y, writes=[s_pP[b][j] for b in range(3)])
                for b in range(3):
                    P.op("dve", lambda e, b=b, j=j: e.tensor_tensor(m[b][:], pP[b][j][:], gt[j][:, b, :], ALU.mult),
                         reads=[s_pP[b][j], s_gt[j]], writes=[s_m[b]])
                P.op("pool", lambda e: e.tensor_tensor(m[0][:], m[0][:], m[1][:], ALU.add), reads=[s_m[0], s_m[1]], writes=[s_m[0]])
                P.op("pool", lambda e, oc=oc: e.tensor_tensor(mg[:, oc, :], m[0][:], m[2][:], ALU.add),
                     reads=[s_m[0], s_m[2]], writes=[s_mg])
            for oc in range(KD):
                j = oc % 2
                cs = slice(oc * 128, (oc + 1) * 128)

                def mm2(e, j=j, cs=cs):
                    ins = None
                    for k in range(KD):
                        ins = e.matmul(pO[j][:], W[3][:, k, cs], mg[:, k, :], start=(k == 0), stop=(k == KD - 1))
                    return ins
                P.op("pe", mm2, reads=[s_W[3], s_mg], writes=[s_pO[j]])
                P.op("dve", lambda e, j=j, oc=oc: e.tensor_tensor(xg[:, oc, :], pO[j][:], xg[:, oc, :], ALU.add),
                     reads=[s_pO[j], s_xg], writes=[s_xg])
            P.dma("sp", xT3[:, :, t0:t0 + 512], xg[:], reads=[s_xg])
        P.end()


def phase_final(P, nc, C, xT, outT, T):
    xT3 = xT.rearrange("(k p) t -> p k t", p=128)
    oT3 = outT.rearrange("(k p) t -> p k t", p=128)
    with ExitStack() as es:
        sb = mk_sb(nc, es)
        pt = mk_pt(nc, es)
        mk_stage(P, sb)
        xg = [sb(f"n_xg{i}", [128, KD, 512], F32) for i in range(2)]
        og = [sb(f"n_og{i}", [128, KD, 512], F32) for i in range(2)]
        sq = sb("n_sq", [128, KD, 512], BF16)
        lnv = sb("n_lnv", [128, 512], F32)
        rstd = sb("n_rstd", [128, 512], F32)
        pss = pt("n_pss", [128, 512])
        s_xg = [P.slot() for _ in range(2)]
        s_og = [P.slot() for _ in range(2)]
        s_sq, s_ln, s_rstd, s_pss = P.slot(), P.slot(), P.slot(), P.slot()
        P.begin()
        for tix in range(T // 512):
            t0 = tix * 512
            b = tix % 2
            P.dma("sp", xg[b][:], xT3[:, :, t0:t0 + 512], writes=[s_xg[b]])
            rmsnorm_tile(P, C, xg[b][:], s_xg[b], og[b][:], s_og[b],
                         lambda k: C.par[:, 0, PC_FIN + k:PC_FIN + k + 1], KD,
                         sq, s_sq, pss, s_pss, lnv, s_ln, rstd, s_rstd, 512, 1.0 / D)
            P.dma("sp", oT3[:, :, t0:t0 + 512], og[b][:], reads=[s_og[b]])
        P.end()


from concourse.bass_utils import run_bass_kernel_spmd

DEPTH = 4
SEQ = 4096
BATCH = 4
WNAMES = ["ffn1_w_gate_up", "ffn1_w_down", "w_in", "rg_w_a", "rg_w_x", "mla_w_uq", "mla_w_ukv",
          "w_branch_a", "w_branch_b", "w_branch_c", "w_out", "ffn2_w_gate_up", "ffn2_w_down"]


ALLPH = ("ffn1", "win", "rg", "prep", "sb", "mla", "merge", "ffn2")


def build_program(wshapes, depth=DEPTH, seq=SEQ, sel=ALLPH):
    nc = bass.Bass("TRN2", target_bir_lowering=False)
    TA = seq
    xT_in = nc.dram_tensor("xT_in", [D, TA], F32, kind="ExternalInput").ap()
    pos_d = nc.dram_tensor("pos", [1, TA], I32, kind="ExternalInput").ap()
    par_d = nc.dram_tensor("par", [DEPTH, 128, NPAR], F32, kind="ExternalInput").ap()
    ropec_d = nc.dram_tensor("ropec", [64, 2], F32, kind="ExternalInput").ap()
    Wd = {n: nc.dram_tensor(n, list(wshapes[n]), F32, kind="ExternalInput").ap() for n in WNAMES}
    outT = nc.dram_tensor("outT", [D, TA], F32, kind="ExternalOutput").ap()
    xs = nc.dram_tensor("xs", [D, TA], F32).ap()
    S = alloc_scratch(nc, TA)
    R = alloc_rope(nc, TA)
    with ExitStack() as es:
        P = Prog(nc)
        C = setup_consts(P, nc, es, par_d, DEPTH)
        setup_attn_consts(P, nc, C, es)
        P.begin()
        P.dma("sp", xs, xT_in)
        P.end()
        phase_rope(P, nc, C, pos_d, ropec_d, R, TA)
        for l in range(depth):
            if "ffn1" in sel:
                phase_ffn(P, nc, C, xs, Wd["ffn1_w_gate_up"][l], Wd["ffn1_w_down"][l], l, PC_FFN1, TA)
            if "win" in sel:
                phase_win(P, nc, C, xs, Wd["w_in"][l], l, S, TA)
            if "rg" in sel:
                phase_rg(P, nc, C, Wd["rg_w_a"][l], Wd["rg_w_x"][l], l, S, TA)
            if "prep" in sel:
                phase_mlaprep(P, nc, C, Wd["mla_w_uq"][l], Wd["mla_w_ukv"][l], l, S, R, TA)
            if "sb" in sel:
                phase_sb(P, nc, C, S, TA)
            if "mla" in sel:
                phase_mla(P, nc, C, S, TA)
            if "merge" in sel:
                phase_merge(P, nc, C, xs, Wd["w_branch_a"][l], Wd["w_branch_b"][l], Wd["w_branch_c"][l], Wd["w_out"][l], S, TA)
            if "ffn2" in sel:
                phase_ffn(P, nc, C, xs, Wd["ffn2_w_gate_up"][l], Wd["ffn2_w_down"][l], l, PC_FFN2, TA)
        phase_final(P, nc, C, xs, outT, TA)
    return nc


def pack_params(inp):
    par = np.zeros((DEPTH, 128, NPAR), np.float32)
    pm = lambda v: np.asarray(v, np.float32).reshape(-1, 128).T
    for l in range(DEPTH):
        par[l, :, PC_FFN1:PC_FFN1 + 8] = pm(inp["ffn1_norm"][l])
        par[l, :, PC_MIX:PC_MIX + 8] = pm(inp["mix_norm"][l])
        par[l, :, PC_FFN2:PC_FFN2 + 8] = pm(inp["ffn2_norm"][l])
        for tap in range(4):
            par[l, :, PC_CONVW + tap * 8:PC_CONVW + tap * 8 + 8] = pm(inp["conv_w"][l][tap])
        par[l, :, PC_CONVB:PC_CONVB + 8] = pm(inp["conv_b"][l])
        par[l, :, PC_RGBA:PC_RGBA + 8] = pm(inp["rg_b_a"][l])
        par[l, :, PC_RGBX:PC_RGBX + 8] = pm(inp["rg_b_x"][l])
        par[l, :, PC_LAM:PC_LAM + 8] = pm(inp["rg_lambda"][l])
        par[l, :, PC_QN:PC_QN + 2] = pm(inp["mla_q_norm"][l])
        par[l, :, PC_KVN:PC_KVN + 2] = pm(inp["mla_kv_norm"][l])
        par[l, :, PC_FIN:PC_FIN + 8] = pm(inp["final_norm"])
    return par


def rope_consts():
    inv = (np.float32(10000.0) ** (-np.arange(0, 64, 2, dtype=np.float32) / np.float32(64))).astype(np.float32)
    ropec = np.zeros((64, 2), np.float32)
    ropec[:, 0] = np.concatenate([inv, inv])
    ropec[:, 1] = np.concatenate([-np.ones(32, np.float32), np.ones(32, np.float32)])
    return ropec


def kernel(**inputs):
    inp = {k: np.asarray(v) for k, v in inputs.items()}
    x = inp["x"].astype(np.float32, copy=False)
    pos = inp["positions"].astype(np.int32, copy=False)
    par = pack_params(inp)
    ropec = rope_consts()
    W = {n: np.ascontiguousarray(inp[n], dtype=np.float32) for n in WNAMES}
    nc = build_program({n: W[n].shape for n in WNAMES})
    ACTIVE = [0, 1, 4, 5]
    zW = {n: np.zeros_like(W[n]) for n in WNAMES}
    zmap = {"xT_in": np.zeros((D, SEQ), np.float32), "pos": np.zeros((1, SEQ), np.int32),
            "par": np.zeros_like(par), "ropec": ropec}
    zmap.update(zW)
    in_maps = []
    for c in range(8):
        if c in ACTIVE:
            b = ACTIVE.index(c)
            m = {"xT_in": np.ascontiguousarray(x[b].T), "pos": np.ascontiguousarray(pos[b][None, :]),
                 "par": par, "ropec": ropec}
            m.update(W)
            in_maps.append(m)
        else:
            in_maps.append(zmap)
    res = run_bass_kernel_spmd(nc, in_maps, core_ids=list(range(8)))
    out = np.empty((BATCH, SEQ, D), np.float32)
    for b in range(BATCH):
        out[b] = np.asarray(res.results[ACTIVE[b]]["outT"]).T
    return out
```
